# Optimizing a Trainium2 kernel written in Bass

```python
import math
import jax, jax.numpy as jnp
from jax import lax
import numpy as np

D_MODEL = 1024
BATCH = 8
SEQ = 2048
DEPTH = 4
DEC_BATCH = 128
DEC_SEQ = 1
PAST_LEN = 2048
PAGE_SIZE = 128

N_A_LAYERS = DEPTH // 2
N_B_LAYERS = DEPTH - N_A_LAYERS
HEAD_DIM = 64
N_HEADS = D_MODEL // HEAD_DIM
ATTN_DIM = N_HEADS * HEAD_DIM
CONV_DIM = D_MODEL
CONV_WIDTH = 31
Q_BLOCK = 128
EPS = 1e-6

kernel_name = "yoco_conformer_conv_fox_decoder_step"


def rmsnorm(x, g):
    xf = x.astype(jnp.float32)
    y = xf * lax.rsqrt(jnp.mean(xf * xf, axis=-1, keepdims=True) + EPS)
    return (y * g.astype(jnp.float32)).astype(x.dtype)


def layernorm(x, g, b):
    xf = x.astype(jnp.float32)
    xc = xf - jnp.mean(xf, axis=-1, keepdims=True)
    y = xc * lax.rsqrt(jnp.mean(xc * xc, axis=-1, keepdims=True) + EPS)
    return (y * g.astype(jnp.float32) + b.astype(jnp.float32)).astype(x.dtype)


def depthwise_causal(u_full, w, b):
    y = lax.conv_general_dilated(
        u_full, w[:, None, :].astype(u_full.dtype), window_strides=(1,), padding='VALID',
        dimension_numbers=('NWC', 'WIO', 'NWC'), feature_group_count=u_full.shape[-1])
    return y + b.astype(y.dtype)


def conv_mixer(x, buf, norm_g, w_in, conv_w, conv_b, ln_g, ln_b, w_out):
    u = rmsnorm(x, norm_g) @ w_in
    a, g, z = jnp.split(u, 3, axis=-1)
    glu = a * jax.nn.sigmoid(g)
    full = jnp.concatenate([buf.astype(glu.dtype), glu], axis=1)
    y = jax.nn.silu(layernorm(depthwise_causal(full, conv_w, conv_b), ln_g, ln_b))
    out = (y * jax.nn.silu(z)) @ w_out
    new_buf = full[:, full.shape[1] - (CONV_WIDTH - 1):]
    return x + out, new_buf


def shared_kv(x, kv_norm, kv_w, kv_fb, k_norm):
    b, t, _ = x.shape
    u = rmsnorm(x, kv_norm) @ kv_w
    k = rmsnorm(u[..., :ATTN_DIM].reshape(b, t, N_HEADS, HEAD_DIM), k_norm)
    v = u[..., ATTN_DIM:2 * ATTN_DIM].reshape(b, t, N_HEADS, HEAD_DIM)
    logf = jax.nn.log_sigmoid((u[..., 2 * ATTN_DIM:] + kv_fb).astype(jnp.float32))
    return k, v, logf


def fox_attend(q, k, v, cq, ck, q_pos, k_pos):
    b, tq, h, d = q.shape
    blk = min(Q_BLOCK, tq)
    nb = tq // blk
    kf = k.astype(jnp.float32)
    vf = v.astype(jnp.float32)
    ckh = jnp.swapaxes(ck, 1, 2)
    qb = jnp.swapaxes(q.reshape(b, nb, blk, h, d), 0, 1)
    cqb = jnp.swapaxes(cq.reshape(b, nb, blk, h), 0, 1)
    pb = q_pos.reshape(nb, blk)
    scale = 1.0 / math.sqrt(d)

    def block(args):
        qi, ci, pi = args
        s = jnp.einsum('bqhd,bkhd->bhqk', qi.astype(jnp.float32), kf) * scale
        s = s + jnp.swapaxes(ci, 1, 2)[..., :, None] - ckh[..., None, :]
        s = jnp.where((k_pos[None, :] <= pi[:, None])[None, None], s, -jnp.inf)
        p = jax.nn.softmax(s, axis=-1)
        return jnp.einsum('bhqk,bkhd->bqhd', p, vf)

    o = lax.map(block, (qb, cqb, pb))
    return jnp.swapaxes(o, 0, 1).reshape(b, tq, h, d).astype(v.dtype)


def fox_mixer(x, k, v, cq, ck, q_pos, k_pos, norm_g, w_in, q_g, w_out):
    b, t, _ = x.shape
    u = rmsnorm(x, norm_g) @ w_in
    q = rmsnorm(u[..., :ATTN_DIM].reshape(b, t, N_HEADS, HEAD_DIM), q_g)
    z = u[..., ATTN_DIM:]
    o = fox_attend(q, k, v, cq, ck, q_pos, k_pos).reshape(b, t, ATTN_DIM)
    return x + (o * jax.nn.silu(z)) @ w_out


def setup_inputs(seed: int = 0) -> dict:
    key = jax.random.key(seed)
    ks = jax.random.split(key, 24)
    n_pages = PAST_LEN // PAGE_SIZE
    n_used = DEC_BATCH * n_pages
    n_pool = n_used + max(1, n_used // 4)
    nrm = jax.random.normal
    f32 = jnp.float32
    page_table = jax.random.permutation(ks[0], n_pool)[:n_used].reshape(DEC_BATCH, n_pages).astype(jnp.int32)
    return {
        'x_prompt': nrm(ks[1], (BATCH, SEQ, D_MODEL), f32),
        'x_sample': nrm(ks[2], (DEC_BATCH, DEC_SEQ, D_MODEL), f32),
        'state_conv': nrm(ks[3], (N_A_LAYERS, DEC_BATCH, CONV_WIDTH - 1, CONV_DIM), f32),
        'cache_k': nrm(ks[4], (n_pool, PAGE_SIZE, N_HEADS, HEAD_DIM), f32),
        'cache_v': nrm(ks[5], (n_pool, PAGE_SIZE, N_HEADS, HEAD_DIM), f32),
        'cache_logf': jax.nn.log_sigmoid(2.0 + 0.5 * nrm(ks[6], (n_pool, PAGE_SIZE, N_HEADS), f32)),
        'page_table': page_table,
        'a_norm': 1.0 + 0.02 * nrm(ks[7], (N_A_LAYERS, D_MODEL), f32),
        'a_w_in': nrm(ks[8], (N_A_LAYERS, D_MODEL, 3 * CONV_DIM), f32) * D_MODEL ** -0.5,
        'a_conv_w': nrm(ks[9], (N_A_LAYERS, CONV_WIDTH, CONV_DIM), f32) * CONV_WIDTH ** -0.5,
        'a_conv_b': 0.02 * nrm(ks[10], (N_A_LAYERS, CONV_DIM), f32),
        'a_ln_g': 1.0 + 0.02 * nrm(ks[11], (N_A_LAYERS, CONV_DIM), f32),
        'a_ln_b': 0.02 * nrm(ks[12], (N_A_LAYERS, CONV_DIM), f32),
        'a_w_out': nrm(ks[13], (N_A_LAYERS, CONV_DIM, D_MODEL), f32) * CONV_DIM ** -0.5,
        'kv_norm': 1.0 + 0.02 * nrm(ks[14], (D_MODEL,), f32),
        'kv_w': nrm(ks[15], (D_MODEL, 2 * ATTN_DIM + N_HEADS), f32) * D_MODEL ** -0.5,
        'kv_fb': 2.0 + 0.5 * nrm(ks[16], (N_HEADS,), f32),
        'k_norm': 1.0 + 0.02 * nrm(ks[17], (HEAD_DIM,), f32),
        'b_norm': 1.0 + 0.02 * nrm(ks[18], (N_B_LAYERS, D_MODEL), f32),
        'b_w_in': nrm(ks[19], (N_B_LAYERS, D_MODEL, 2 * ATTN_DIM), f32) * D_MODEL ** -0.5,
        'q_norm': 1.0 + 0.02 * nrm(ks[20], (N_B_LAYERS, HEAD_DIM), f32),
        'b_w_out': nrm(ks[21], (N_B_LAYERS, ATTN_DIM, D_MODEL), f32) * ATTN_DIM ** -0.5,
    }


def reference(x_prompt, x_sample, state_conv, cache_k, cache_v, cache_logf, page_table,
              a_norm, a_w_in, a_conv_w, a_conv_b, a_ln_g, a_ln_b, a_w_out,
              kv_norm, kv_w, kv_fb, k_norm, b_norm, b_w_in, q_norm, b_w_out):
    xp, xs = x_prompt, x_sample
    bp, tp, _ = xp.shape
    bs, ts, _ = xs.shape
    conv_p, conv_s = [], []
    for layer in range(DEPTH):
        if layer < N_A_LAYERS:
            i = layer
            buf0 = jnp.zeros((bp, CONV_WIDTH - 1, CONV_DIM), xp.dtype)
            xp, nbp = conv_mixer(xp, buf0, a_norm[i], a_w_in[i], a_conv_w[i], a_conv_b[i],
                                 a_ln_g[i], a_ln_b[i], a_w_out[i])
            xs, nbs = conv_mixer(xs, state_conv[i], a_norm[i], a_w_in[i], a_conv_w[i], a_conv_b[i],
                                 a_ln_g[i], a_ln_b[i], a_w_out[i])
            conv_p.append(nbp)
            conv_s.append(nbs)
            if layer == N_A_LAYERS - 1:
                k_p, v_p, logf_p = shared_kv(xp, kv_norm, kv_w, kv_fb, k_norm)
                k_s, v_s, logf_s = shared_kv(xs, kv_norm, kv_w, kv_fb, k_norm)
                c_p = jnp.cumsum(logf_p, axis=1)
                pos_p = jnp.arange(tp)
                past_k = cache_k[page_table].reshape(bs, -1, N_HEADS, HEAD_DIM)
                past_v = cache_v[page_table].reshape(bs, -1, N_HEADS, HEAD_DIM)
                past_f = cache_logf[page_table].reshape(bs, -1, N_HEADS)
                past_len = past_k.shape[1]
                k_all = jnp.concatenate([past_k.astype(k_s.dtype), k_s], axis=1)
                v_all = jnp.concatenate([past_v.astype(v_s.dtype), v_s], axis=1)
                c_all = jnp.cumsum(jnp.concatenate([past_f.astype(jnp.float32), logf_s], axis=1), axis=1)
                c_s = c_all[:, past_len:]
                qpos_s = past_len + jnp.arange(ts)
                kpos_s = jnp.arange(past_len + ts)
        else:
            j = layer - N_A_LAYERS
            xp = fox_mixer(xp, k_p, v_p, c_p, c_p, pos_p, pos_p,
                           b_norm[j], b_w_in[j], q_norm[j], b_w_out[j])
            xs = fox_mixer(xs, k_all, v_all, c_s, c_all, qpos_s, kpos_s,
                           b_norm[j], b_w_in[j], q_norm[j], b_w_out[j])
    conv_prompt = jnp.stack(conv_p, axis=0)
    conv_sample = jnp.stack(conv_s, axis=0)
    return (xp, xs, conv_prompt, conv_sample, k_p, v_p, logf_p, k_s, v_s, logf_s)
```

```python
import numpy as np
import ml_dtypes
from contextlib import ExitStack
import concourse.bass as bass
import concourse.mybir as mybir
from concourse.bass_utils import run_bass_kernel_spmd

F32 = mybir.dt.float32
BF16 = mybir.dt.bfloat16
I32 = mybir.dt.int32
ALU = mybir.AluOpType
AF = mybir.ActivationFunctionType
AX = mybir.AxisListType
EPS = 1e-6
NCORES = 8


def I(name, *a, **kw):
    return (name, a, kw)


class Res:
    __slots__ = ("name", "w", "r")

    def __init__(self, name=""):
        self.name = name
        self.w = None
        self.r = {}


class Chan:
    __slots__ = ("sem", "count", "name")

    def __init__(self, sem, name):
        self.sem = sem
        self.count = 0
        self.name = name


class Sched:
    ENG = ("pe", "act", "dve", "pool", "sp")
    CE = ("pe", "act", "dve", "pool")

    def __init__(self, nc, esem, free):
        self.nc = nc
        self.items = {e: [] for e in self.ENG}
        self.cnt = {e: 0 for e in self.ENG}
        self.waited = {e: {} for e in self.ENG}
        self.esem = esem
        self.free = list(free)
        self.chans = []

    def chan(self, name=""):
        c = Chan(self.free.pop(), name)
        self.chans.append(c)
        return c

    def op(self, eng, fn, reads=(), writes=(), chan=None):
        deps = {}

        def add(k, v):
            if deps.get(k, 0) < v:
                deps[k] = v
        for r in reads:
            if r.w is not None:
                add(*r.w)
        for w in writes:
            if w.w is not None:
                add(*w.w)
            for k, v in w.r.items():
                add(k, v)
        waits = []
        wd = self.waited[eng]
        for k, v in deps.items():
            if chan is None and k == eng and eng == "pe":
                continue
            if isinstance(k, Chan):
                v = k.count
            if wd.get(k, 0) >= v:
                continue
            wd[k] = v
            waits.append((k.sem if isinstance(k, Chan) else self.esem[k], v))
        if chan is not None:
            chan.count += 16
            ev = (chan, chan.count)
            inc = (chan.sem, 16)
        else:
            self.cnt[eng] += 1
            ev = (eng, self.cnt[eng])
            inc = (self.esem[eng], 1)
        self.items[eng].append((waits, fn, inc))
        for w in writes:
            w.w = ev
            w.r = {}
        for r in reads:
            if r.r.get(ev[0], 0) < ev[1]:
                r.r[ev[0]] = ev[1]
        return ev

    def barrier(self):
        for e in self.ENG:
            waits = []
            wd = self.waited[e]
            for c in self.chans:
                if c.count and wd.get(c, 0) < c.count:
                    wd[c] = c.count
                    waits.append((c.sem, c.count))
            for o in self.CE:
                if o != e and self.cnt[o] and wd.get(o, 0) < self.cnt[o]:
                    wd[o] = self.cnt[o]
                    waits.append((self.esem[o], self.cnt[o]))
            if waits:
                self.items[e].append((waits, None, None))

    def emit(self, block):
        items = self.items

        def runner(name):
            def run(e):
                for waits, fn, inc in items[name]:
                    for sem, val in waits:
                        e.wait_ge(sem, val)
                    if fn is not None:
                        calls = fn if isinstance(fn, list) else [fn]
                        for (nm, a, kw) in calls:
                            ins = getattr(e, nm)(*a, **kw)
                        ins.then_inc(inc[0], inc[1])
            return run
        block.tensor(runner("pe"))
        block.scalar(runner("act"))
        block.vector(runner("dve"))
        block.gpsimd(runner("pool"))
        block.sync(runner("sp"))


def make_consts():
    bf = ml_dtypes.bfloat16
    k = np.arange(128)
    c = {}
    c["identb"] = np.eye(128, dtype=np.float32).astype(bf)
    c["identf"] = np.eye(128, dtype=np.float32)
    c["tri"] = (k[None, :] >= k[:, None]).astype(np.float32).astype(bf)
    c["lincl"] = (k[:, None] <= k[None, :]).astype(np.float32)
    c["ustr"] = (k[:, None] > k[None, :]).astype(np.float32)
    c["onesdiv"] = np.full((128, 128), 1.0 / 1024.0, np.float32)
    c["ones1"] = np.ones((128, 128), np.float32)
    c["rs0"] = np.zeros((128, 128), np.float32)
    c["rs0"][0, :] = 1.0
    eb = np.zeros((16, 16, 128), np.float32)
    ec = np.zeros((16, 16, 16), np.float32)
    for s in range(16):
        eb[s, s, :] = 1.0
        ec[:, s, s] = 1.0
    c["eb"] = eb.reshape(16, 2048)
    c["ec"] = ec.reshape(16, 256)
    bm = np.zeros((16, 1024), np.float32)
    for h in range(16):
        bm[h, h * 64:(h + 1) * 64] = 1.0
    c["bmask"] = bm
    c["pidx"] = k.astype(np.float32)[:, None].copy()
    return c


CONST_SPECS = [("identb", [128, 128], BF16), ("identf", [128, 128], F32), ("tri", [128, 128], BF16),
               ("lincl", [128, 128], F32), ("ustr", [128, 128], F32), ("onesdiv", [128, 128], F32),
               ("ones1", [128, 128], F32), ("rs0", [128, 128], F32), ("eb", [16, 2048], F32),
               ("ec", [16, 256], F32), ("bmask", [16, 1024], F32), ("pidx", [128, 1], F32)]


def build(NP=2560, upto=99, attn=True):
    nc = bass.Bass("TRN2", target_bir_lowering=False)

    def din(name, shape, dt=F32):
        return nc.dram_tensor(name, shape, dt, kind="ExternalInput").ap()

    def dout(name, shape, dt=F32):
        return nc.dram_tensor(name, shape, dt, kind="ExternalOutput").ap()

    def dscr(name, shape, dt=BF16):
        return nc.dram_tensor(name, shape, dt, kind="Internal").ap()

    x_d = din("x", [2048, 1024]); xs_d = din("xs", [16, 1024]); st_d = din("state", [2, 480, 1024])
    if attn:
        ck_d = din("ck", [NP * 128, 1024]); cv_d = din("cv", [NP * 128, 1024]); cl_d = din("cl", [NP * 128, 16])
        pt_d = din("pt", [1, 256], I32)
    a_norm = din("a_norm", [2, 1024]); a_w_in = din("a_w_in", [2, 1024, 3072]); a_cw = din("a_cw", [2, 31, 1024])
    a_cb = din("a_cb", [2, 1024]); a_lg = din("a_lg", [2, 1024]); a_lb = din("a_lb", [2, 1024])
    a_w_out = din("a_w_out", [2, 1024, 1024]); kv_norm = din("kv_norm", [1, 1024]); kv_w = din("kv_w", [1024, 2064])
    kv_fb = din("kv_fb", [1, 16]); k_norm = din("k_norm", [1, 64]); b_norm = din("b_norm", [2, 1024])
    b_w_in = din("b_w_in", [2, 1024, 2048]); q_norm = din("q_norm", [2, 64]); b_w_out = din("b_w_out", [2, 1024, 1024])
    cd = {n: din("c_" + n, s, dt) for n, s, dt in CONST_SPECS}

    y_d = dout("y", [2048, 1024]); ys_d = dout("ys", [16, 1024]); convp_d = dout("convp", [2, 30, 1024])
    convs_d = dout("convs", [2, 480, 1024]); kp_d = dout("kp", [2048, 1024]); vp_d = dout("vp", [2048, 1024])
    lfp_d = dout("lfp", [2048, 16]); ks_d = dout("ks", [16, 1024]); vs_d = dout("vs", [16, 1024]); lfs_d = dout("lfs", [16, 16])

    s_awin = dscr("s_awin", [2, 6, 128, 4096]); s_awout = dscr("s_awout", [2, 2, 128, 4096])
    s_kvw = dscr("s_kvw", [4, 128, 4096]); s_kvf = dscr("s_kvf", [128, 128])
    s_bwin = dscr("s_bwin", [2, 4, 128, 4096]); s_bwout = dscr("s_bwout", [2, 2, 128, 4096])

    with ExitStack() as G:
        esem = {e: G.enter_context(nc.semaphore("s_" + e)) for e in Sched.CE}
        free = [G.enter_context(nc.semaphore("ch%d" % i)) for i in range(80)]
        S = Sched(nc, esem, free)

        def sbt(es, name, shape, dt):
            return es.enter_context(nc.sbuf_tensor(name, shape, dt))

        def dma(dst, src, reads, writes, ch, eng="sp"):
            S.op(eng, I("dma_start", out=dst, in_=src), reads, writes, chan=ch)

        def mm(out_ap, pairs, reads, writes):
            n = len(pairs)
            S.op("pe", [I("matmul", out_ap, l, r, start=(i == 0), stop=(i == n - 1)) for i, (l, r) in enumerate(pairs)],
                 reads, writes)

        X = sbt(G, "X", [128, 16, 1024], F32); XS = sbt(G, "XS", [16, 1024], F32)
        C = {n: sbt(G, "k_" + n, s, dt) for n, s, dt in CONST_SPECS if n not in ("eb", "ec", "bmask")}
        epsc = sbt(G, "epsc", [128, 1], F32); onec = sbt(G, "onec", [128, 1], F32)
        rX = [Res("X%d" % i) for i in range(16)]; rXS = Res("XS"); rC = Res("consts")
        PS = [G.enter_context(nc.psum_tensor("ps%d" % i, [128, 512], F32)) for i in range(7)]
        PSB = G.enter_context(nc.psum_tensor("psb", [128, 1024], BF16))
        rPS = [Res("ps%d" % i) for i in range(7)]; rPSB = Res("psb")
        ld_c = S.chan("ldc"); ld_x = S.chan("ldx")
        for n in C:
            dma(C[n][:], cd[n], [], [rC], ld_c)
        S.op("dve", I("memset", epsc[:], EPS), [], [rC])
        S.op("dve", I("memset", onec[:], 1.0), [], [rC])
        xv = x_d.rearrange("(i p) d -> p i d", p=128)
        for i in range(16):
            dma(X[:, i, :], xv[:, i, :], [], [rX[i]], ld_x)
        dma(XS[:], xs_d, [], [rXS], ld_x)

        with ExitStack() as P0:
            stg = [sbt(P0, "stg%d" % i, [128, 8, 512], F32) for i in range(2)]
            cvb = [sbt(P0, "cvb%d" % i, [128, 8, 512], BF16) for i in range(2)]
            rstg = [Res(), Res()]; rcvb = [Res(), Res()]
            chl = [S.chan("cvl0"), S.chan("cvl1")]; chs = [S.chan("cvs0"), S.chan("cvs1")]
            rscr = Res("scratch")
            state = {"n": 0}

            def conv_unit(W, pieces, dst, scale, width=512):
                k = state["n"] % 2
                state["n"] += 1
                for (c0, w, d0) in pieces:
                    dma(stg[k][:, :, d0:d0 + w], W[:, c0:c0 + w].rearrange("(c p) n -> p c n", p=128), [], [rstg[k]], chl[k])
                eng = ("dve", "act")[state["n"] % 2]
                src = stg[k][:, :, 0:width]; dstt = cvb[k][:, :, 0:width]
                if eng == "act":
                    S.op("act", I("mul", dstt, src, scale), [rstg[k]], [rcvb[k]])
                elif scale == 1.0:
                    S.op(eng, I("tensor_copy", dstt, src), [rstg[k]], [rcvb[k]])
                else:
                    S.op(eng, I("tensor_scalar", dstt, src, scale, None, ALU.mult), [rstg[k]], [rcvb[k]])
                dma(dst.rearrange("p (c n) -> p c n", c=8), dstt, [rcvb[k]], [rscr], chs[k])

            for l in range(2):
                for u in range(4):
                    conv_unit(a_w_in[l], [(256 * u, 256, 0), (1024 + 256 * u, 256, 256)], s_awin[l, u], 0.5)
                for u in range(2):
                    conv_unit(a_w_in[l], [(2048 + 512 * u, 512, 0)], s_awin[l, 4 + u], 0.5)
                for u in range(2):
                    conv_unit(a_w_out[l], [(512 * u, 512, 0)], s_awout[l, u], 1.0)
            for u in range(4):
                conv_unit(kv_w, [(512 * u, 512, 0)], s_kvw[u], 1.0)
            conv_unit(kv_w, [(2048, 16, 0)], s_kvf, 1.0, width=16)
            for l in range(2):
                for u in range(4):
                    conv_unit(b_w_in[l], [(512 * u, 512, 0)], s_bwin[l, u], 1.0 if u < 2 else 0.5)
                for u in range(2):
                    conv_unit(b_w_out[l], [(512 * u, 512, 0)], s_bwout[l, u], 1.0)
            S.barrier()

        def rmsnorm_rstd(es_tmp, srcs, P, ncol, rres, wres, tag):
            ss = es_tmp["ss"]; junk = es_tmp["junk"]; rstd = es_tmp["rstd"]
            for i, (ap, r) in enumerate(srcs):
                S.op("act", I("activation", junk[:P, :], ap, AF.Square, accum_out=ss[:P, i:i + 1]),
                     [r], [es_tmp["rjunk"], es_tmp["rss"]])
            n = len(srcs)
            S.op("dve", I("tensor_scalar", rstd[:P, 0:n], ss[:P, 0:n], 1.0 / 1024.0, EPS, ALU.mult, ALU.add),
                 [es_tmp["rss"]], [es_tmp["rrstd"]])
            S.op("act", I("activation", rstd[:P, 0:n], rstd[:P, 0:n], AF.Sqrt), [es_tmp["rrstd"]], [es_tmp["rrstd"]])
            S.op("dve", I("reciprocal", rstd[:P, 0:n], rstd[:P, 0:n]), [es_tmp["rrstd"]], [es_tmp["rrstd"]])
            return rstd

        def norm_tmp(es, tag):
            return dict(ss=sbt(es, "ss" + tag, [128, 8], F32), junk=sbt(es, "junk" + tag, [128, 1024], BF16),
                        rstd=sbt(es, "rstd" + tag, [128, 8], F32), rjunk=Res(), rss=Res(), rrstd=Res())

        import os as _os
        dbg_list = []
        DEBUG = bool(_os.environ.get("KDEBUG"))
        dbg_ch = S.chan("dbg") if DEBUG else None

        dbg_state = {}
        if DEBUG:
            dbg_state["t"] = G.enter_context(nc.sbuf_tensor("dbgt", [128, 256], F32))
            dbg_state["r"] = Res()

        def dbg(name, ap, reads, P=128, N=256):
            if not DEBUG:
                return
            if "t" not in dbg_state:
                dbg_state["t"] = G.enter_context(nc.sbuf_tensor("dbgt", [128, 256], F32))
                dbg_state["r"] = Res()
            t, r = dbg_state["t"], dbg_state["r"]
            d = nc.dram_tensor("dbg_" + name, [P, N], F32, kind="ExternalOutput").ap()
            S.op("dve", I("tensor_copy", t[:P, 0:N], ap), reads, [r])
            dma(d, t[:P, 0:N], [r], [], dbg_ch)
        import types
        K = types.SimpleNamespace(**{k: v for k, v in locals().items() if k != "K"})
        K.G = G
        if upto >= 1:
            phase_a(K)
        if upto >= 2:
            phase_kv(K)
        if upto >= 3:
            phase_b(K)

        st_y = S.chan("sty")
        yv = y_d.rearrange("(i p) d -> p i d", p=128)
        for i in range(16):
            dma(yv[:, i, :], X[:, i, :], [rX[i]], [], st_y)
        dma(ys_d, XS[:], [rXS], [], st_y)
        S.barrier()
        with nc.Block() as block:
            S.emit(block)
    return nc


def brow(ap2d, row, ncols, P):
    t = ap2d.tensor
    w = ap2d.shape[1]
    return bass.AP(t, row * w, [[0, P], [1, ncols]])


def phase_a(K):
    nc, S, X, XS, C, PS, PSB = K.nc, K.S, K.X, K.XS, K.C, K.PS, K.PSB
    rX, rXS, rPS, rPSB, rC = K.rX, K.rXS, K.rPS, K.rPSB, K.rC
    dma, mm = K.dma, K.mm
    identb, identf, onesdiv = C["identb"], C["identf"], C["onesdiv"]
    with ExitStack() as P:
        def sb(name, shape, dt):
            return P.enter_context(nc.sbuf_tensor("pa_" + name, shape, dt))
        gb = sb("gb", [128, 1024], F32); prow = sb("prow", [34, 1024], F32)
        pcol = sb("pcol", [128, 8, 34], F32); lg2 = sb("lg2", [128, 8, 2], F32)
        xn = [sb("xn%d" % i, [128, 1024], BF16) for i in range(2)]
        xnT = sb("xnT", [128, 8, 512], BF16)
        wbuf = [sb("wb%d" % i, [128, 8, 512], BF16) for i in range(2)]
        glu = sb("glu", [128, 8, 542], F32)
        th = [sb("th%d" % i, [128, 512], F32) for i in range(2)]
        cv = sb("cv", [128, 8, 512], F32)
        sq = [sb("sq%d" % i, [128, 512], F32) for i in range(2)]
        mean_sb = sb("mean_sb", [128, 512], F32); rstd_sb = sb("rstd_sb", [128, 512], F32)
        sz = sb("sz", [128, 8, 512], BF16); h = sb("h", [128, 8, 512], BF16)
        sttok = [sb("sttok0", [120, 1024], F32)] * 2
        stT = sb("stT", [128, 8, 480], F32); rowt = sttok[0]
        NPE = 16
        diag = [sb("diag%d" % i, [128, NPE, 128], BF16) for i in range(2)]
        gluB = [sb("gluB%d" % i, [128, 542], BF16) for i in range(2)]
        r_diag = [Res(), Res()]; r_gluB = [Res(), Res()]
        rr = sb("rr", [128, 16], F32)
        NT = K.norm_tmp(P, "a")
        r_gb, r_prow, r_pcol = Res(), Res(), Res()
        r_xn = [Res(), Res()]; r_xnT = Res(); r_wb = [Res(), Res()]
        r_glu = [Res() for _ in range(8)]; r_cv = [Res() for _ in range(8)]; r_th = [Res(), Res()]
        r_sq = [Res(), Res()]; r_mean, r_rstd = Res(), Res(); r_sz = [Res() for _ in range(8)]
        r_h = [Res() for _ in range(8)]; r_sttok = [Res()] * 2; r_stT = Res(); r_rowt = r_sttok[0]; r_rr = Res()
        ch_w = [S.chan("aw0"), S.chan("aw1")]; ch_p = S.chan("ap"); ch_st = [S.chan("ast0")] * 2
        ch_o = S.chan("aout")
        cnt = {"w": 0, "bank": 0, "th": 0, "sq": 0, "dg": 0}

        def nxt(key, n):
            v = cnt[key] % n
            cnt[key] += 1
            return v

        for l in range(2):
            dma(gb[:], brow(K.a_norm, l, 1024, 128), [], [r_gb], ch_p)
            dma(prow[0:31, :], K.a_cw[l], [], [r_prow], ch_p)
            dma(prow[31:32, :], K.a_cb[l:l + 1, :], [], [r_prow], ch_p)
            dma(prow[32:33, :], K.a_lg[l:l + 1, :], [], [r_prow], ch_p)
            dma(prow[33:34, :], K.a_lb[l:l + 1, :], [], [r_prow], ch_p)

            S.op("pe", [I("transpose", PS[6][:, cc * 34:(cc + 1) * 34], prow[0:34, cc * 128:(cc + 1) * 128], identf[0:34, 0:34])
                        for cc in range(8)], [r_prow, rC], [rPS[6]])
            S.op("act", I("copy", pcol[:].rearrange("p c k -> p (c k)"), PS[6][:, 0:272]), [rPS[6]], [r_pcol])
            S.op("dve", I("tensor_scalar", lg2[:], pcol[:, :, 32:34], 0.5, None, ALU.mult), [r_pcol], [r_pcol])
            S.op("pool", I("memset", glu[:, :, 0:30], 0.0), [], r_glu)

            for blk in range(5):
                samp = blk == 4
                N = 16 if samp else 512
                Pn = 16 if samp else 128
                if samp:
                    srcs = [(XS[:, :], rXS)]
                else:
                    srcs = [(X[:, 4 * blk + t, :], rX[4 * blk + t]) for t in range(4)]
                rstd = K.rmsnorm_rstd(NT, srcs, Pn, len(srcs), None, None, "a")
                for t, (ap, r) in enumerate(srcs):
                    k = t % 2
                    S.op("dve", I("scalar_tensor_tensor",
                        xn[k][:Pn, :], ap, rstd[:Pn, t:t + 1], gb[:Pn, :], ALU.mult, ALU.mult),
                        [r, NT["rrstd"], r_gb], [r_xn[k]])
                    if samp:
                        S.op("pe", [I("transpose", PSB[:, c * 16:(c + 1) * 16], xn[k][:16, c * 128:(c + 1) * 128], identb[0:16, 0:16])
                                    for c in range(8)], [r_xn[k], rC], [rPSB])
                        S.op("act", I("copy", xnT[:, :, 0:16], PSB[:, 0:128].rearrange("p (c n) -> p c n", c=8)),
                             [rPSB], [r_xnT])
                    else:
                        S.op("pe", [I("transpose", PSB[:, c * 128:(c + 1) * 128], xn[k][:, c * 128:(c + 1) * 128], identb[:, :])
                                    for c in range(8)], [r_xn[k], rC], [rPSB])
                        S.op("act", I("copy", xnT[:, :, t * 128:(t + 1) * 128],
                                                          PSB[:, :].rearrange("p (c n) -> p c n", c=8)), [rPSB], [r_xnT])
                if samp:
                    for t in range(4):
                        k = t % 2
                        dma(sttok[k][:, :], K.st_d[l, 120 * t:120 * (t + 1), :], [], [r_sttok[k]], ch_st[k])
                        for half in range(2):
                            S.op("pe", [I("transpose", PS[6][:, j * 120:(j + 1) * 120],
                                          sttok[k][0:120, (4 * half + j) * 128:(4 * half + j + 1) * 128], identf[0:120, 0:120])
                                        for j in range(4)], [r_sttok[k], rC], [rPS[6]])
                            S.op("act", I("copy",
                                stT[:, 4 * half:4 * half + 4, 120 * t:120 * (t + 1)],
                                PS[6][:, 0:480].rearrange("p (c n) -> p c n", c=4)), [rPS[6]], [r_stT])
                pend = []

                def flush():
                    for cc, kq in pend:
                        S.op("pe", I("matmul", PS[4][:, 0:N], onesdiv[:, :], cv[:, cc, 0:N], start=(cc == 0), stop=(cc == 7)),
                             [r_cv[cc], rC], [rPS[4]])
                        S.op("pe", I("matmul", PS[5][:, 0:N], onesdiv[:, :], sq[kq][:, 0:N], start=(cc == 0), stop=(cc == 7)),
                             [r_sq[kq], rC], [rPS[5]])
                    pend.clear()

                for u in range(6):
                    ws = nxt("w", 2)
                    dma(wbuf[ws][:], K.s_awin[l, u].rearrange("p (c n) -> p c n", c=8), [], [r_wb[ws]], ch_w[ws])

                    def mmf(fc, bank):
                        mm(PS[bank][:, 0:N], [(wbuf[ws][:, c, fc * 128:(fc + 1) * 128], xnT[:, c, 0:N]) for c in range(8)],
                           [r_wb[ws], r_xnT], [rPS[bank]])
                    if u < 4:
                        for j in range(2):
                            cc = 2 * u + j
                            ba = nxt("bank", 4); bg = nxt("bank", 4)
                            mmf(j, ba); mmf(2 + j, bg)
                            flush()
                            kt = nxt("th", 2)
                            S.op("act", I("activation", th[kt][:, 0:N], PS[bg][:, 0:N], AF.Tanh),
                                 [rPS[bg]], [r_th[kt]])
                            S.op("dve", I("scalar_tensor_tensor",
                                glu[:, cc, 30:30 + N], th[kt][:, 0:N], 1.0, PS[ba][:, 0:N], ALU.add, ALU.mult),
                                [r_th[kt], rPS[ba]], [r_glu[cc]])
                            if not samp:
                                dg = nxt("dg", 2)
                                S.op("pool", I("tensor_copy", gluB[dg][:, :], glu[:, cc, :]), [r_glu[cc]], [r_gluB[dg]])
                                for kk in range(NPE):
                                    S.op("act", I("mul", diag[dg][:, kk, :], identb[:, :], pcol[:, cc, kk:kk + 1]),
                                         [rC, r_pcol], [r_diag[dg]])
                                S.op("pe", [I("matmul", PS[6][:, 0:N], diag[dg][:, kk, :], gluB[dg][:, kk:kk + N],
                                              start=(kk == 0), stop=(kk == NPE - 1)) for kk in range(NPE)],
                                     [r_diag[dg], r_gluB[dg]], [rPS[6]])
                                S.op("dve", I("tensor_scalar",
                                    cv[:, cc, 0:N], glu[:, cc, NPE:NPE + N], pcol[:, cc, NPE:NPE + 1], pcol[:, cc, 31:32], ALU.mult, ALU.add),
                                    [r_glu[cc], r_pcol], [r_cv[cc]])
                                for kk in range(NPE + 1, 31):
                                    S.op("dve", I("scalar_tensor_tensor",
                                        cv[:, cc, 0:N], glu[:, cc, kk:kk + N], pcol[:, cc, kk:kk + 1], cv[:, cc, 0:N],
                                        ALU.mult, ALU.add), [r_glu[cc], r_pcol, r_cv[cc]], [r_cv[cc]])
                                S.op("dve", I("tensor_tensor", cv[:, cc, 0:N], cv[:, cc, 0:N], PS[6][:, 0:N], ALU.add),
                                     [r_cv[cc], rPS[6]], [r_cv[cc]])
                            else:
                                kq0 = nxt("sq", 2)
                                S.op("dve", I("tensor_tensor",
                                    sq[kq0][:, 0:480].rearrange("p (s k) -> p s k", s=16),
                                    stT[:, cc, :].rearrange("p (s k) -> p s k", s=16),
                                    pcol[:, cc, 0:30].unsqueeze(1).to_broadcast([128, 16, 30]), ALU.mult),
                                    [r_stT, r_pcol], [r_sq[kq0]])
                                S.op("dve", I("tensor_reduce",
                                    rr[:, 0:16], sq[kq0][:, 0:480].rearrange("p (s k) -> p s k", s=16), AX.X, ALU.add),
                                    [r_sq[kq0]], [r_rr])
                                S.op("dve", I("scalar_tensor_tensor",
                                    cv[:, cc, 0:16], glu[:, cc, 30:46], pcol[:, cc, 30:31], rr[:, 0:16], ALU.mult, ALU.add),
                                    [r_glu[cc], r_pcol, r_rr], [r_cv[cc]])
                                S.op("dve", I("tensor_scalar",
                                    cv[:, cc, 0:16], cv[:, cc, 0:16], pcol[:, cc, 31:32], None, ALU.add),
                                    [r_cv[cc], r_pcol], [r_cv[cc]])
                            kq = nxt("sq", 2)
                            S.op("act", I("activation", sq[kq][:, 0:N], cv[:, cc, 0:N], AF.Square),
                                 [r_cv[cc]], [r_sq[kq]])
                            pend.append((cc, kq))
                    else:
                        for fc in range(4):
                            cc = 4 * (u - 4) + fc
                            bz = nxt("bank", 4)
                            mmf(fc, bz)
                            flush()
                            kt = nxt("th", 2)
                            S.op("act", I("activation", th[kt][:, 0:N], PS[bz][:, 0:N], AF.Tanh),
                                 [rPS[bz]], [r_th[kt]])
                            S.op("dve", I("scalar_tensor_tensor",
                                sz[:, cc, 0:N], th[kt][:, 0:N], 1.0, PS[bz][:, 0:N], ALU.add, ALU.mult),
                                [r_th[kt], rPS[bz]], [r_sz[cc]])
                flush()
                kt = nxt("th", 2)
                S.op("act", I("activation", th[kt][:, 0:N], PS[4][:, 0:N], AF.Square), [rPS[4]], [r_th[kt]])
                S.op("dve", I("tensor_tensor", rstd_sb[:, 0:N], PS[5][:, 0:N], th[kt][:, 0:N], ALU.subtract),
                     [rPS[5], r_th[kt]], [r_rstd])
                S.op("act", I("activation", rstd_sb[:, 0:N], rstd_sb[:, 0:N], AF.Sqrt, bias=K.epsc[:, 0:1]),
                     [r_rstd, rC], [r_rstd])
                S.op("dve", I("reciprocal", rstd_sb[:, 0:N], rstd_sb[:, 0:N]), [r_rstd], [r_rstd])
                S.op("act", I("copy", mean_sb[:, 0:N], PS[4][:, 0:N]), [rPS[4]], [r_mean])
                for cc in range(8):
                    c_ = cv[:, cc, 0:N]
                    S.op("dve", I("tensor_tensor", c_, c_, mean_sb[:, 0:N], ALU.subtract), [r_cv[cc], r_mean], [r_cv[cc]])
                    S.op("dve", I("tensor_tensor", c_, c_, rstd_sb[:, 0:N], ALU.mult), [r_cv[cc], r_rstd], [r_cv[cc]])
                    S.op("dve", I("tensor_scalar", c_, c_, lg2[:, cc, 0:1], lg2[:, cc, 1:2], ALU.mult, ALU.add),
                         [r_cv[cc], r_pcol], [r_cv[cc]])
                    kt = nxt("th", 2)
                    S.op("act", I("activation", th[kt][:, 0:N], c_, AF.Tanh), [r_cv[cc]], [r_th[kt]])
                    S.op("dve", I("scalar_tensor_tensor", c_, th[kt][:, 0:N], 1.0, c_, ALU.add, ALU.mult),
                         [r_cv[cc], r_th[kt]], [r_cv[cc]])
                    S.op("dve", I("tensor_tensor", h[:, cc, 0:N], c_, sz[:, cc, 0:N], ALU.mult),
                         [r_cv[cc], r_sz[cc]], [r_h[cc]])
                for u in range(2):
                    ws = nxt("w", 2)
                    dma(wbuf[ws][:], K.s_awout[l, u].rearrange("p (c n) -> p c n", c=8), [], [r_wb[ws]], ch_w[ws])
                    for t in range(1 if samp else 4):
                        bk = nxt("bank", 4)
                        if samp:
                            mm(PS[bk][0:16, :], [(h[:, c, 0:16], wbuf[ws][:, c, :]) for c in range(8)], r_h + [r_wb[ws]], [rPS[bk]])
                            xs_ = XS[:, u * 512:(u + 1) * 512]
                            S.op("dve", I("tensor_tensor", xs_, xs_, PS[bk][0:16, :], ALU.add),
                                 [rXS, rPS[bk]], [rXS])
                        else:
                            i = 4 * blk + t
                            mm(PS[bk][:, :], [(h[:, c, t * 128:(t + 1) * 128], wbuf[ws][:, c, :]) for c in range(8)],
                               r_h + [r_wb[ws]], [rPS[bk]])
                            x_ = X[:, i, u * 512:(u + 1) * 512]
                            S.op("dve", I("tensor_tensor", x_, x_, PS[bk][:, :], ALU.add),
                                 [rX[i], rPS[bk]], [rX[i]])
                if blk < 3:
                    S.op("pool", I("tensor_copy", glu[:, :, 0:30], glu[:, :, 512:542]), r_glu, r_glu)
                elif blk == 3 or samp:
                    nr = 16 if samp else 30
                    c0 = 30 if samp else 512
                    for half in range(2):
                        S.op("pe", [I("transpose", PS[6][0:nr, j * 128:(j + 1) * 128], glu[:, 4 * half + j, c0:c0 + nr], identf[:, :])
                                    for j in range(4)], r_glu + [rC], [rPS[6]])
                        S.op("act", I("copy", rowt[0:nr, half * 512:(half + 1) * 512], PS[6][0:nr, :]),
                             [rPS[6]], [r_rowt])
                    if samp:
                        cs = K.convs_d[l].rearrange("(s k) d -> s k d", k=30)
                        dma(cs[:, 29, :], rowt[0:16, :], [r_rowt], [], ch_o)
                        dma(cs[:, 0:29, :], K.st_d[l].rearrange("(s k) d -> s k d", k=30)[:, 1:30, :], [], [], ch_o)
                    else:
                        dma(K.convp_d[l], rowt[0:30, :], [r_rowt], [], ch_o)
        S.barrier()


def make_in_maps(inputs, cores, NP=2560, attn=True, pt_override=None):
    consts = make_consts()
    f = lambda a: np.ascontiguousarray(a, dtype=np.float32)
    maps = []
    for c in cores:
        m = {
            "x": f(inputs["x_prompt"][c]),
            "xs": f(inputs["x_sample"][16 * c:16 * c + 16, 0]),
            "state": f(inputs["state_conv"][:, 16 * c:16 * c + 16]).reshape(2, 480, 1024),
            "a_norm": f(inputs["a_norm"]), "a_w_in": f(inputs["a_w_in"]), "a_cw": f(inputs["a_conv_w"]),
            "a_cb": f(inputs["a_conv_b"]), "a_lg": f(inputs["a_ln_g"]), "a_lb": f(inputs["a_ln_b"]),
            "a_w_out": f(inputs["a_w_out"]), "kv_norm": f(inputs["kv_norm"]).reshape(1, 1024), "kv_w": f(inputs["kv_w"]),
            "kv_fb": f(inputs["kv_fb"]).reshape(1, 16), "k_norm": f(inputs["k_norm"]).reshape(1, 64),
            "b_norm": f(inputs["b_norm"]), "b_w_in": f(inputs["b_w_in"]), "q_norm": f(inputs["q_norm"]),
            "b_w_out": f(inputs["b_w_out"]),
        }
        if attn:
            m["ck"] = inputs["cache_k"].reshape(NP * 128, 1024)
            m["cv"] = inputs["cache_v"].reshape(NP * 128, 1024)
            m["cl"] = inputs["cache_logf"].reshape(NP * 128, 16)
            pt = inputs["page_table"] if pt_override is None else pt_override
            m["pt"] = np.ascontiguousarray(pt[16 * c:16 * c + 16]).reshape(1, 256).astype(np.int32)
        for n, arr in consts.items():
            m["c_" + n] = arr
        maps.append(m)
    return maps


def assemble(results):
    cat = lambda k: np.stack([r[k] for r in results], 0)
    n = len(results)
    y = cat("y").reshape(n, 2048, 1024)
    ys = cat("ys").reshape(n * 16, 1, 1024)
    convp = np.transpose(cat("convp"), (1, 0, 2, 3))
    convs = np.transpose(cat("convs").reshape(n, 2, 16, 30, 1024), (1, 0, 2, 3, 4)).reshape(2, n * 16, 30, 1024)
    kp = cat("kp").reshape(n, 2048, 16, 64); vp = cat("vp").reshape(n, 2048, 16, 64); lfp = cat("lfp").reshape(n, 2048, 16)
    ks = cat("ks").reshape(n * 16, 1, 16, 64); vs = cat("vs").reshape(n * 16, 1, 16, 64); lfs = cat("lfs").reshape(n * 16, 1, 16)
    return tuple(np.ascontiguousarray(a, dtype=np.float32) for a in (y, ys, convp, convs, kp, vp, lfp, ks, vs, lfs))


def kernel(**inputs):
    NP = inputs["cache_k"].shape[0]
    nc = build(NP=NP)
    maps = make_in_maps(inputs, list(range(NCORES)), NP=NP)
    res = run_bass_kernel_spmd(nc, maps, core_ids=list(range(NCORES)))
    return assemble(res.results)


def phase_kv(K):
    nc, S, X, XS, C, PS, PSB = K.nc, K.S, K.X, K.XS, K.C, K.PS, K.PSB
    rX, rXS, rPS, rPSB, rC = K.rX, K.rXS, K.rPS, K.rPSB, K.rC
    dma, mm = K.dma, K.mm
    identb = C["identb"]
    PB = K.G

    def sbp(name, shape, dt):
        return PB.enter_context(nc.sbuf_tensor("pb_" + name, shape, dt))
    KT = sbp("KT", [128, 8, 2048], BF16); VA = sbp("VA", [128, 16, 1040], BF16)
    biasT = sbp("biasT", [128, 8, 16, 16], F32)
    ksn = sbp("ksn", [16, 1024], F32); vss = sbp("vss", [16, 1024], F32); lfs = sbp("lfs", [16, 16], F32)
    Cc = sbp("Cc", [128, 16, 16], F32); Rsb = sbp("Rsb", [128, 8, 16], F32)
    K.Cc, K.Rsb = Cc, Rsb
    K.KT, K.VA, K.biasT, K.ksn, K.vss, K.lfs = KT, VA, biasT, ksn, vss, lfs
    K.r_KT, K.r_VA, K.r_bias, K.r_ksn, K.r_vss, K.r_lfs = Res(), Res(), Res(), Res(), Res(), Res()
    with ExitStack() as P:
        def sb(name, shape, dt):
            return P.enter_context(nc.sbuf_tensor("kv_" + name, shape, dt))
        wkv = sb("wkv", [128, 8, 2048], BF16); wkf = sb("wkf", [128, 8, 16], BF16)
        gb = sb("gb", [128, 1024], F32); kg = sb("kg", [128, 64], F32); fbb = sb("fbb", [128, 16], F32)
        xn = sb("xn", [128, 1024], BF16); xnT = sb("xnT", [128, 8, 128], BF16)
        ofull = sb("ofull", [128, 1024], F32); sqt = sb("sqt", [128, 512], F32); kb16 = sb("kb16", [128, 1024], BF16)
        ss8 = sb("ss8", [128, 8], F32); lt = [sb("lt%d" % i, [128, 16], F32) for i in range(4)]
        lfa = sb("lfa", [128, 16, 16], F32); carry = sb("carry", [128, 16, 16], F32)
        NT = K.norm_tmp(P, "k")
        r_w, r_gb, r_xn, r_xnT, r_of, r_sq, r_kb, r_ss8, r_lt, r_lfa, r_carry, r_Cc, r_R = [Res() for _ in range(13)]
        ch_w = S.chan("kvw"); ch_o = S.chan("kvo")
        for u in range(4):
            dma(wkv[:, :, u * 512:(u + 1) * 512], K.s_kvw[u].rearrange("p (c n) -> p c n", c=8), [], [r_w], ch_w)
        dma(wkf[:], K.s_kvf[:, :].rearrange("p (c n) -> p c n", c=8), [], [r_w], ch_w)
        dma(gb[:], brow(K.kv_norm, 0, 1024, 128), [], [r_gb], ch_w)
        dma(kg[:], brow(K.k_norm, 0, 64, 128), [], [r_gb], ch_w)
        dma(fbb[:], brow(K.kv_fb, 0, 16, 128), [], [r_gb], ch_w)
        S.op("pool", I("memset", VA[:].rearrange("p i (h e) -> p (i h) e", e=65)[:, :, 64:65], 1.0), [], [K.r_VA])
        bank = [0]

        def nb():
            bank[0] = (bank[0] + 1) % 4
            return bank[0]
        kpv = K.kp_d.rearrange("(i p) d -> p i d", p=128); vpv = K.vp_d.rearrange("(i p) d -> p i d", p=128)
        lfv = K.lfp_d.rearrange("(i p) d -> p i d", p=128)
        for i in range(17):
            samp = i == 16
            Pn = 16 if samp else 128
            src, rs = (XS[:, :], rXS) if samp else (X[:, i, :], rX[i])
            rstd = K.rmsnorm_rstd(NT, [(src, rs)], Pn, 1, None, None, "k")
            S.op("dve", I("scalar_tensor_tensor", xn[:Pn, :], src, rstd[:Pn, 0:1], gb[:Pn, :], ALU.mult, ALU.mult),
                 [rs, NT["rrstd"], r_gb], [r_xn])
            S.op("pe", [I("transpose", PSB[:, c * Pn:(c + 1) * Pn], xn[:Pn, c * 128:(c + 1) * 128], identb[0:Pn, 0:Pn])
                        for c in range(8)], [r_xn, rC], [rPSB])
            S.op("act", I("copy", xnT[:, :, 0:Pn], PSB[:, 0:8 * Pn].rearrange("p (c n) -> p c n", c=8)), [rPSB], [r_xnT])
            for u in range(4):
                bk = nb()
                mm(PS[bk][:Pn, :], [(xnT[:, c, 0:Pn], wkv[:, c, u * 512:(u + 1) * 512]) for c in range(8)], [r_xnT, r_w], [rPS[bk]])
                osl = ofull[:Pn, (u % 2) * 512:(u % 2 + 1) * 512]
                if u < 2:
                    S.op("act", I("activation", sqt[:Pn, :], PS[bk][:Pn, :], AF.Square), [rPS[bk]], [r_sq])
                    S.op("dve", I("tensor_reduce", ss8[:Pn, :], sqt[:Pn, :].rearrange("p (h d) -> p h d", d=64), AX.X, ALU.add),
                         [r_sq], [r_ss8])
                    S.op("dve", I("tensor_scalar", ss8[:Pn, :], ss8[:Pn, :], 1.0 / 64.0, EPS, ALU.mult, ALU.add), [r_ss8], [r_ss8])
                    S.op("act", I("activation", ss8[:Pn, :], ss8[:Pn, :], AF.Sqrt), [r_ss8], [r_ss8])
                    S.op("dve", I("reciprocal", ss8[:Pn, :], ss8[:Pn, :]), [r_ss8], [r_ss8])
                    o3 = osl.rearrange("p (h d) -> p h d", d=64)
                    S.op("dve", I("tensor_tensor", o3, PS[bk][:Pn, :].rearrange("p (h d) -> p h d", d=64),
                                  ss8[:Pn, :].unsqueeze(2).to_broadcast([Pn, 8, 64]), ALU.mult), [rPS[bk], r_ss8], [r_of])
                    S.op("dve", I("tensor_tensor", o3, o3, kg[:Pn, :].unsqueeze(1).to_broadcast([Pn, 8, 64]), ALU.mult),
                         [r_of, r_gb], [r_of])
                    if u == 1:
                        if samp:
                            dma(K.ks_d, ofull[:16, :], [r_of], [], ch_o)
                            S.op("act", I("copy", ksn[:, :], ofull[:16, :]), [r_of], [K.r_ksn])
                        else:
                            dma(kpv[:, i, :], ofull[:, :], [r_of], [], ch_o)
                            S.op("pool", I("tensor_copy", kb16[:, :], ofull[:, :]), [r_of], [r_kb])
                            S.op("pe", [I("transpose", PSB[:, c * 128:(c + 1) * 128], kb16[:, c * 128:(c + 1) * 128], identb[:, :])
                                        for c in range(8)], [r_kb, rC], [rPSB])
                            S.op("act", I("copy", KT[:, :, i * 128:(i + 1) * 128], PSB[:, :].rearrange("p (c n) -> p c n", c=8)),
                                 [rPSB], [K.r_KT])
                else:
                    S.op("act", I("copy", osl, PS[bk][:Pn, :]), [rPS[bk]], [r_of])
                    if not samp:
                        va = VA[:, i, :].rearrange("p (h e) -> p h e", e=65)[:, 8 * (u - 2):8 * (u - 2) + 8, 0:64]
                        S.op("pool", I("tensor_copy", va, osl.rearrange("p (h d) -> p h d", d=64)), [r_of], [K.r_VA])
                    if u == 3:
                        if samp:
                            dma(K.vs_d, ofull[:16, :], [r_of], [], ch_o)
                            S.op("act", I("copy", vss[:, :], ofull[:16, :]), [r_of], [K.r_vss])
                        else:
                            dma(vpv[:, i, :], ofull[:, :], [r_of], [], ch_o)
            bk = nb()
            mm(PS[bk][:Pn, 0:16], [(xnT[:, c, 0:Pn], wkf[:, c, :]) for c in range(8)], [r_xnT, r_w], [rPS[bk]])
            u_, a_, e_, m_ = [t[:Pn, :] for t in lt]
            S.op("dve", I("tensor_tensor", u_, PS[bk][:Pn, 0:16], fbb[:Pn, :], ALU.add), [rPS[bk], r_gb], [r_lt])
            S.op("act", I("activation", a_, u_, AF.Abs), [r_lt], [r_lt])
            S.op("act", I("activation", e_, a_, AF.Exp, scale=-1.0), [r_lt], [r_lt])
            S.op("act", I("activation", e_, e_, AF.Ln, bias=K.onec[:Pn, 0:1]), [r_lt, rC], [r_lt])
            S.op("dve", I("tensor_scalar_min", m_, u_, 0.0), [r_lt], [r_lt])
            if samp:
                S.op("dve", I("tensor_tensor", lfs[:, :], m_, e_, ALU.subtract), [r_lt], [K.r_lfs])
                dma(K.lfs_d, lfs[:, :], [K.r_lfs], [], ch_o)
            else:
                S.op("dve", I("tensor_tensor", lfa[:, i, :], m_, e_, ALU.subtract), [r_lt], [r_lfa])
        dma(lfv, lfa[:, :, :], [r_lfa], [], ch_o)
        lff = lfa[:].rearrange("p i h -> p (i h)")
        mm(PS[5][:, 0:256], [(C["lincl"][:, :], lff)], [r_lfa, rC], [rPS[5]])
        mm(PS[4][:, 0:256], [(C["ones1"][:, :], lff)], [r_lfa, rC], [rPS[4]])
        S.op("dve", I("memset", carry[:, 0, :], 0.0), [], [r_carry])
        for i in range(1, 16):
            S.op("dve", I("tensor_tensor", carry[:, i, :], carry[:, i - 1, :], PS[4][:, (i - 1) * 16:i * 16], ALU.add),
                 [r_carry, rPS[4]], [r_carry])
        S.op("dve", I("tensor_tensor", Cc[:].rearrange("p i h -> p (i h)"), PS[5][:, 0:256], carry[:].rearrange("p i h -> p (i h)"),
                      ALU.add), [rPS[5], r_carry], [r_Cc])
        S.op("pe", [I("matmul", PS[6][:, m * 16:(m + 1) * 16], C["rs0"][:, :], Cc[:, 2 * m + 1, :], start=True, stop=True)
                    for m in range(8)], [r_Cc, rC], [rPS[6]])
        S.op("act", I("copy", Rsb[:].rearrange("p b h -> p (b h)"), PS[6][:, 0:128]), [rPS[6]], [r_R])
        for m in range(8):
            nj = 2 * m + 2
            S.op("dve", I("tensor_tensor", biasT[:, m, 0:nj, :], Rsb[:, m, :].unsqueeze(1).to_broadcast([128, nj, 16]),
                          Cc[:, 0:nj, :], ALU.subtract), [r_R, r_Cc], [K.r_bias])
        S.barrier()


def phase_b(K):
    nc, S, X, XS, C, PS, PSB = K.nc, K.S, K.X, K.XS, K.C, K.PS, K.PSB
    rX, rXS, rPS, rPSB, rC = K.rX, K.rXS, K.rPS, K.rPSB, K.rC
    dma, mm = K.dma, K.mm
    identb, tri = C["identb"], C["tri"]
    KT, VA, biasT = K.KT, K.VA, K.biasT
    for l in range(2):
        with ExitStack() as P:
            def sb(name, shape, dt):
                return P.enter_context(nc.sbuf_tensor("b%d_" % l + name, shape, dt))
            wb = [sb("wb%d" % i, [128, 8, 256], BF16) for i in range(2)]
            gb = sb("gb", [128, 1024], F32); qg = sb("qg", [128, 64], F32)
            xn = sb("xn", [128, 1024], BF16); xnT = sb("xnT", [128, 8, 512], BF16); QT = sb("QT", [128, 8, 512], BF16)
            qn = sb("qn", [128, 256], BF16); szb = sb("szb", [128, 4, 1024], BF16); g = sb("g", [128, 4, 1024], BF16)
            pt = [sb("pt%d" % i, [128, 256], BF16) for i in range(4)]
            th = sb("th", [128, 256], F32); sqt = sb("sqt", [128, 256], F32); ss4 = sb("ss4", [128, 4], F32)
            rec = sb("rec", [128, 4], F32)
            NT = K.norm_tmp(P, "b%d" % l)
            r_wb = [Res(), Res()]; r_gb, r_xn, r_xnT, r_QT, r_qn, r_g, r_th, r_sq, r_ss4, r_rec = [Res() for _ in range(10)]
            r_sz = [Res() for _ in range(4)]; r_pt = [Res() for _ in range(4)]
            ch_w = [S.chan("bw%d_%d" % (l, i)) for i in range(2)]; ch_p = S.chan("bp%d" % l)
            cnt = {"w": 0, "s": 0, "p": 0, "pt": 0}

            def nxt(key, n):
                v = cnt[key] % n
                cnt[key] += 1
                return v
            dma(gb[:], brow(K.b_norm, l, 1024, 128), [], [r_gb], ch_p)
            dma(qg[:], brow(K.q_norm, l, 64, 128), [], [r_gb], ch_p)
            for b in range(4):
                srcs = [(X[:, 4 * b + t, :], rX[4 * b + t]) for t in range(4)]
                rstd = K.rmsnorm_rstd(NT, srcs, 128, 4, None, None, "b")
                for t, (ap, r) in enumerate(srcs):
                    S.op("dve", I("scalar_tensor_tensor", xn[:, :], ap, rstd[:, t:t + 1], gb[:, :], ALU.mult, ALU.mult),
                         [r, NT["rrstd"], r_gb], [r_xn])
                    S.op("pe", [I("transpose", PSB[:, c * 128:(c + 1) * 128], xn[:, c * 128:(c + 1) * 128], identb[:, :])
                                for c in range(8)], [r_xn, rC], [rPSB])
                    S.op("act", I("copy", xnT[:, :, t * 128:(t + 1) * 128], PSB[:, :].rearrange("p (c n) -> p c n", c=8)),
                         [rPSB], [r_xnT])
                for v in range(8):
                    ws = nxt("w", 2)
                    dma(wb[ws][:], K.s_bwin[l, v // 2].rearrange("p (c n) -> p c n", c=8)[:, :, (v % 2) * 256:(v % 2 + 1) * 256],
                        [], [r_wb[ws]], ch_w[ws])
                    for t in range(4):
                        bk = 5 + nxt("p", 2)
                        mm(PS[bk][:, 0:256], [(xnT[:, c, t * 128:(t + 1) * 128], wb[ws][:, c, :]) for c in range(8)],
                           [r_xnT, r_wb[ws]], [rPS[bk]])
                        ps = PS[bk][:, 0:256]
                        if v < 4:
                            S.op("act", I("activation", sqt[:, :], ps, AF.Square), [rPS[bk]], [r_sq])
                            S.op("dve", I("tensor_reduce", ss4[:, :], sqt[:, :].rearrange("p (h d) -> p h d", d=64), AX.X, ALU.add),
                                 [r_sq], [r_ss4])
                            S.op("dve", I("tensor_scalar", ss4[:, :], ss4[:, :], 1.0 / 64.0, EPS, ALU.mult, ALU.add), [r_ss4], [r_ss4])
                            S.op("act", I("activation", ss4[:, :], ss4[:, :], AF.Sqrt), [r_ss4], [r_ss4])
                            S.op("dve", I("reciprocal", ss4[:, :], ss4[:, :]), [r_ss4], [r_ss4])
                            S.op("dve", I("tensor_tensor", th[:, :].rearrange("p (h d) -> p h d", d=64),
                                          ps.rearrange("p (h d) -> p h d", d=64),
                                          ss4[:, :].unsqueeze(2).to_broadcast([128, 4, 64]), ALU.mult), [rPS[bk], r_ss4], [r_th])
                            S.op("dve", I("tensor_tensor", qn[:, :].rearrange("p (h d) -> p h d", d=64),
                                          th[:, :].rearrange("p (h d) -> p h d", d=64),
                                          qg[:, :].unsqueeze(1).to_broadcast([128, 4, 64]), ALU.mult), [r_th, r_gb], [r_qn])
                            S.op("pe", [I("transpose", PSB[:, j * 128:(j + 1) * 128], qn[:, j * 128:(j + 1) * 128], identb[:, :])
                                        for j in range(2)], [r_qn, rC], [rPSB])
                            S.op("act", I("copy", QT[:, 2 * v:2 * v + 2, t * 128:(t + 1) * 128],
                                          PSB[:, 0:256].rearrange("p (c n) -> p c n", c=2)), [rPSB], [r_QT])
                        else:
                            S.op("act", I("activation", th[:, :], ps, AF.Tanh), [rPS[bk]], [r_th])
                            S.op("dve", I("scalar_tensor_tensor", szb[:, t, (v - 4) * 256:(v - 3) * 256], th[:, :], 1.0, ps,
                                          ALU.add, ALU.mult), [r_th, rPS[bk]], [r_sz[t]])
                if l == 0 and b == 0:
                    K.dbg("bias", biasT[:, 0, :, :].rearrange("p j h -> p (j h)"), [K.r_bias])
                    K.dbg("qt", QT[:, 0, 0:256], [r_QT])
                    K.dbg("kt", KT[:, 0, 0:256], [K.r_KT])
                    K.dbg("va", VA[:, 0, 0:256], [K.r_VA])
                    K.dbg("szb", szb[:, 0, 0:256], r_sz)
                steps = []
                for h in range(16):
                    for q2 in range(2):
                        m = 2 * b + q2
                        for j in range(2 * m + 2):
                            steps.append(dict(h=h, q2=q2, m=m, j=j, n0=max(0, j - 2 * m) * 128, last=(q2 == 1 and j == 2 * m + 1)))
                LA = 2

                def emit_front(st):
                    h, q2, m, j, n0 = st["h"], st["q2"], st["m"], st["j"], st["n0"]
                    c, po, q0 = h // 2, 64 * (h % 2), q2 * 256
                    bk = nxt("s", 3)
                    mm(PS[bk][:, n0:256], [(KT[po:po + 64, c, j * 128:(j + 1) * 128], QT[po:po + 64, c, q0 + n0:q0 + 256])],
                       [K.r_KT, r_QT], [rPS[bk]])
                    k = nxt("pt", 4)
                    st["k"] = k
                    S.op("act", I("activation", pt[k][:, n0:256], PS[bk][:, n0:256], AF.Exp,
                                  bias=biasT[:, m, j, h:h + 1], scale=0.125), [rPS[bk], K.r_bias], [r_pt[k]])
                    if j >= 2 * m:
                        S.op("pool", I("tensor_tensor", pt[k][:, n0:n0 + 128], pt[k][:, n0:n0 + 128], tri[:, :], ALU.mult),
                             [r_pt[k], rC], [r_pt[k]])

                def emit_back(st):
                    h, q2, m, j, n0, k = st["h"], st["q2"], st["m"], st["j"], st["n0"], st["k"]
                    ob = 3 + h % 2
                    S.op("pe", [I("matmul", PS[ob][:, (2 * q2 + tl) * 65:(2 * q2 + tl + 1) * 65], pt[k][:, tl * 128:(tl + 1) * 128],
                                  VA[:, j, h * 65:(h + 1) * 65], start=(j == 0 and q2 == 0 and tl == 0), stop=(j == 2 * m + tl))
                                for tl in range(n0 // 128, 2)], [r_pt[k], K.r_VA], [rPS[ob]])
                    if st["last"]:
                        o3 = PS[ob][:, 0:260].rearrange("p (t e) -> p t e", e=65)
                        S.op("dve", I("reciprocal", rec[:, :], o3[:, :, 64]), [rPS[ob]], [r_rec])
                        for tq in range(4):
                            S.op("dve", I("scalar_tensor_tensor", g[:, tq, h * 64:(h + 1) * 64], o3[:, tq, 0:64], rec[:, tq:tq + 1],
                                          szb[:, tq, h * 64:(h + 1) * 64], ALU.mult, ALU.mult), [rPS[ob], r_rec, r_sz[tq]], [r_g])

                for i in range(len(steps) + LA):
                    if i < len(steps):
                        emit_front(steps[i])
                    if i - LA >= 0:
                        emit_back(steps[i - LA])
                if l == 0 and b == 0:
                    K.dbg("g", g[:, 0, 0:256], [r_g])
                for t in range(4):
                    S.op("pe", [I("transpose", PSB[:, c * 128:(c + 1) * 128], g[:, t, c * 128:(c + 1) * 128], identb[:, :])
                                for c in range(8)], [r_g, rC], [rPSB])
                    S.op("act", I("copy", xnT[:, :, t * 128:(t + 1) * 128], PSB[:, :].rearrange("p (c n) -> p c n", c=8)),
                         [rPSB], [r_xnT])
                for v in range(4):
                    ws = nxt("w", 2)
                    dma(wb[ws][:], K.s_bwout[l, v // 2].rearrange("p (c n) -> p c n", c=8)[:, :, (v % 2) * 256:(v % 2 + 1) * 256],
                        [], [r_wb[ws]], ch_w[ws])
                    for t in range(4):
                        i = 4 * b + t
                        bk = 5 + nxt("p", 2)
                        mm(PS[bk][:, 0:256], [(xnT[:, c, t * 128:(t + 1) * 128], wb[ws][:, c, :]) for c in range(8)],
                           [r_xnT, r_wb[ws]], [rPS[bk]])
                        x_ = X[:, i, v * 256:(v + 1) * 256]
                        S.op("dve", I("tensor_tensor", x_, x_, PS[bk][:, 0:256], ALU.add), [rX[i], rPS[bk]], [rX[i]])
            S.barrier()
        if K.attn:
            phase_b_sample(K, l)


def phase_b_sample(K, l):
    nc, S, XS, C, PS, PSB = K.nc, K.S, K.XS, K.C, K.PS, K.PSB
    rXS, rPS, rPSB, rC = K.rXS, K.rPS, K.rPSB, K.rC
    dma, mm = K.dma, K.mm
    identb, identf, ones1, ustr = C["identb"], C["identf"], C["ones1"], C["ustr"]
    ksn, vss, lfs = K.ksn, K.vss, K.lfs
    cnt = {"w": 0, "p": 0, "k": 0, "v": 0}

    def nxt(key, n):
        v = cnt[key] % n
        cnt[key] += 1
        return v
    with ExitStack() as P:
        def sb(name, shape, dt, es=P):
            return es.enter_context(nc.sbuf_tensor("s%d_" % l + name, shape, dt))
        ec = sb("ec", [16, 256], F32); bmask = sb("bmask", [16, 1024], F32)
        xnT = sb("xnT", [128, 8, 16], BF16)
        qs = sb("qs", [16, 1024], F32); szs = sb("szs", [16, 1024], F32)
        lfsb = sb("lfsb", [128, 256], F32); dacc = sb("dacc", [16, 16], F32); pself = sb("pself", [16, 16], F32)
        masked = sb("masked", [16, 1024], F32); idx = sb("idx", [128, 256], I32)
        r_ec, r_xnT, r_qs, r_szs, r_lfsb, r_dacc, r_pself, r_masked, r_idx = [Res() for _ in range(9)]
        ch_c = S.chan("sc%d" % l); ch_w = [S.chan("sw%d_%d" % (l, i)) for i in range(2)]
        ch_k = [S.chan("sk%d_%d" % (l, i)) for i in range(4)]; ch_v = [S.chan("sv%d_%d" % (l, i)) for i in range(3)]
        ch_l = S.chan("sl%d" % l)
        dma(ec[:], K.cd["ec"], [], [r_ec], ch_c); dma(bmask[:], K.cd["bmask"], [], [r_ec], ch_c)
        with ExitStack() as T:
            wb = [sb("wb%d" % i, [128, 8, 256], BF16, T) for i in range(2)]
            gb = sb("gb", [16, 1024], F32, T); qg = sb("qg", [16, 64], F32, T)
            xn = sb("xn", [16, 1024], BF16, T)
            ptb = sb("ptb", [128, 256], I32, T); ptf = sb("ptf", [128, 256], F32, T)
            th = sb("th", [16, 256], F32, T); sqt = sb("sqt", [16, 256], F32, T); ss4 = sb("ss4", [16, 4], F32, T)
            NT = K.norm_tmp(T, "s%d" % l)
            r_wb = [Res(), Res()]; r_gb, r_qg, r_xn, r_th, r_sq, r_ss4 = [Res() for _ in range(6)]
            dma(gb[:], brow(K.b_norm, l, 1024, 16), [], [r_gb], ch_c); dma(qg[:], brow(K.q_norm, l, 64, 16), [], [r_qg], ch_c)
            dma(ptb[:], bass.AP(K.pt_d.tensor, 0, [[0, 128], [1, 256]]), [], [r_idx], ch_c)
            S.op("dve", I("tensor_copy", ptf[:, :], ptb[:, :]), [r_idx], [r_idx])
            S.op("dve", I("tensor_scalar", ptf[:, :], ptf[:, :], 128.0, C["pidx"][:, 0:1], ALU.mult, ALU.add), [r_idx, rC], [r_idx])
            S.op("dve", I("tensor_copy", idx[:, :], ptf[:, :]), [r_idx], [r_idx])
            S.op("dve", I("memset", dacc[:, :], 0.0), [], [r_dacc])
            rstd = K.rmsnorm_rstd(NT, [(XS[:, :], rXS)], 16, 1, None, None, "s")
            S.op("dve", I("scalar_tensor_tensor", xn[:, :], XS[:, :], rstd[:16, 0:1], gb[:, :], ALU.mult, ALU.mult),
                 [rXS, NT["rrstd"], r_gb], [r_xn])
            S.op("pe", [I("transpose", PSB[:, c * 16:(c + 1) * 16], xn[:, c * 128:(c + 1) * 128], identb[0:16, 0:16]) for c in range(8)],
                 [r_xn, rC], [rPSB])
            S.op("act", I("copy", xnT[:, :, :], PSB[:, 0:128].rearrange("p (c n) -> p c n", c=8)), [rPSB], [r_xnT])
            for v in range(8):
                ws = nxt("w", 2)
                dma(wb[ws][:], K.s_bwin[l, v // 2].rearrange("p (c n) -> p c n", c=8)[:, :, (v % 2) * 256:(v % 2 + 1) * 256],
                    [], [r_wb[ws]], ch_w[ws])
                bk = 5 + nxt("p", 2)
                mm(PS[bk][:16, 0:256], [(xnT[:, c, :], wb[ws][:, c, :]) for c in range(8)], [r_xnT, r_wb[ws]], [rPS[bk]])
                ps = PS[bk][:16, 0:256]
                if v < 4:
                    S.op("act", I("activation", sqt[:, :], ps, AF.Square), [rPS[bk]], [r_sq])
                    S.op("dve", I("tensor_reduce", ss4[:, :], sqt[:, :].rearrange("p (h d) -> p h d", d=64), AX.X, ALU.add), [r_sq], [r_ss4])
                    S.op("dve", I("tensor_scalar", ss4[:, :], ss4[:, :], 1.0 / 64.0, EPS, ALU.mult, ALU.add), [r_ss4], [r_ss4])
                    S.op("act", I("activation", ss4[:, :], ss4[:, :], AF.Sqrt), [r_ss4], [r_ss4])
                    S.op("dve", I("reciprocal", ss4[:, :], ss4[:, :]), [r_ss4], [r_ss4])
                    S.op("dve", I("tensor_tensor", th[:, :].rearrange("p (h d) -> p h d", d=64), ps.rearrange("p (h d) -> p h d", d=64),
                                  ss4[:, :].unsqueeze(2).to_broadcast([16, 4, 64]), ALU.mult), [rPS[bk], r_ss4], [r_th])
                    S.op("dve", I("tensor_tensor", qs[:, v * 256:(v + 1) * 256].rearrange("p (h d) -> p h d", d=64),
                                  th[:, :].rearrange("p (h d) -> p h d", d=64),
                                  qg[:, :].unsqueeze(1).to_broadcast([16, 4, 64]), ALU.mult), [r_th, r_qg], [r_qs])
                else:
                    S.op("act", I("activation", th[:, :], ps, AF.Tanh), [rPS[bk]], [r_th])
                    S.op("dve", I("scalar_tensor_tensor", szs[:, (v - 4) * 256:(v - 3) * 256], th[:, :], 1.0, ps, ALU.add, ALU.mult),
                         [r_th, rPS[bk]], [r_szs])
            S.op("dve", I("tensor_tensor", masked[:, :], qs[:, :], ksn[:, :], ALU.mult), [r_qs, K.r_ksn], [r_masked])
            S.op("dve", I("tensor_reduce", pself[:, :], masked[:, :].rearrange("p (h d) -> p h d", d=64), AX.X, ALU.add),
                 [r_masked], [r_pself])
            S.op("act", I("activation", pself[:, :], pself[:, :], AF.Exp, scale=0.125), [r_pself], [r_pself])
            S.op("pe", [I("matmul", PS[6][:, s * 16:(s + 1) * 16], identf[0:16, s:s + 1].to_broadcast([16, 128]), lfs[:, :],
                          start=True, stop=True) for s in range(16)], [K.r_lfs, rC], [rPS[6]])
            S.op("act", I("copy", lfsb[:, :], PS[6][:, 0:256]), [rPS[6]], [r_lfsb])
            S.barrier()
        with ExitStack() as T:
            qb = sb("qb", [128, 1024], F32, T)
            kb = [sb("kb%d" % i, [128, 1024], F32, T) for i in range(3)]
            vb = [sb("vb%d" % i, [128, 1024], F32, T) for i in range(2)]
            vb16 = [sb("vb16_%d" % i, [128, 1024], BF16, T) for i in range(2)]
            sc = sb("sc", [128, 256], F32, T); bias_s = sb("bias_s", [128, 256], F32, T)
            p16 = [sb("p16_%d" % i, [128, 256], BF16, T) for i in range(2)]
            lfpg = sb("lfpg", [128, 256], F32, T); suf = sb("suf", [128, 256], F32, T)
            psumh = sb("psumh", [128, 16], F32, T); den_sb = [sb("den%d" % i, [16, 1], F32, T) for i in range(2)]
            dmask = sb("dmask", [16, 16], F32, T)
            r_qb, r_sc, r_bias, r_lfpg, r_suf, r_psumh, r_dmask = [Res() for _ in range(7)]
            r_lfpgs = [Res() for _ in range(16)]
            r_kb = [Res() for _ in range(3)]; r_vb = [Res(), Res(), Res()]; r_vb16 = [Res(), Res()]
            r_p16 = [Res(), Res()]; r_den = [Res(), Res()]
            ck2, cv2, cl2 = K.ck_d, K.cv_d, K.cl_d

            def k_pass(s):
                pbuf = s % 2
                for hf in range(2):
                    mm(PS[hf][:, :], [(identf[0:16, s:s + 1].to_broadcast([16, 128]), qs[:, hf * 512:(hf + 1) * 512])],
                       [r_qs, rC], [rPS[hf]])
                    S.op("act", I("copy", qb[:, hf * 512:(hf + 1) * 512], PS[hf][:, :]), [rPS[hf]], [r_qb])
                for pg in range(16):
                    S.op("pool", I("indirect_dma_start", out=lfpg[:, pg * 16:(pg + 1) * 16], out_offset=None, in_=cl2,
                                   in_offset=bass.IndirectOffsetOnAxis(ap=idx[:, s * 16 + pg:s * 16 + pg + 1], axis=0)),
                         [r_idx], [r_lfpgs[pg]], chan=ch_l)
                yield
                mm(PS[6][:, 0:256], [(ustr[:, :], lfpg[:, :])], r_lfpgs + [rC], [rPS[6]])
                mm(PS[6][:, 256:512], [(ones1[:, :], lfpg[:, :])], r_lfpgs + [rC], [rPS[6]])
                S.op("dve", I("tensor_copy", suf[:, 240:256], lfsb[:, s * 16:(s + 1) * 16]), [r_lfsb], [r_suf])
                for pg in range(14, -1, -1):
                    S.op("dve", I("tensor_tensor", suf[:, pg * 16:(pg + 1) * 16], suf[:, (pg + 1) * 16:(pg + 2) * 16],
                                  PS[6][:, 256 + (pg + 1) * 16:256 + (pg + 2) * 16], ALU.add), [r_suf, rPS[6]], [r_suf])
                S.op("dve", I("tensor_tensor", bias_s[:, :], PS[6][:, 0:256], suf[:, :], ALU.add), [rPS[6], r_suf], [r_bias])
                yield
                for pg in range(16):
                    k = nxt("k", 3)
                    S.op("pool", I("indirect_dma_start", out=kb[k][:, :], out_offset=None, in_=ck2,
                                   in_offset=bass.IndirectOffsetOnAxis(ap=idx[:, s * 16 + pg:s * 16 + pg + 1], axis=0)),
                         [r_idx], [r_kb[k]], chan=ch_k[k])
                    S.op("dve", I("tensor_tensor", kb[k][:, :], kb[k][:, :], qb[:, :], ALU.mult), [r_kb[k], r_qb], [r_kb[k]])
                    S.op("dve", I("tensor_reduce", sc[:, pg * 16:(pg + 1) * 16], kb[k][:, :].rearrange("p (h d) -> p h d", d=64),
                                  AX.X, ALU.add), [r_kb[k]], [r_sc])
                    yield
                S.op("dve", I("scalar_tensor_tensor", sc[:, :], sc[:, :], 0.125, bias_s[:, :], ALU.mult, ALU.add), [r_sc, r_bias], [r_sc])
                S.op("act", I("activation", p16[pbuf][:, :], sc[:, :], AF.Exp), [r_sc], [r_p16[pbuf]])
                S.op("dve", I("tensor_reduce", psumh[:, :], p16[pbuf][:, :].rearrange("p (g h) -> p h g", h=16), AX.X, ALU.add),
                     [r_p16[pbuf]], [r_psumh])
                mm(PS[6][:16, 0:1], [(psumh[:, :], ones1[:, 0:1])], [r_psumh, rC], [rPS[6]])
                S.op("act", I("copy", den_sb[pbuf][:, :], PS[6][:16, 0:1]), [rPS[6]], [r_den[pbuf]])
                yield

            def v_pass(s):
                pbuf = s % 2
                for pg in range(16):
                    k = nxt("v", 2); k2 = pg % 2
                    S.op("pool", I("indirect_dma_start", out=vb[k][:, :], out_offset=None, in_=cv2,
                                   in_offset=bass.IndirectOffsetOnAxis(ap=idx[:, s * 16 + pg:s * 16 + pg + 1], axis=0)),
                         [r_idx], [r_vb[k]], chan=ch_v[k])
                    S.op("act", I("copy", vb16[k2][:, :], vb[k][:, :]), [r_vb[k]], [r_vb16[k2]])
                    for hf in range(2):
                        S.op("pe", I("matmul", PS[2 + hf][:16, :], p16[pbuf][:, pg * 16:(pg + 1) * 16], vb16[k2][:, hf * 512:(hf + 1) * 512],
                                     start=(pg == 0), stop=(pg == 15)), [r_p16[pbuf], r_vb16[k2]], [rPS[2 + hf]])
                    yield
                for hf in range(2):
                    S.op("dve", I("tensor_tensor", masked[:, hf * 512:(hf + 1) * 512], PS[2 + hf][:16, :], bmask[:, hf * 512:(hf + 1) * 512],
                                  ALU.mult), [rPS[2 + hf], r_ec], [r_masked])
                    S.op("pe", I("matmul", PS[4 + hf][:16, :], ec[:, s * 16:(s + 1) * 16], masked[:, hf * 512:(hf + 1) * 512],
                                 start=(s == 0), stop=(s == 15)), [r_masked, r_ec], [rPS[4 + hf]])
                S.op("dve", I("tensor_scalar", dmask[:, :], identf[0:16, 0:16], den_sb[pbuf][:, 0:1], None, ALU.mult),
                     [r_den[pbuf], rC], [r_dmask])
                mm(PS[6][:16, 32:48], [(ec[:, s * 16:(s + 1) * 16], dmask[:, :])], [r_dmask, r_ec], [rPS[6]])
                S.op("dve", I("tensor_tensor", dacc[:, :], dacc[:, :], PS[6][:16, 32:48], ALU.add), [r_dacc, rPS[6]], [r_dacc])
                yield

            for s in range(17):
                gens = []
                if s < 16:
                    gens.append(k_pass(s))
                if s >= 1:
                    gens.append(v_pass(s - 1))
                while gens:
                    for g_ in list(gens):
                        try:
                            next(g_)
                        except StopIteration:
                            gens.remove(g_)
            S.barrier()
        with ExitStack() as T:
            wb = [sb("wc%d" % i, [128, 8, 256], BF16, T) for i in range(2)]
            xn = sb("xo", [16, 1024], BF16, T)
            r_wb = [Res(), Res()]; r_xn = Res()
            S.op("dve", I("tensor_tensor", masked[:, :].rearrange("p (h d) -> p h d", d=64), vss[:, :].rearrange("p (h d) -> p h d", d=64),
                          pself[:, :].unsqueeze(2).to_broadcast([16, 16, 64]), ALU.mult), [K.r_vss, r_pself], [r_masked])
            for hf in range(2):
                S.op("dve", I("tensor_tensor", masked[:, hf * 512:(hf + 1) * 512], masked[:, hf * 512:(hf + 1) * 512], PS[4 + hf][:16, :],
                              ALU.add), [r_masked, rPS[4 + hf]], [r_masked])
            S.op("dve", I("tensor_tensor", dacc[:, :], dacc[:, :], pself[:, :], ALU.add), [r_dacc, r_pself], [r_dacc])
            S.op("dve", I("reciprocal", dacc[:, :], dacc[:, :]), [r_dacc], [r_dacc])
            S.op("dve", I("tensor_tensor", masked[:, :].rearrange("p (h d) -> p h d", d=64), masked[:, :].rearrange("p (h d) -> p h d", d=64),
                          dacc[:, :].unsqueeze(2).to_broadcast([16, 16, 64]), ALU.mult), [r_masked, r_dacc], [r_masked])
            S.op("dve", I("tensor_tensor", xn[:, :], masked[:, :], szs[:, :], ALU.mult), [r_masked, r_szs], [r_xn])
            S.op("pe", [I("transpose", PSB[:, c * 16:(c + 1) * 16], xn[:, c * 128:(c + 1) * 128], identb[0:16, 0:16]) for c in range(8)],
                 [r_xn, rC], [rPSB])
            S.op("act", I("copy", xnT[:, :, :], PSB[:, 0:128].rearrange("p (c n) -> p c n", c=8)), [rPSB], [r_xnT])
            for v in range(4):
                ws = nxt("w", 2)
                dma(wb[ws][:], K.s_bwout[l, v // 2].rearrange("p (c n) -> p c n", c=8)[:, :, (v % 2) * 256:(v % 2 + 1) * 256],
                    [], [r_wb[ws]], ch_w[ws])
                bk = 5 + nxt("p", 2)
                mm(PS[bk][:16, 0:256], [(xnT[:, c, :], wb[ws][:, c, :]) for c in range(8)], [r_xnT, r_wb[ws]], [rPS[bk]])
                xs_ = XS[:, v * 256:(v + 1) * 256]
                S.op("dve", I("tensor_tensor", xs_, xs_, PS[bk][:16, 0:256], ALU.add), [rXS, rPS[bk]], [rXS])
            S.barrier()
```

```python
import numpy as np
import ml_dtypes
from contextlib import ExitStack
import concourse.bass as bass
import concourse.mybir as mybir
from concourse.bass_utils import run_bass_kernel_spmd

F32 = mybir.dt.float32
BF16 = mybir.dt.bfloat16
I32 = mybir.dt.int32
ALU = mybir.AluOpType
AF = mybir.ActivationFunctionType
AX = mybir.AxisListType
EPS = 1e-6
NCORES = 8


def I(name, *a, **kw):
    return (name, a, kw)


class Res:
    __slots__ = ("name", "w", "r")

    def __init__(self, name=""):
        self.name = name
        self.w = None
        self.r = {}


class Chan:
    __slots__ = ("sem", "count", "name")

    def __init__(self, sem, name):
        self.sem = sem
        self.count = 0
        self.name = name


class Sched:
    ENG = ("pe", "act", "dve", "pool", "sp")
    CE = ("pe", "act", "dve", "pool")

    def __init__(self, nc, esem, free):
        self.nc = nc
        self.items = {e: [] for e in self.ENG}
        self.cnt = {e: 0 for e in self.ENG}
        self.waited = {e: {} for e in self.ENG}
        self.esem = esem
        self.free = list(free)
        self.chans = []

    def chan(self, name=""):
        c = Chan(self.free.pop(), name)
        self.chans.append(c)
        return c

    def op(self, eng, fn, reads=(), writes=(), chan=None):
        deps = {}

        def add(k, v):
            if deps.get(k, 0) < v:
                deps[k] = v
        for r in reads:
            if r.w is not None:
                add(*r.w)
        for w in writes:
            if w.w is not None:
                add(*w.w)
            for k, v in w.r.items():
                add(k, v)
        waits = []
        wd = self.waited[eng]
        for k, v in deps.items():
            if chan is None and k == eng and eng == "pe":
                continue
            if isinstance(k, Chan):
                v = k.count
            if wd.get(k, 0) >= v:
                continue
            wd[k] = v
            waits.append((k.sem if isinstance(k, Chan) else self.esem[k], v))
        if chan is not None:
            chan.count += 16
            ev = (chan, chan.count)
            inc = (chan.sem, 16)
        else:
            self.cnt[eng] += 1
            ev = (eng, self.cnt[eng])
            inc = (self.esem[eng], 1)
        self.items[eng].append((waits, fn, inc))
        for w in writes:
            w.w = ev
            w.r = {}
        for r in reads:
            if r.r.get(ev[0], 0) < ev[1]:
                r.r[ev[0]] = ev[1]
        return ev

    def barrier(self):
        for e in self.ENG:
            waits = []
            wd = self.waited[e]
            for c in self.chans:
                if c.count and wd.get(c, 0) < c.count:
                    wd[c] = c.count
                    waits.append((c.sem, c.count))
            for o in self.CE:
                if o != e and self.cnt[o] and wd.get(o, 0) < self.cnt[o]:
                    wd[o] = self.cnt[o]
                    waits.append((self.esem[o], self.cnt[o]))
            if waits:
                self.items[e].append((waits, None, None))

    def emit(self, block):
        items = self.items

        def runner(name):
            def run(e):
                for waits, fn, inc in items[name]:
                    for sem, val in waits:
                        e.wait_ge(sem, val)
                    if fn is not None:
                        calls = fn if isinstance(fn, list) else [fn]
                        for (nm, a, kw) in calls:
                            ins = getattr(e, nm)(*a, **kw)
                        ins.then_inc(inc[0], inc[1])
            return run
        block.tensor(runner("pe"))
        block.scalar(runner("act"))
        block.vector(runner("dve"))
        block.gpsimd(runner("pool"))
        block.sync(runner("sp"))


def make_consts():
    bf = ml_dtypes.bfloat16
    k = np.arange(128)
    c = {}
    c["identb"] = np.eye(128, dtype=np.float32).astype(bf)
    c["identf"] = np.eye(128, dtype=np.float32)
    c["tri"] = (k[None, :] >= k[:, None]).astype(np.float32).astype(bf)
    c["lincl"] = (k[:, None] <= k[None, :]).astype(np.float32)
    c["ustr"] = (k[:, None] > k[None, :]).astype(np.float32)
    c["onesdiv"] = np.full((128, 128), 1.0 / 1024.0, np.float32)
    c["ones1"] = np.ones((128, 128), np.float32)
    c["rs0"] = np.zeros((128, 128), np.float32)
    c["rs0"][0, :] = 1.0
    eb = np.zeros((16, 16, 128), np.float32)
    ec = np.zeros((16, 16, 16), np.float32)
    for s in range(16):
        eb[s, s, :] = 1.0
        ec[:, s, s] = 1.0
    c["eb"] = eb.reshape(16, 2048)
    c["ec"] = ec.reshape(16, 256)
    bm = np.zeros((16, 1024), np.float32)
    for h in range(16):
        bm[h, h * 64:(h + 1) * 64] = 1.0
    c["bmask"] = bm
    c["pidx"] = k.astype(np.float32)[:, None].copy()
    return c


CONST_SPECS = [("identb", [128, 128], BF16), ("identf", [128, 128], F32), ("tri", [128, 128], BF16),
               ("lincl", [128, 128], F32), ("ustr", [128, 128], F32), ("onesdiv", [128, 128], F32),
               ("ones1", [128, 128], F32), ("rs0", [128, 128], F32), ("eb", [16, 2048], F32),
               ("ec", [16, 256], F32), ("bmask", [16, 1024], F32), ("pidx", [128, 1], F32)]


def build(NP=2560, upto=99, attn=True):
    nc = bass.Bass("TRN2", target_bir_lowering=False)

    def din(name, shape, dt=F32):
        return nc.dram_tensor(name, shape, dt, kind="ExternalInput").ap()

    def dout(name, shape, dt=F32):
        return nc.dram_tensor(name, shape, dt, kind="ExternalOutput").ap()

    def dscr(name, shape, dt=BF16):
        return nc.dram_tensor(name, shape, dt, kind="Internal").ap()

    x_d = din("x", [2048, 1024]); xs_d = din("xs", [16, 1024]); st_d = din("state", [2, 480, 1024])
    if attn:
        ck_d = din("ck", [NP * 128, 1024]); cv_d = din("cv", [NP * 128, 1024]); cl_d = din("cl", [NP * 128, 16])
        pt_d = din("pt", [1, 256], I32)
    a_norm = din("a_norm", [2, 1024]); a_w_in = din("a_w_in", [2, 1024, 3072]); a_cw = din("a_cw", [2, 31, 1024])
    a_cb = din("a_cb", [2, 1024]); a_lg = din("a_lg", [2, 1024]); a_lb = din("a_lb", [2, 1024])
    a_w_out = din("a_w_out", [2, 1024, 1024]); kv_norm = din("kv_norm", [1, 1024]); kv_w = din("kv_w", [1024, 2064])
    kv_fb = din("kv_fb", [1, 16]); k_norm = din("k_norm", [1, 64]); b_norm = din("b_norm", [2, 1024])
    b_w_in = din("b_w_in", [2, 1024, 2048]); q_norm = din("q_norm", [2, 64]); b_w_out = din("b_w_out", [2, 1024, 1024])
    cd = {n: din("c_" + n, s, dt) for n, s, dt in CONST_SPECS}

    y_d = dout("y", [2048, 1024]); ys_d = dout("ys", [16, 1024]); convp_d = dout("convp", [2, 30, 1024])
    convs_d = dout("convs", [2, 480, 1024]); kp_d = dout("kp", [2048, 1024]); vp_d = dout("vp", [2048, 1024])
    lfp_d = dout("lfp", [2048, 16]); ks_d = dout("ks", [16, 1024]); vs_d = dout("vs", [16, 1024]); lfs_d = dout("lfs", [16, 16])

    s_awin = dscr("s_awin", [2, 6, 128, 4096]); s_awout = dscr("s_awout", [2, 2, 128, 4096])
    s_kvw = dscr("s_kvw", [4, 128, 4096]); s_kvf = dscr("s_kvf", [128, 128])
    s_bwin = dscr("s_bwin", [2, 4, 128, 4096]); s_bwout = dscr("s_bwout", [2, 2, 128, 4096])

    with ExitStack() as G:
        esem = {e: G.enter_context(nc.semaphore("s_" + e)) for e in Sched.CE}
        free = [G.enter_context(nc.semaphore("ch%d" % i)) for i in range(80)]
        S = Sched(nc, esem, free)

        def sbt(es, name, shape, dt):
            return es.enter_context(nc.sbuf_tensor(name, shape, dt))

        def dma(dst, src, reads, writes, ch, eng="sp"):
            S.op(eng, I("dma_start", out=dst, in_=src), reads, writes, chan=ch)

        def mm(out_ap, pairs, reads, writes):
            n = len(pairs)
            S.op("pe", [I("matmul", out_ap, l, r, start=(i == 0), stop=(i == n - 1)) for i, (l, r) in enumerate(pairs)],
                 reads, writes)

        X = sbt(G, "X", [128, 16, 1024], F32); XS = sbt(G, "XS", [16, 1024], F32)
        C = {n: sbt(G, "k_" + n, s, dt) for n, s, dt in CONST_SPECS if n not in ("eb", "ec", "bmask")}
        epsc = sbt(G, "epsc", [128, 1], F32); onec = sbt(G, "onec", [128, 1], F32)
        rX = [Res("X%d" % i) for i in range(16)]; rXS = Res("XS"); rC = Res("consts")
        PS = [G.enter_context(nc.psum_tensor("ps%d" % i, [128, 512], F32)) for i in range(7)]
        PSB = G.enter_context(nc.psum_tensor("psb", [128, 1024], BF16))
        rPS = [Res("ps%d" % i) for i in range(7)]; rPSB = Res("psb")
        ld_c = S.chan("ldc"); ld_x = S.chan("ldx")
        for n in C:
            dma(C[n][:], cd[n], [], [rC], ld_c)
        S.op("dve", I("memset", epsc[:], EPS), [], [rC])
        S.op("dve", I("memset", onec[:], 1.0), [], [rC])
        xv = x_d.rearrange("(i p) d -> p i d", p=128)
        for i in range(16):
            dma(X[:, i, :], xv[:, i, :], [], [rX[i]], ld_x)
        dma(XS[:], xs_d, [], [rXS], ld_x)

        with ExitStack() as P0:
            stg = [sbt(P0, "stg%d" % i, [128, 8, 512], F32) for i in range(2)]
            cvb = [sbt(P0, "cvb%d" % i, [128, 8, 512], BF16) for i in range(2)]
            rstg = [Res(), Res()]; rcvb = [Res(), Res()]
            chl = [S.chan("cvl0"), S.chan("cvl1")]; chs = [S.chan("cvs0"), S.chan("cvs1")]
            rscr = Res("scratch")
            state = {"n": 0}

            def conv_unit(W, pieces, dst, scale, width=512):
                k = state["n"] % 2
                state["n"] += 1
                for (c0, w, d0) in pieces:
                    dma(stg[k][:, :, d0:d0 + w], W[:, c0:c0 + w].rearrange("(c p) n -> p c n", p=128), [], [rstg[k]], chl[k])
                eng = ("dve", "act")[state["n"] % 2]
                src = stg[k][:, :, 0:width]; dstt = cvb[k][:, :, 0:width]
                if eng == "act":
                    S.op("act", I("mul", dstt, src, scale), [rstg[k]], [rcvb[k]])
                elif scale == 1.0:
                    S.op(eng, I("tensor_copy", dstt, src), [rstg[k]], [rcvb[k]])
                else:
                    S.op(eng, I("tensor_scalar", dstt, src, scale, None, ALU.mult), [rstg[k]], [rcvb[k]])
                dma(dst.rearrange("p (c n) -> p c n", c=8), dstt, [rcvb[k]], [rscr], chs[k])

            for l in range(2):
                for u in range(4):
                    conv_unit(a_w_in[l], [(256 * u, 256, 0), (1024 + 256 * u, 256, 256)], s_awin[l, u], 0.5)
                for u in range(2):
                    conv_unit(a_w_in[l], [(2048 + 512 * u, 512, 0)], s_awin[l, 4 + u], 0.5)
                for u in range(2):
                    conv_unit(a_w_out[l], [(512 * u, 512, 0)], s_awout[l, u], 1.0)
            for u in range(4):
                conv_unit(kv_w, [(512 * u, 512, 0)], s_kvw[u], 1.0)
            conv_unit(kv_w, [(2048, 16, 0)], s_kvf, 1.0, width=16)
            for l in range(2):
                for u in range(4):
                    conv_unit(b_w_in[l], [(512 * u, 512, 0)], s_bwin[l, u], 1.0 if u < 2 else 0.5)
                for u in range(2):
                    conv_unit(b_w_out[l], [(512 * u, 512, 0)], s_bwout[l, u], 1.0)
            S.barrier()

        def rmsnorm_rstd(es_tmp, srcs, P, ncol, rres, wres, tag):
            ss = es_tmp["ss"]; junk = es_tmp["junk"]; rstd = es_tmp["rstd"]
            for i, (ap, r) in enumerate(srcs):
                S.op("act", I("activation", junk[:P, :], ap, AF.Square, accum_out=ss[:P, i:i + 1]),
                     [r], [es_tmp["rjunk"], es_tmp["rss"]])
            n = len(srcs)
            S.op("dve", I("tensor_scalar", rstd[:P, 0:n], ss[:P, 0:n], 1.0 / 1024.0, EPS, ALU.mult, ALU.add),
                 [es_tmp["rss"]], [es_tmp["rrstd"]])
            S.op("act", I("activation", rstd[:P, 0:n], rstd[:P, 0:n], AF.Sqrt), [es_tmp["rrstd"]], [es_tmp["rrstd"]])
            S.op("dve", I("reciprocal", rstd[:P, 0:n], rstd[:P, 0:n]), [es_tmp["rrstd"]], [es_tmp["rrstd"]])
            return rstd

        def norm_tmp(es, tag):
            return dict(ss=sbt(es, "ss" + tag, [128, 8], F32), junk=sbt(es, "junk" + tag, [128, 1024], BF16),
                        rstd=sbt(es, "rstd" + tag, [128, 8], F32), rjunk=Res(), rss=Res(), rrstd=Res())

        import os as _os
        dbg_list = []
        DEBUG = bool(_os.environ.get("KDEBUG"))
        dbg_ch = S.chan("dbg") if DEBUG else None

        dbg_state = {}
        if DEBUG:
            dbg_state["t"] = G.enter_context(nc.sbuf_tensor("dbgt", [128, 256], F32))
            dbg_state["r"] = Res()

        def dbg(name, ap, reads, P=128, N=256):
            if not DEBUG:
                return
            if "t" not in dbg_state:
                dbg_state["t"] = G.enter_context(nc.sbuf_tensor("dbgt", [128, 256], F32))
                dbg_state["r"] = Res()
            t, r = dbg_state["t"], dbg_state["r"]
            d = nc.dram_tensor("dbg_" + name, [P, N], F32, kind="ExternalOutput").ap()
            S.op("dve", I("tensor_copy", t[:P, 0:N], ap), reads, [r])
            dma(d, t[:P, 0:N], [r], [], dbg_ch)
        import types
        K = types.SimpleNamespace(**{k: v for k, v in locals().items() if k != "K"})
        K.G = G
        if upto >= 1:
            phase_a(K)
        if upto >= 2:
            phase_kv(K)
        if upto >= 3:
            phase_b(K)

        st_y = S.chan("sty")
        yv = y_d.rearrange("(i p) d -> p i d", p=128)
        for i in range(16):
            dma(yv[:, i, :], X[:, i, :], [rX[i]], [], st_y)
        dma(ys_d, XS[:], [rXS], [], st_y)
        S.barrier()
        with nc.Block() as block:
            S.emit(block)
    return nc


def brow(ap2d, row, ncols, P):
    t = ap2d.tensor
    w = ap2d.shape[1]
    return bass.AP(t, row * w, [[0, P], [1, ncols]])


def phase_a(K):
    nc, S, X, XS, C, PS, PSB = K.nc, K.S, K.X, K.XS, K.C, K.PS, K.PSB
    rX, rXS, rPS, rPSB, rC = K.rX, K.rXS, K.rPS, K.rPSB, K.rC
    dma, mm = K.dma, K.mm
    identb, identf, onesdiv = C["identb"], C["identf"], C["onesdiv"]
    with ExitStack() as P:
        def sb(name, shape, dt):
            return P.enter_context(nc.sbuf_tensor("pa_" + name, shape, dt))
        gb = sb("gb", [128, 1024], F32); prow = sb("prow", [34, 1024], F32)
        pcol = sb("pcol", [128, 8, 34], F32); lg2 = sb("lg2", [128, 8, 2], F32)
        xn = [sb("xn%d" % i, [128, 1024], BF16) for i in range(2)]
        xnT = sb("xnT", [128, 8, 512], BF16)
        wbuf = [sb("wb%d" % i, [128, 8, 512], BF16) for i in range(2)]
        glu = sb("glu", [128, 8, 542], F32)
        th = [sb("th%d" % i, [128, 512], F32) for i in range(2)]
        cv = sb("cv", [128, 8, 512], F32)
        sq = [sb("sq%d" % i, [128, 512], F32) for i in range(2)]
        mean_sb = sb("mean_sb", [128, 512], F32); rstd_sb = sb("rstd_sb", [128, 512], F32)
        sz = sb("sz", [128, 8, 512], BF16); h = sb("h", [128, 8, 512], BF16)
        sttok = [sb("sttok0", [120, 1024], F32)] * 2
        stT = sb("stT", [128, 8, 480], F32); rowt = sttok[0]
        NPE = 16
        diag = [sb("diag%d" % i, [128, NPE, 128], BF16) for i in range(2)]
        gluB = [sb("gluB%d" % i, [128, 542], BF16) for i in range(2)]
        r_diag = [Res(), Res()]; r_gluB = [Res(), Res()]
        rr = sb("rr", [128, 16], F32)
        NT = K.norm_tmp(P, "a")
        r_gb, r_prow, r_pcol = Res(), Res(), Res()
        r_xn = [Res(), Res()]; r_xnT = Res(); r_wb = [Res(), Res()]
        r_glu = [Res() for _ in range(8)]; r_cv = [Res() for _ in range(8)]; r_th = [Res(), Res()]
        r_sq = [Res(), Res()]; r_mean, r_rstd = Res(), Res(); r_sz = [Res() for _ in range(8)]
        r_h = [Res() for _ in range(8)]; r_sttok = [Res()] * 2; r_stT = Res(); r_rowt = r_sttok[0]; r_rr = Res()
        ch_w = [S.chan("aw0"), S.chan("aw1")]; ch_p = S.chan("ap"); ch_st = [S.chan("ast0")] * 2
        ch_o = S.chan("aout")
        cnt = {"w": 0, "bank": 0, "th": 0, "sq": 0, "dg": 0}

        def nxt(key, n):
            v = cnt[key] % n
            cnt[key] += 1
            return v

        for l in range(2):
            dma(gb[:], brow(K.a_norm, l, 1024, 128), [], [r_gb], ch_p)
            dma(prow[0:31, :], K.a_cw[l], [], [r_prow], ch_p)
            dma(prow[31:32, :], K.a_cb[l:l + 1, :], [], [r_prow], ch_p)
            dma(prow[32:33, :], K.a_lg[l:l + 1, :], [], [r_prow], ch_p)
            dma(prow[33:34, :], K.a_lb[l:l + 1, :], [], [r_prow], ch_p)

            S.op("pe", [I("transpose", PS[6][:, cc * 34:(cc + 1) * 34], prow[0:34, cc * 128:(cc + 1) * 128], identf[0:34, 0:34])
                        for cc in range(8)], [r_prow, rC], [rPS[6]])
            S.op("act", I("copy", pcol[:].rearrange("p c k -> p (c k)"), PS[6][:, 0:272]), [rPS[6]], [r_pcol])
            S.op("dve", I("tensor_scalar", lg2[:], pcol[:, :, 32:34], 0.5, None, ALU.mult), [r_pcol], [r_pcol])
            S.op("pool", I("memset", glu[:, :, 0:30], 0.0), [], r_glu)

            for blk in range(5):
                samp = blk == 4
                N = 16 if samp else 512
                Pn = 16 if samp else 128
                if samp:
                    srcs = [(XS[:, :], rXS)]
                else:
                    srcs = [(X[:, 4 * blk + t, :], rX[4 * blk + t]) for t in range(4)]
                rstd = K.rmsnorm_rstd(NT, srcs, Pn, len(srcs), None, None, "a")
                for t, (ap, r) in enumerate(srcs):
                    k = t % 2
                    S.op("dve", I("scalar_tensor_tensor",
                        xn[k][:Pn, :], ap, rstd[:Pn, t:t + 1], gb[:Pn, :], ALU.mult, ALU.mult),
                        [r, NT["rrstd"], r_gb], [r_xn[k]])
                    if samp:
                        S.op("pe", [I("transpose", PSB[:, c * 16:(c + 1) * 16], xn[k][:16, c * 128:(c + 1) * 128], identb[0:16, 0:16])
                                    for c in range(8)], [r_xn[k], rC], [rPSB])
                        S.op("act", I("copy", xnT[:, :, 0:16], PSB[:, 0:128].rearrange("p (c n) -> p c n", c=8)),
                             [rPSB], [r_xnT])
                    else:
                        S.op("pe", [I("transpose", PSB[:, c * 128:(c + 1) * 128], xn[k][:, c * 128:(c + 1) * 128], identb[:, :])
                                    for c in range(8)], [r_xn[k], rC], [rPSB])
                        S.op("act", I("copy", xnT[:, :, t * 128:(t + 1) * 128],
                                                          PSB[:, :].rearrange("p (c n) -> p c n", c=8)), [rPSB], [r_xnT])
                if samp:
                    for t in range(4):
                        k = t % 2
                        dma(sttok[k][:, :], K.st_d[l, 120 * t:120 * (t + 1), :], [], [r_sttok[k]], ch_st[k])
                        for half in range(2):
                            S.op("pe", [I("transpose", PS[6][:, j * 120:(j + 1) * 120],
                                          sttok[k][0:120, (4 * half + j) * 128:(4 * half + j + 1) * 128], identf[0:120, 0:120])
                                        for j in range(4)], [r_sttok[k], rC], [rPS[6]])
                            S.op("act", I("copy",
                                stT[:, 4 * half:4 * half + 4, 120 * t:120 * (t + 1)],
                                PS[6][:, 0:480].rearrange("p (c n) -> p c n", c=4)), [rPS[6]], [r_stT])
                pend = []
                pend_diag = []
                stats_q = []
                fin_add = []

                def flush():
                    while len(stats_q) > (0 if flush.final else 1):
                        cc, kq = stats_q.pop(0)
                        S.op("pe", I("matmul", PS[4][:, 0:N], onesdiv[:, :], cv[:, cc, 0:N], start=(cc == 0), stop=(cc == 7)),
                             [r_cv[cc], rC], [rPS[4]])
                        S.op("pe", I("matmul", PS[5][:, 0:N], onesdiv[:, :], sq[kq][:, 0:N], start=(cc == 0), stop=(cc == 7)),
                             [r_sq[kq], rC], [rPS[5]])
                    for (cc, dg, cb_) in pend_diag:
                        S.op("pe", [I("matmul", PS[cb_][:, 0:N], diag[dg][:, kk, :], gluB[dg][:, kk:kk + N],
                                      start=(kk == 0), stop=(kk == NPE - 1)) for kk in range(NPE)],
                             [r_diag[dg], r_gluB[dg]], [rPS[cb_]])
                    pend_diag.clear()
                    for (cc, cb_) in fin_add:
                        S.op("dve", I("tensor_tensor", cv[:, cc, 0:N], cv[:, cc, 0:N], PS[cb_][:, 0:N], ALU.add),
                             [r_cv[cc], rPS[cb_]], [r_cv[cc]])
                        kq = nxt("sq", 2)
                        S.op("act", I("activation", sq[kq][:, 0:N], cv[:, cc, 0:N], AF.Square), [r_cv[cc]], [r_sq[kq]])
                        pend.append((cc, kq))
                    fin_add.clear()
                    for item in pend:
                        stats_q.append(item)
                    pend.clear()
                flush.final = False

                for u in range(6):
                    ws = nxt("w", 2)
                    dma(wbuf[ws][:], K.s_awin[l, u].rearrange("p (c n) -> p c n", c=8), [], [r_wb[ws]], ch_w[ws])

                    def mmf(fc, bank):
                        mm(PS[bank][:, 0:N], [(wbuf[ws][:, c, fc * 128:(fc + 1) * 128], xnT[:, c, 0:N]) for c in range(8)],
                           [r_wb[ws], r_xnT], [rPS[bank]])
                    if u < 4:
                        for j in range(2):
                            cc = 2 * u + j
                            ba = nxt("bank", 3); bg = nxt("bank", 3)
                            mmf(j, ba); mmf(2 + j, bg)
                            flush()
                            kt = nxt("th", 2)
                            S.op("act", I("activation", th[kt][:, 0:N], PS[bg][:, 0:N], AF.Tanh),
                                 [rPS[bg]], [r_th[kt]])
                            S.op("dve", I("scalar_tensor_tensor",
                                glu[:, cc, 30:30 + N], th[kt][:, 0:N], 1.0, PS[ba][:, 0:N], ALU.add, ALU.mult),
                                [r_th[kt], rPS[ba]], [r_glu[cc]])
                            if not samp:
                                dg = nxt("dg", 2)
                                S.op("pool", I("tensor_copy", gluB[dg][:, :], glu[:, cc, :]), [r_glu[cc]], [r_gluB[dg]])
                                for kk in range(NPE):
                                    S.op("act", I("mul", diag[dg][:, kk, :], identb[:, :], pcol[:, cc, kk:kk + 1]),
                                         [rC, r_pcol], [r_diag[dg]])
                                cb_ = (3, 6)[dg]
                                pend_diag.append((cc, dg, cb_))
                                S.op("dve", I("tensor_scalar",
                                    cv[:, cc, 0:N], glu[:, cc, NPE:NPE + N], pcol[:, cc, NPE:NPE + 1], pcol[:, cc, 31:32], ALU.mult, ALU.add),
                                    [r_glu[cc], r_pcol], [r_cv[cc]])
                                for kk in range(NPE + 1, 31):
                                    S.op("dve", I("scalar_tensor_tensor",
                                        cv[:, cc, 0:N], glu[:, cc, kk:kk + N], pcol[:, cc, kk:kk + 1], cv[:, cc, 0:N],
                                        ALU.mult, ALU.add), [r_glu[cc], r_pcol, r_cv[cc]], [r_cv[cc]])
                                fin_add.append((cc, cb_))
                            else:
                                kq0 = nxt("sq", 2)
                                S.op("dve", I("tensor_tensor",
                                    sq[kq0][:, 0:480].rearrange("p (s k) -> p s k", s=16),
                                    stT[:, cc, :].rearrange("p (s k) -> p s k", s=16),
                                    pcol[:, cc, 0:30].unsqueeze(1).to_broadcast([128, 16, 30]), ALU.mult),
                                    [r_stT, r_pcol], [r_sq[kq0]])
                                S.op("dve", I("tensor_reduce",
                                    rr[:, 0:16], sq[kq0][:, 0:480].rearrange("p (s k) -> p s k", s=16), AX.X, ALU.add),
                                    [r_sq[kq0]], [r_rr])
                                S.op("dve", I("scalar_tensor_tensor",
                                    cv[:, cc, 0:16], glu[:, cc, 30:46], pcol[:, cc, 30:31], rr[:, 0:16], ALU.mult, ALU.add),
                                    [r_glu[cc], r_pcol, r_rr], [r_cv[cc]])
                                S.op("dve", I("tensor_scalar",
                                    cv[:, cc, 0:16], cv[:, cc, 0:16], pcol[:, cc, 31:32], None, ALU.add),
                                    [r_cv[cc], r_pcol], [r_cv[cc]])
                            if samp:
                                kq = nxt("sq", 2)
                                S.op("act", I("activation", sq[kq][:, 0:N], cv[:, cc, 0:N], AF.Square),
                                     [r_cv[cc]], [r_sq[kq]])
                                S.op("pe", I("matmul", PS[4][:, 0:N], onesdiv[:, :], cv[:, cc, 0:N], start=(cc == 0), stop=(cc == 7)),
                                     [r_cv[cc], rC], [rPS[4]])
                                S.op("pe", I("matmul", PS[5][:, 0:N], onesdiv[:, :], sq[kq][:, 0:N], start=(cc == 0), stop=(cc == 7)),
                                     [r_sq[kq], rC], [rPS[5]])
                    else:
                        for fc in range(4):
                            cc = 4 * (u - 4) + fc
                            bz = nxt("bank", 3)
                            mmf(fc, bz)
                            flush()
                            kt = nxt("th", 2)
                            S.op("act", I("activation", th[kt][:, 0:N], PS[bz][:, 0:N], AF.Tanh),
                                 [rPS[bz]], [r_th[kt]])
                            S.op("dve", I("scalar_tensor_tensor",
                                sz[:, cc, 0:N], th[kt][:, 0:N], 1.0, PS[bz][:, 0:N], ALU.add, ALU.mult),
                                [r_th[kt], rPS[bz]], [r_sz[cc]])
                flush(); flush.final = True; flush(); flush()
                kt = nxt("th", 2)
                S.op("act", I("activation", th[kt][:, 0:N], PS[4][:, 0:N], AF.Square), [rPS[4]], [r_th[kt]])
                S.op("dve", I("tensor_tensor", rstd_sb[:, 0:N], PS[5][:, 0:N], th[kt][:, 0:N], ALU.subtract),
                     [rPS[5], r_th[kt]], [r_rstd])
                S.op("act", I("activation", rstd_sb[:, 0:N], rstd_sb[:, 0:N], AF.Sqrt, bias=K.epsc[:, 0:1]),
                     [r_rstd, rC], [r_rstd])
                S.op("dve", I("reciprocal", rstd_sb[:, 0:N], rstd_sb[:, 0:N]), [r_rstd], [r_rstd])
                S.op("act", I("copy", mean_sb[:, 0:N], PS[4][:, 0:N]), [rPS[4]], [r_mean])
                for cc in range(8):
                    c_ = cv[:, cc, 0:N]
                    S.op("dve", I("tensor_tensor", c_, c_, mean_sb[:, 0:N], ALU.subtract), [r_cv[cc], r_mean], [r_cv[cc]])
                    S.op("dve", I("tensor_tensor", c_, c_, rstd_sb[:, 0:N], ALU.mult), [r_cv[cc], r_rstd], [r_cv[cc]])
                    S.op("dve", I("tensor_scalar", c_, c_, lg2[:, cc, 0:1], lg2[:, cc, 1:2], ALU.mult, ALU.add),
                         [r_cv[cc], r_pcol], [r_cv[cc]])
                    kt = nxt("th", 2)
                    S.op("act", I("activation", th[kt][:, 0:N], c_, AF.Tanh), [r_cv[cc]], [r_th[kt]])
                    S.op("dve", I("scalar_tensor_tensor", c_, th[kt][:, 0:N], 1.0, c_, ALU.add, ALU.mult),
                         [r_cv[cc], r_th[kt]], [r_cv[cc]])
                    S.op("dve", I("tensor_tensor", h[:, cc, 0:N], c_, sz[:, cc, 0:N], ALU.mult),
                         [r_cv[cc], r_sz[cc]], [r_h[cc]])
                for u in range(2):
                    ws = nxt("w", 2)
                    dma(wbuf[ws][:], K.s_awout[l, u].rearrange("p (c n) -> p c n", c=8), [], [r_wb[ws]], ch_w[ws])
                    for t in range(1 if samp else 4):
                        bk = nxt("bank", 3)
                        if samp:
                            mm(PS[bk][0:16, :], [(h[:, c, 0:16], wbuf[ws][:, c, :]) for c in range(8)], r_h + [r_wb[ws]], [rPS[bk]])
                            xs_ = XS[:, u * 512:(u + 1) * 512]
                            S.op("dve", I("tensor_tensor", xs_, xs_, PS[bk][0:16, :], ALU.add),
                                 [rXS, rPS[bk]], [rXS])
                        else:
                            i = 4 * blk + t
                            mm(PS[bk][:, :], [(h[:, c, t * 128:(t + 1) * 128], wbuf[ws][:, c, :]) for c in range(8)],
                               r_h + [r_wb[ws]], [rPS[bk]])
                            x_ = X[:, i, u * 512:(u + 1) * 512]
                            S.op("dve", I("tensor_tensor", x_, x_, PS[bk][:, :], ALU.add),
                                 [rX[i], rPS[bk]], [rX[i]])
                if blk < 3:
                    S.op("pool", I("tensor_copy", glu[:, :, 0:30], glu[:, :, 512:542]), r_glu, r_glu)
                elif blk == 3 or samp:
                    nr = 16 if samp else 30
                    c0 = 30 if samp else 512
                    for half in range(2):
                        S.op("pe", [I("transpose", PS[6][0:nr, j * 128:(j + 1) * 128], glu[:, 4 * half + j, c0:c0 + nr], identf[:, :])
                                    for j in range(4)], r_glu + [rC], [rPS[6]])
                        S.op("act", I("copy", rowt[0:nr, half * 512:(half + 1) * 512], PS[6][0:nr, :]),
                             [rPS[6]], [r_rowt])
                    if samp:
                        cs = K.convs_d[l].rearrange("(s k) d -> s k d", k=30)
                        dma(cs[:, 29, :], rowt[0:16, :], [r_rowt], [], ch_o)
                        dma(cs[:, 0:29, :], K.st_d[l].rearrange("(s k) d -> s k d", k=30)[:, 1:30, :], [], [], ch_o)
                    else:
                        dma(K.convp_d[l], rowt[0:30, :], [r_rowt], [], ch_o)
        S.barrier()


def make_in_maps(inputs, cores, NP=2560, attn=True, pt_override=None):
    consts = make_consts()
    f = lambda a: np.ascontiguousarray(a, dtype=np.float32)
    maps = []
    for c in cores:
        m = {
            "x": f(inputs["x_prompt"][c]),
            "xs": f(inputs["x_sample"][16 * c:16 * c + 16, 0]),
            "state": f(inputs["state_conv"][:, 16 * c:16 * c + 16]).reshape(2, 480, 1024),
            "a_norm": f(inputs["a_norm"]), "a_w_in": f(inputs["a_w_in"]), "a_cw": f(inputs["a_conv_w"]),
            "a_cb": f(inputs["a_conv_b"]), "a_lg": f(inputs["a_ln_g"]), "a_lb": f(inputs["a_ln_b"]),
            "a_w_out": f(inputs["a_w_out"]), "kv_norm": f(inputs["kv_norm"]).reshape(1, 1024), "kv_w": f(inputs["kv_w"]),
            "kv_fb": f(inputs["kv_fb"]).reshape(1, 16), "k_norm": f(inputs["k_norm"]).reshape(1, 64),
            "b_norm": f(inputs["b_norm"]), "b_w_in": f(inputs["b_w_in"]), "q_norm": f(inputs["q_norm"]),
            "b_w_out": f(inputs["b_w_out"]),
        }
        if attn:
            m["ck"] = inputs["cache_k"].reshape(NP * 128, 1024)
            m["cv"] = inputs["cache_v"].reshape(NP * 128, 1024)
            m["cl"] = inputs["cache_logf"].reshape(NP * 128, 16)
            pt = inputs["page_table"] if pt_override is None else pt_override
            m["pt"] = np.ascontiguousarray(pt[16 * c:16 * c + 16]).reshape(1, 256).astype(np.int32)
        for n, arr in consts.items():
            m["c_" + n] = arr
        maps.append(m)
    return maps


def assemble(results):
    cat = lambda k: np.stack([r[k] for r in results], 0)
    n = len(results)
    y = cat("y").reshape(n, 2048, 1024)
    ys = cat("ys").reshape(n * 16, 1, 1024)
    convp = np.transpose(cat("convp"), (1, 0, 2, 3))
    convs = np.transpose(cat("convs").reshape(n, 2, 16, 30, 1024), (1, 0, 2, 3, 4)).reshape(2, n * 16, 30, 1024)
    kp = cat("kp").reshape(n, 2048, 16, 64); vp = cat("vp").reshape(n, 2048, 16, 64); lfp = cat("lfp").reshape(n, 2048, 16)
    ks = cat("ks").reshape(n * 16, 1, 16, 64); vs = cat("vs").reshape(n * 16, 1, 16, 64); lfs = cat("lfs").reshape(n * 16, 1, 16)
    return tuple(np.ascontiguousarray(a, dtype=np.float32) for a in (y, ys, convp, convs, kp, vp, lfp, ks, vs, lfs))


def kernel(**inputs):
    NP = inputs["cache_k"].shape[0]
    nc = build(NP=NP)
    maps = make_in_maps(inputs, list(range(NCORES)), NP=NP)
    res = run_bass_kernel_spmd(nc, maps, core_ids=list(range(NCORES)))
    return assemble(res.results)


def phase_kv(K):
    nc, S, X, XS, C, PS, PSB = K.nc, K.S, K.X, K.XS, K.C, K.PS, K.PSB
    rX, rXS, rPS, rPSB, rC = K.rX, K.rXS, K.rPS, K.rPSB, K.rC
    dma, mm = K.dma, K.mm
    identb = C["identb"]
    PB = K.G

    def sbp(name, shape, dt):
        return PB.enter_context(nc.sbuf_tensor("pb_" + name, shape, dt))
    KT = sbp("KT", [128, 8, 2048], BF16); VA = sbp("VA", [128, 16, 1040], BF16)
    biasT = sbp("biasT", [128, 8, 16, 16], F32)
    ksn = sbp("ksn", [16, 1024], F32); vss = sbp("vss", [16, 1024], F32); lfs = sbp("lfs", [16, 16], F32)
    Cc = sbp("Cc", [128, 16, 16], F32); Rsb = sbp("Rsb", [128, 8, 16], F32)
    K.Cc, K.Rsb = Cc, Rsb
    K.KT, K.VA, K.biasT, K.ksn, K.vss, K.lfs = KT, VA, biasT, ksn, vss, lfs
    K.r_KT, K.r_VA, K.r_bias, K.r_ksn, K.r_vss, K.r_lfs = Res(), Res(), Res(), Res(), Res(), Res()
    with ExitStack() as P:
        def sb(name, shape, dt):
            return P.enter_context(nc.sbuf_tensor("kv_" + name, shape, dt))
        wkv = sb("wkv", [128, 8, 2048], BF16); wkf = sb("wkf", [128, 8, 16], BF16)
        gb = sb("gb", [128, 1024], F32); kg = sb("kg", [128, 64], F32); fbb = sb("fbb", [128, 16], F32)
        xn = sb("xn", [128, 1024], BF16); xnT = sb("xnT", [128, 8, 128], BF16)
        ofull = sb("ofull", [128, 1024], F32); sqt = sb("sqt", [128, 512], F32); kb16 = sb("kb16", [128, 1024], BF16)
        ss8 = sb("ss8", [128, 8], F32); lt = [sb("lt%d" % i, [128, 16], F32) for i in range(4)]
        lfa = sb("lfa", [128, 16, 16], F32); carry = sb("carry", [128, 16, 16], F32)
        NT = K.norm_tmp(P, "k")
        r_w, r_gb, r_xn, r_xnT, r_of, r_sq, r_kb, r_ss8, r_lt, r_lfa, r_carry, r_Cc, r_R = [Res() for _ in range(13)]
        ch_w = S.chan("kvw"); ch_o = S.chan("kvo")
        for u in range(4):
            dma(wkv[:, :, u * 512:(u + 1) * 512], K.s_kvw[u].rearrange("p (c n) -> p c n", c=8), [], [r_w], ch_w)
        dma(wkf[:], K.s_kvf[:, :].rearrange("p (c n) -> p c n", c=8), [], [r_w], ch_w)
        dma(gb[:], brow(K.kv_norm, 0, 1024, 128), [], [r_gb], ch_w)
        dma(kg[:], brow(K.k_norm, 0, 64, 128), [], [r_gb], ch_w)
        dma(fbb[:], brow(K.kv_fb, 0, 16, 128), [], [r_gb], ch_w)
        S.op("pool", I("memset", VA[:].rearrange("p i (h e) -> p (i h) e", e=65)[:, :, 64:65], 1.0), [], [K.r_VA])
        bank = [0]

        def nb():
            bank[0] = (bank[0] + 1) % 4
            return bank[0]
        kpv = K.kp_d.rearrange("(i p) d -> p i d", p=128); vpv = K.vp_d.rearrange("(i p) d -> p i d", p=128)
        lfv = K.lfp_d.rearrange("(i p) d -> p i d", p=128)
        for i in range(17):
            samp = i == 16
            Pn = 16 if samp else 128
            src, rs = (XS[:, :], rXS) if samp else (X[:, i, :], rX[i])
            rstd = K.rmsnorm_rstd(NT, [(src, rs)], Pn, 1, None, None, "k")
            S.op("dve", I("scalar_tensor_tensor", xn[:Pn, :], src, rstd[:Pn, 0:1], gb[:Pn, :], ALU.mult, ALU.mult),
                 [rs, NT["rrstd"], r_gb], [r_xn])
            S.op("pe", [I("transpose", PSB[:, c * Pn:(c + 1) * Pn], xn[:Pn, c * 128:(c + 1) * 128], identb[0:Pn, 0:Pn])
                        for c in range(8)], [r_xn, rC], [rPSB])
            S.op("act", I("copy", xnT[:, :, 0:Pn], PSB[:, 0:8 * Pn].rearrange("p (c n) -> p c n", c=8)), [rPSB], [r_xnT])
            for u in range(4):
                bk = nb()
                mm(PS[bk][:Pn, :], [(xnT[:, c, 0:Pn], wkv[:, c, u * 512:(u + 1) * 512]) for c in range(8)], [r_xnT, r_w], [rPS[bk]])
                osl = ofull[:Pn, (u % 2) * 512:(u % 2 + 1) * 512]
                if u < 2:
                    S.op("act", I("activation", sqt[:Pn, :], PS[bk][:Pn, :], AF.Square), [rPS[bk]], [r_sq])
                    S.op("dve", I("tensor_reduce", ss8[:Pn, :], sqt[:Pn, :].rearrange("p (h d) -> p h d", d=64), AX.X, ALU.add),
                         [r_sq], [r_ss8])
                    S.op("dve", I("tensor_scalar", ss8[:Pn, :], ss8[:Pn, :], 1.0 / 64.0, EPS, ALU.mult, ALU.add), [r_ss8], [r_ss8])
                    S.op("act", I("activation", ss8[:Pn, :], ss8[:Pn, :], AF.Sqrt), [r_ss8], [r_ss8])
                    S.op("dve", I("reciprocal", ss8[:Pn, :], ss8[:Pn, :]), [r_ss8], [r_ss8])
                    o3 = osl.rearrange("p (h d) -> p h d", d=64)
                    S.op("dve", I("tensor_tensor", o3, PS[bk][:Pn, :].rearrange("p (h d) -> p h d", d=64),
                                  ss8[:Pn, :].unsqueeze(2).to_broadcast([Pn, 8, 64]), ALU.mult), [rPS[bk], r_ss8], [r_of])
                    S.op("dve", I("tensor_tensor", o3, o3, kg[:Pn, :].unsqueeze(1).to_broadcast([Pn, 8, 64]), ALU.mult),
                         [r_of, r_gb], [r_of])
                    if u == 1:
                        if samp:
                            dma(K.ks_d, ofull[:16, :], [r_of], [], ch_o)
                            S.op("act", I("copy", ksn[:, :], ofull[:16, :]), [r_of], [K.r_ksn])
                        else:
                            dma(kpv[:, i, :], ofull[:, :], [r_of], [], ch_o)
                            S.op("pool", I("tensor_copy", kb16[:, :], ofull[:, :]), [r_of], [r_kb])
                            S.op("pe", [I("transpose", PSB[:, c * 128:(c + 1) * 128], kb16[:, c * 128:(c + 1) * 128], identb[:, :])
                                        for c in range(8)], [r_kb, rC], [rPSB])
                            S.op("act", I("copy", KT[:, :, i * 128:(i + 1) * 128], PSB[:, :].rearrange("p (c n) -> p c n", c=8)),
                                 [rPSB], [K.r_KT])
                else:
                    S.op("act", I("copy", osl, PS[bk][:Pn, :]), [rPS[bk]], [r_of])
                    if not samp:
                        va = VA[:, i, :].rearrange("p (h e) -> p h e", e=65)[:, 8 * (u - 2):8 * (u - 2) + 8, 0:64]
                        S.op("pool", I("tensor_copy", va, osl.rearrange("p (h d) -> p h d", d=64)), [r_of], [K.r_VA])
                    if u == 3:
                        if samp:
                            dma(K.vs_d, ofull[:16, :], [r_of], [], ch_o)
                            S.op("act", I("copy", vss[:, :], ofull[:16, :]), [r_of], [K.r_vss])
                        else:
                            dma(vpv[:, i, :], ofull[:, :], [r_of], [], ch_o)
            bk = nb()
            mm(PS[bk][:Pn, 0:16], [(xnT[:, c, 0:Pn], wkf[:, c, :]) for c in range(8)], [r_xnT, r_w], [rPS[bk]])
            u_, a_, e_, m_ = [t[:Pn, :] for t in lt]
            S.op("dve", I("tensor_tensor", u_, PS[bk][:Pn, 0:16], fbb[:Pn, :], ALU.add), [rPS[bk], r_gb], [r_lt])
            S.op("act", I("activation", a_, u_, AF.Abs), [r_lt], [r_lt])
            S.op("act", I("activation", e_, a_, AF.Exp, scale=-1.0), [r_lt], [r_lt])
            S.op("act", I("activation", e_, e_, AF.Ln, bias=K.onec[:Pn, 0:1]), [r_lt, rC], [r_lt])
            S.op("dve", I("tensor_scalar_min", m_, u_, 0.0), [r_lt], [r_lt])
            if samp:
                S.op("dve", I("tensor_tensor", lfs[:, :], m_, e_, ALU.subtract), [r_lt], [K.r_lfs])
                dma(K.lfs_d, lfs[:, :], [K.r_lfs], [], ch_o)
            else:
                S.op("dve", I("tensor_tensor", lfa[:, i, :], m_, e_, ALU.subtract), [r_lt], [r_lfa])
        dma(lfv, lfa[:, :, :], [r_lfa], [], ch_o)
        lff = lfa[:].rearrange("p i h -> p (i h)")
        mm(PS[5][:, 0:256], [(C["lincl"][:, :], lff)], [r_lfa, rC], [rPS[5]])
        mm(PS[4][:, 0:256], [(C["ones1"][:, :], lff)], [r_lfa, rC], [rPS[4]])
        S.op("dve", I("memset", carry[:, 0, :], 0.0), [], [r_carry])
        for i in range(1, 16):
            S.op("dve", I("tensor_tensor", carry[:, i, :], carry[:, i - 1, :], PS[4][:, (i - 1) * 16:i * 16], ALU.add),
                 [r_carry, rPS[4]], [r_carry])
        S.op("dve", I("tensor_tensor", Cc[:].rearrange("p i h -> p (i h)"), PS[5][:, 0:256], carry[:].rearrange("p i h -> p (i h)"),
                      ALU.add), [rPS[5], r_carry], [r_Cc])
        S.op("pe", [I("matmul", PS[6][:, m * 16:(m + 1) * 16], C["rs0"][:, :], Cc[:, 2 * m + 1, :], start=True, stop=True)
                    for m in range(8)], [r_Cc, rC], [rPS[6]])
        S.op("act", I("copy", Rsb[:].rearrange("p b h -> p (b h)"), PS[6][:, 0:128]), [rPS[6]], [r_R])
        for m in range(8):
            nj = 2 * m + 2
            S.op("dve", I("tensor_tensor", biasT[:, m, 0:nj, :], Rsb[:, m, :].unsqueeze(1).to_broadcast([128, nj, 16]),
                          Cc[:, 0:nj, :], ALU.subtract), [r_R, r_Cc], [K.r_bias])
        S.barrier()


def phase_b(K):
    nc, S, X, XS, C, PS, PSB = K.nc, K.S, K.X, K.XS, K.C, K.PS, K.PSB
    rX, rXS, rPS, rPSB, rC = K.rX, K.rXS, K.rPS, K.rPSB, K.rC
    dma, mm = K.dma, K.mm
    identb, tri = C["identb"], C["tri"]
    KT, VA, biasT = K.KT, K.VA, K.biasT
    for l in range(2):
        with ExitStack() as P:
            def sb(name, shape, dt):
                return P.enter_context(nc.sbuf_tensor("b%d_" % l + name, shape, dt))
            wb = [sb("wb%d" % i, [128, 8, 256], BF16) for i in range(2)]
            gb = sb("gb", [128, 1024], F32); qg = sb("qg", [128, 64], F32)
            xn = sb("xn", [128, 1024], BF16); xnT = sb("xnT", [128, 8, 512], BF16); QT = sb("QT", [128, 8, 512], BF16)
            qn = sb("qn", [128, 256], BF16); szb = sb("szb", [128, 4, 1024], BF16); g = sb("g", [128, 4, 1024], BF16)
            pt = [sb("pt%d" % i, [128, 256], BF16) for i in range(4)]
            th = sb("th", [128, 256], F32); sqt = sb("sqt", [128, 256], F32); ss4 = sb("ss4", [128, 4], F32)
            rec = sb("rec", [128, 4], F32)
            NT = K.norm_tmp(P, "b%d" % l)
            r_wb = [Res(), Res()]; r_gb, r_xn, r_xnT, r_QT, r_qn, r_g, r_th, r_sq, r_ss4, r_rec = [Res() for _ in range(10)]
            r_sz = [Res() for _ in range(4)]; r_pt = [Res() for _ in range(4)]
            ch_w = [S.chan("bw%d_%d" % (l, i)) for i in range(2)]; ch_p = S.chan("bp%d" % l)
            cnt = {"w": 0, "s": 0, "p": 0, "pt": 0}

            def nxt(key, n):
                v = cnt[key] % n
                cnt[key] += 1
                return v
            dma(gb[:], brow(K.b_norm, l, 1024, 128), [], [r_gb], ch_p)
            dma(qg[:], brow(K.q_norm, l, 64, 128), [], [r_gb], ch_p)
            for b in range(4):
                srcs = [(X[:, 4 * b + t, :], rX[4 * b + t]) for t in range(4)]
                rstd = K.rmsnorm_rstd(NT, srcs, 128, 4, None, None, "b")
                for t, (ap, r) in enumerate(srcs):
                    S.op("dve", I("scalar_tensor_tensor", xn[:, :], ap, rstd[:, t:t + 1], gb[:, :], ALU.mult, ALU.mult),
                         [r, NT["rrstd"], r_gb], [r_xn])
                    S.op("pe", [I("transpose", PSB[:, c * 128:(c + 1) * 128], xn[:, c * 128:(c + 1) * 128], identb[:, :])
                                for c in range(8)], [r_xn, rC], [rPSB])
                    S.op("act", I("copy", xnT[:, :, t * 128:(t + 1) * 128], PSB[:, :].rearrange("p (c n) -> p c n", c=8)),
                         [rPSB], [r_xnT])
                for v in range(8):
                    ws = nxt("w", 2)
                    dma(wb[ws][:], K.s_bwin[l, v // 2].rearrange("p (c n) -> p c n", c=8)[:, :, (v % 2) * 256:(v % 2 + 1) * 256],
                        [], [r_wb[ws]], ch_w[ws])
                    for t in range(4):
                        bk = 5 + nxt("p", 2)
                        mm(PS[bk][:, 0:256], [(xnT[:, c, t * 128:(t + 1) * 128], wb[ws][:, c, :]) for c in range(8)],
                           [r_xnT, r_wb[ws]], [rPS[bk]])
                        ps = PS[bk][:, 0:256]
                        if v < 4:
                            S.op("act", I("activation", sqt[:, :], ps, AF.Square), [rPS[bk]], [r_sq])
                            S.op("dve", I("tensor_reduce", ss4[:, :], sqt[:, :].rearrange("p (h d) -> p h d", d=64), AX.X, ALU.add),
                                 [r_sq], [r_ss4])
                            S.op("dve", I("tensor_scalar", ss4[:, :], ss4[:, :], 1.0 / 64.0, EPS, ALU.mult, ALU.add), [r_ss4], [r_ss4])
                            S.op("act", I("activation", ss4[:, :], ss4[:, :], AF.Sqrt), [r_ss4], [r_ss4])
                            S.op("dve", I("reciprocal", ss4[:, :], ss4[:, :]), [r_ss4], [r_ss4])
                            S.op("dve", I("tensor_tensor", th[:, :].rearrange("p (h d) -> p h d", d=64),
                                          ps.rearrange("p (h d) -> p h d", d=64),
                                          ss4[:, :].unsqueeze(2).to_broadcast([128, 4, 64]), ALU.mult), [rPS[bk], r_ss4], [r_th])
                            S.op("dve", I("tensor_tensor", qn[:, :].rearrange("p (h d) -> p h d", d=64),
                                          th[:, :].rearrange("p (h d) -> p h d", d=64),
                                          qg[:, :].unsqueeze(1).to_broadcast([128, 4, 64]), ALU.mult), [r_th, r_gb], [r_qn])
                            S.op("pe", [I("transpose", PSB[:, j * 128:(j + 1) * 128], qn[:, j * 128:(j + 1) * 128], identb[:, :])
                                        for j in range(2)], [r_qn, rC], [rPSB])
                            S.op("act", I("copy", QT[:, 2 * v:2 * v + 2, t * 128:(t + 1) * 128],
                                          PSB[:, 0:256].rearrange("p (c n) -> p c n", c=2)), [rPSB], [r_QT])
                        else:
                            S.op("act", I("activation", th[:, :], ps, AF.Tanh), [rPS[bk]], [r_th])
                            S.op("dve", I("scalar_tensor_tensor", szb[:, t, (v - 4) * 256:(v - 3) * 256], th[:, :], 1.0, ps,
                                          ALU.add, ALU.mult), [r_th, rPS[bk]], [r_sz[t]])
                if l == 0 and b == 0:
                    K.dbg("bias", biasT[:, 0, :, :].rearrange("p j h -> p (j h)"), [K.r_bias])
                    K.dbg("qt", QT[:, 0, 0:256], [r_QT])
                    K.dbg("kt", KT[:, 0, 0:256], [K.r_KT])
                    K.dbg("va", VA[:, 0, 0:256], [K.r_VA])
                    K.dbg("szb", szb[:, 0, 0:256], r_sz)
                steps = []
                for h in range(16):
                    for q2 in range(2):
                        m = 2 * b + q2
                        for j in range(2 * m + 2):
                            steps.append(dict(h=h, q2=q2, m=m, j=j, n0=max(0, j - 2 * m) * 128, last=(q2 == 1 and j == 2 * m + 1)))
                LA = 2

                def emit_front(st):
                    h, q2, m, j, n0 = st["h"], st["q2"], st["m"], st["j"], st["n0"]
                    c, po, q0 = h // 2, 64 * (h % 2), q2 * 256
                    bk = nxt("s", 3)
                    mm(PS[bk][:, n0:256], [(KT[po:po + 64, c, j * 128:(j + 1) * 128], QT[po:po + 64, c, q0 + n0:q0 + 256])],
                       [K.r_KT, r_QT], [rPS[bk]])
                    k = nxt("pt", 4)
                    st["k"] = k
                    S.op("act", I("activation", pt[k][:, n0:256], PS[bk][:, n0:256], AF.Exp,
                                  bias=biasT[:, m, j, h:h + 1], scale=0.125), [rPS[bk], K.r_bias], [r_pt[k]])
                    if j >= 2 * m:
                        S.op("pool", I("tensor_tensor", pt[k][:, n0:n0 + 128], pt[k][:, n0:n0 + 128], tri[:, :], ALU.mult),
                             [r_pt[k], rC], [r_pt[k]])

                def emit_back(st):
                    h, q2, m, j, n0, k = st["h"], st["q2"], st["m"], st["j"], st["n0"], st["k"]
                    ob = 3 + h % 2
                    S.op("pe", [I("matmul", PS[ob][:, (2 * q2 + tl) * 65:(2 * q2 + tl + 1) * 65], pt[k][:, tl * 128:(tl + 1) * 128],
                                  VA[:, j, h * 65:(h + 1) * 65], start=(j == 0 and q2 == 0 and tl == 0), stop=(j == 2 * m + tl))
                                for tl in range(n0 // 128, 2)], [r_pt[k], K.r_VA], [rPS[ob]])
                    if st["last"]:
                        o3 = PS[ob][:, 0:260].rearrange("p (t e) -> p t e", e=65)
                        S.op("dve", I("reciprocal", rec[:, :], o3[:, :, 64]), [rPS[ob]], [r_rec])
                        for tq in range(4):
                            S.op("dve", I("scalar_tensor_tensor", g[:, tq, h * 64:(h + 1) * 64], o3[:, tq, 0:64], rec[:, tq:tq + 1],
                                          szb[:, tq, h * 64:(h + 1) * 64], ALU.mult, ALU.mult), [rPS[ob], r_rec, r_sz[tq]], [r_g])

                for i in range(len(steps) + LA):
                    if i < len(steps):
                        emit_front(steps[i])
                    if i - LA >= 0:
                        emit_back(steps[i - LA])
                if l == 0 and b == 0:
                    K.dbg("g", g[:, 0, 0:256], [r_g])
                for t in range(4):
                    S.op("pe", [I("transpose", PSB[:, c * 128:(c + 1) * 128], g[:, t, c * 128:(c + 1) * 128], identb[:, :])
                                for c in range(8)], [r_g, rC], [rPSB])
                    S.op("act", I("copy", xnT[:, :, t * 128:(t + 1) * 128], PSB[:, :].rearrange("p (c n) -> p c n", c=8)),
                         [rPSB], [r_xnT])
                for v in range(4):
                    ws = nxt("w", 2)
                    dma(wb[ws][:], K.s_bwout[l, v // 2].rearrange("p (c n) -> p c n", c=8)[:, :, (v % 2) * 256:(v % 2 + 1) * 256],
                        [], [r_wb[ws]], ch_w[ws])
                    for t in range(4):
                        i = 4 * b + t
                        bk = 5 + nxt("p", 2)
                        mm(PS[bk][:, 0:256], [(xnT[:, c, t * 128:(t + 1) * 128], wb[ws][:, c, :]) for c in range(8)],
                           [r_xnT, r_wb[ws]], [rPS[bk]])
                        x_ = X[:, i, v * 256:(v + 1) * 256]
                        S.op("dve", I("tensor_tensor", x_, x_, PS[bk][:, 0:256], ALU.add), [rX[i], rPS[bk]], [rX[i]])
            S.barrier()
        if K.attn:
            phase_b_sample(K, l)


def phase_b_sample(K, l):
    nc, S, XS, C, PS, PSB = K.nc, K.S, K.XS, K.C, K.PS, K.PSB
    rXS, rPS, rPSB, rC = K.rXS, K.rPS, K.rPSB, K.rC
    dma, mm = K.dma, K.mm
    identb, identf, ones1, ustr = C["identb"], C["identf"], C["ones1"], C["ustr"]
    ksn, vss, lfs = K.ksn, K.vss, K.lfs
    cnt = {"w": 0, "p": 0, "k": 0, "v": 0}

    def nxt(key, n):
        v = cnt[key] % n
        cnt[key] += 1
        return v
    with ExitStack() as P:
        def sb(name, shape, dt, es=P):
            return es.enter_context(nc.sbuf_tensor("s%d_" % l + name, shape, dt))
        ec = sb("ec", [16, 256], F32); bmask = sb("bmask", [16, 1024], F32)
        xnT = sb("xnT", [128, 8, 16], BF16)
        qs = sb("qs", [16, 1024], F32); szs = sb("szs", [16, 1024], F32)
        lfsb = sb("lfsb", [128, 256], F32); dacc = sb("dacc", [16, 16], F32); pself = sb("pself", [16, 16], F32)
        masked = sb("masked", [16, 1024], F32); idx = sb("idx", [128, 256], I32)
        r_ec, r_xnT, r_qs, r_szs, r_lfsb, r_dacc, r_pself, r_masked, r_idx = [Res() for _ in range(9)]
        ch_c = S.chan("sc%d" % l); ch_w = [S.chan("sw%d_%d" % (l, i)) for i in range(2)]
        ch_k = [S.chan("sk%d_%d" % (l, i)) for i in range(4)]; ch_v = [S.chan("sv%d_%d" % (l, i)) for i in range(4)]
        ch_l = S.chan("sl%d" % l)
        dma(ec[:], K.cd["ec"], [], [r_ec], ch_c); dma(bmask[:], K.cd["bmask"], [], [r_ec], ch_c)
        with ExitStack() as T:
            wb = [sb("wb%d" % i, [128, 8, 256], BF16, T) for i in range(2)]
            gb = sb("gb", [16, 1024], F32, T); qg = sb("qg", [16, 64], F32, T)
            xn = sb("xn", [16, 1024], BF16, T)
            ptb = sb("ptb", [128, 256], I32, T); ptf = sb("ptf", [128, 256], F32, T)
            th = sb("th", [16, 256], F32, T); sqt = sb("sqt", [16, 256], F32, T); ss4 = sb("ss4", [16, 4], F32, T)
            NT = K.norm_tmp(T, "s%d" % l)
            r_wb = [Res(), Res()]; r_gb, r_qg, r_xn, r_th, r_sq, r_ss4 = [Res() for _ in range(6)]
            dma(gb[:], brow(K.b_norm, l, 1024, 16), [], [r_gb], ch_c); dma(qg[:], brow(K.q_norm, l, 64, 16), [], [r_qg], ch_c)
            dma(ptb[:], bass.AP(K.pt_d.tensor, 0, [[0, 128], [1, 256]]), [], [r_idx], ch_c)
            S.op("dve", I("tensor_copy", ptf[:, :], ptb[:, :]), [r_idx], [r_idx])
            S.op("dve", I("tensor_scalar", ptf[:, :], ptf[:, :], 128.0, C["pidx"][:, 0:1], ALU.mult, ALU.add), [r_idx, rC], [r_idx])
            S.op("dve", I("tensor_copy", idx[:, :], ptf[:, :]), [r_idx], [r_idx])
            S.op("dve", I("memset", dacc[:, :], 0.0), [], [r_dacc])
            rstd = K.rmsnorm_rstd(NT, [(XS[:, :], rXS)], 16, 1, None, None, "s")
            S.op("dve", I("scalar_tensor_tensor", xn[:, :], XS[:, :], rstd[:16, 0:1], gb[:, :], ALU.mult, ALU.mult),
                 [rXS, NT["rrstd"], r_gb], [r_xn])
            S.op("pe", [I("transpose", PSB[:, c * 16:(c + 1) * 16], xn[:, c * 128:(c + 1) * 128], identb[0:16, 0:16]) for c in range(8)],
                 [r_xn, rC], [rPSB])
            S.op("act", I("copy", xnT[:, :, :], PSB[:, 0:128].rearrange("p (c n) -> p c n", c=8)), [rPSB], [r_xnT])
            for v in range(8):
                ws = nxt("w", 2)
                dma(wb[ws][:], K.s_bwin[l, v // 2].rearrange("p (c n) -> p c n", c=8)[:, :, (v % 2) * 256:(v % 2 + 1) * 256],
                    [], [r_wb[ws]], ch_w[ws])
                bk = 5 + nxt("p", 2)
                mm(PS[bk][:16, 0:256], [(xnT[:, c, :], wb[ws][:, c, :]) for c in range(8)], [r_xnT, r_wb[ws]], [rPS[bk]])
                ps = PS[bk][:16, 0:256]
                if v < 4:
                    S.op("act", I("activation", sqt[:, :], ps, AF.Square), [rPS[bk]], [r_sq])
                    S.op("dve", I("tensor_reduce", ss4[:, :], sqt[:, :].rearrange("p (h d) -> p h d", d=64), AX.X, ALU.add), [r_sq], [r_ss4])
                    S.op("dve", I("tensor_scalar", ss4[:, :], ss4[:, :], 1.0 / 64.0, EPS, ALU.mult, ALU.add), [r_ss4], [r_ss4])
                    S.op("act", I("activation", ss4[:, :], ss4[:, :], AF.Sqrt), [r_ss4], [r_ss4])
                    S.op("dve", I("reciprocal", ss4[:, :], ss4[:, :]), [r_ss4], [r_ss4])
                    S.op("dve", I("tensor_tensor", th[:, :].rearrange("p (h d) -> p h d", d=64), ps.rearrange("p (h d) -> p h d", d=64),
                                  ss4[:, :].unsqueeze(2).to_broadcast([16, 4, 64]), ALU.mult), [rPS[bk], r_ss4], [r_th])
                    S.op("dve", I("tensor_tensor", qs[:, v * 256:(v + 1) * 256].rearrange("p (h d) -> p h d", d=64),
                                  th[:, :].rearrange("p (h d) -> p h d", d=64),
                                  qg[:, :].unsqueeze(1).to_broadcast([16, 4, 64]), ALU.mult), [r_th, r_qg], [r_qs])
                else:
                    S.op("act", I("activation", th[:, :], ps, AF.Tanh), [rPS[bk]], [r_th])
                    S.op("dve", I("scalar_tensor_tensor", szs[:, (v - 4) * 256:(v - 3) * 256], th[:, :], 1.0, ps, ALU.add, ALU.mult),
                         [r_th, rPS[bk]], [r_szs])
            S.op("dve", I("tensor_tensor", masked[:, :], qs[:, :], ksn[:, :], ALU.mult), [r_qs, K.r_ksn], [r_masked])
            S.op("dve", I("tensor_reduce", pself[:, :], masked[:, :].rearrange("p (h d) -> p h d", d=64), AX.X, ALU.add),
                 [r_masked], [r_pself])
            S.op("act", I("activation", pself[:, :], pself[:, :], AF.Exp, scale=0.125), [r_pself], [r_pself])
            S.op("pe", [I("matmul", PS[6][:, s * 16:(s + 1) * 16], identf[0:16, s:s + 1].to_broadcast([16, 128]), lfs[:, :],
                          start=True, stop=True) for s in range(16)], [K.r_lfs, rC], [rPS[6]])
            S.op("act", I("copy", lfsb[:, :], PS[6][:, 0:256]), [rPS[6]], [r_lfsb])
            S.barrier()
        with ExitStack() as T:
            qb = sb("qb", [128, 1024], F32, T)
            kb = [sb("kb%d" % i, [128, 1024], F32, T) for i in range(4)]
            vb16 = [sb("vb16_%d" % i, [128, 1024], BF16, T) for i in range(4)]
            sc = sb("sc", [128, 256], F32, T); bias_s = sb("bias_s", [128, 256], F32, T)
            p16 = [sb("p16_%d" % i, [128, 256], BF16, T) for i in range(2)]
            lfpg = sb("lfpg", [128, 256], F32, T); suf = sb("suf", [128, 256], F32, T)
            psumh = sb("psumh", [128, 16], F32, T); den_sb = [sb("den%d" % i, [16, 1], F32, T) for i in range(2)]
            dmask = sb("dmask", [16, 16], F32, T)
            r_qb, r_sc, r_bias, r_lfpg, r_suf, r_psumh, r_dmask = [Res() for _ in range(7)]
            r_lfpgs = [Res() for _ in range(16)]
            r_kb = [Res() for _ in range(4)]; r_vb16 = [Res() for _ in range(4)]
            r_p16 = [Res(), Res()]; r_den = [Res(), Res()]
            ck2, cv2, cl2 = K.ck_d, K.cv_d, K.cl_d

            def k_pass(s):
                pbuf = s % 2
                for hf in range(2):
                    mm(PS[hf][:, :], [(identf[0:16, s:s + 1].to_broadcast([16, 128]), qs[:, hf * 512:(hf + 1) * 512])],
                       [r_qs, rC], [rPS[hf]])
                    S.op("act", I("copy", qb[:, hf * 512:(hf + 1) * 512], PS[hf][:, :]), [rPS[hf]], [r_qb])
                for pg in range(16):
                    S.op("pool", I("indirect_dma_start", out=lfpg[:, pg * 16:(pg + 1) * 16], out_offset=None, in_=cl2,
                                   in_offset=bass.IndirectOffsetOnAxis(ap=idx[:, s * 16 + pg:s * 16 + pg + 1], axis=0)),
                         [r_idx], [r_lfpgs[pg]], chan=ch_l)
                yield
                mm(PS[6][:, 0:256], [(ustr[:, :], lfpg[:, :])], r_lfpgs + [rC], [rPS[6]])
                mm(PS[6][:, 256:512], [(ones1[:, :], lfpg[:, :])], r_lfpgs + [rC], [rPS[6]])
                S.op("dve", I("tensor_copy", suf[:, 240:256], lfsb[:, s * 16:(s + 1) * 16]), [r_lfsb], [r_suf])
                for pg in range(14, -1, -1):
                    S.op("dve", I("tensor_tensor", suf[:, pg * 16:(pg + 1) * 16], suf[:, (pg + 1) * 16:(pg + 2) * 16],
                                  PS[6][:, 256 + (pg + 1) * 16:256 + (pg + 2) * 16], ALU.add), [r_suf, rPS[6]], [r_suf])
                S.op("dve", I("tensor_tensor", bias_s[:, :], PS[6][:, 0:256], suf[:, :], ALU.add), [rPS[6], r_suf], [r_bias])
                yield
                for pg in range(16):
                    k = nxt("k", 4)
                    S.op("pool", I("indirect_dma_start", out=kb[k][:, :], out_offset=None, in_=ck2,
                                   in_offset=bass.IndirectOffsetOnAxis(ap=idx[:, s * 16 + pg:s * 16 + pg + 1], axis=0)),
                         [r_idx], [r_kb[k]], chan=ch_k[k])
                    S.op("dve", I("tensor_tensor", kb[k][:, :], kb[k][:, :], qb[:, :], ALU.mult), [r_kb[k], r_qb], [r_kb[k]])
                    S.op("dve", I("tensor_reduce", sc[:, pg * 16:(pg + 1) * 16], kb[k][:, :].rearrange("p (h d) -> p h d", d=64),
                                  AX.X, ALU.add), [r_kb[k]], [r_sc])
                    yield
                S.op("dve", I("scalar_tensor_tensor", sc[:, :], sc[:, :], 0.125, bias_s[:, :], ALU.mult, ALU.add), [r_sc, r_bias], [r_sc])
                S.op("act", I("activation", p16[pbuf][:, :], sc[:, :], AF.Exp), [r_sc], [r_p16[pbuf]])
                S.op("dve", I("tensor_reduce", psumh[:, :], p16[pbuf][:, :].rearrange("p (g h) -> p h g", h=16), AX.X, ALU.add),
                     [r_p16[pbuf]], [r_psumh])
                mm(PS[6][:16, 0:1], [(psumh[:, :], ones1[:, 0:1])], [r_psumh, rC], [rPS[6]])
                S.op("act", I("copy", den_sb[pbuf][:, :], PS[6][:16, 0:1]), [rPS[6]], [r_den[pbuf]])
                yield

            def v_pass(s):
                pbuf = s % 2
                for pg in range(16):
                    k = nxt("v", 4)
                    S.op("pool", I("indirect_dma_start", out=vb16[k][:, :], out_offset=None, in_=cv2,
                                   in_offset=bass.IndirectOffsetOnAxis(ap=idx[:, s * 16 + pg:s * 16 + pg + 1], axis=0)),
                         [r_idx], [r_vb16[k]], chan=ch_v[k])
                    for hf in range(2):
                        S.op("pe", I("matmul", PS[2 + hf][:16, :], p16[pbuf][:, pg * 16:(pg + 1) * 16], vb16[k][:, hf * 512:(hf + 1) * 512],
                                     start=(pg == 0), stop=(pg == 15)), [r_p16[pbuf], r_vb16[k]], [rPS[2 + hf]])
                    yield
                for hf in range(2):
                    S.op("dve", I("tensor_tensor", masked[:, hf * 512:(hf + 1) * 512], PS[2 + hf][:16, :], bmask[:, hf * 512:(hf + 1) * 512],
                                  ALU.mult), [rPS[2 + hf], r_ec], [r_masked])
                    S.op("pe", I("matmul", PS[4 + hf][:16, :], ec[:, s * 16:(s + 1) * 16], masked[:, hf * 512:(hf + 1) * 512],
                                 start=(s == 0), stop=(s == 15)), [r_masked, r_ec], [rPS[4 + hf]])
                S.op("dve", I("tensor_scalar", dmask[:, :], identf[0:16, 0:16], den_sb[pbuf][:, 0:1], None, ALU.mult),
                     [r_den[pbuf], rC], [r_dmask])
                mm(PS[6][:16, 32:48], [(ec[:, s * 16:(s + 1) * 16], dmask[:, :])], [r_dmask, r_ec], [rPS[6]])
                S.op("dve", I("tensor_tensor", dacc[:, :], dacc[:, :], PS[6][:16, 32:48], ALU.add), [r_dacc, rPS[6]], [r_dacc])
                yield

            for s in range(17):
                gens = []
                if s < 16:
                    gens.append(k_pass(s))
                if s >= 1:
                    gens.append(v_pass(s - 1))
                while gens:
                    for g_ in list(gens):
                        try:
                            next(g_)
                        except StopIteration:
                            gens.remove(g_)
            S.barrier()
        with ExitStack() as T:
            wb = [sb("wc%d" % i, [128, 8, 256], BF16, T) for i in range(2)]
            xn = sb("xo", [16, 1024], BF16, T)
            r_wb = [Res(), Res()]; r_xn = Res()
            S.op("dve", I("tensor_tensor", masked[:, :].rearrange("p (h d) -> p h d", d=64), vss[:, :].rearrange("p (h d) -> p h d", d=64),
                          pself[:, :].unsqueeze(2).to_broadcast([16, 16, 64]), ALU.mult), [K.r_vss, r_pself], [r_masked])
            for hf in range(2):
                S.op("dve", I("tensor_tensor", masked[:, hf * 512:(hf + 1) * 512], masked[:, hf * 512:(hf + 1) * 512], PS[4 + hf][:16, :],
                              ALU.add), [r_masked, rPS[4 + hf]], [r_masked])
            S.op("dve", I("tensor_tensor", dacc[:, :], dacc[:, :], pself[:, :], ALU.add), [r_dacc, r_pself], [r_dacc])
            S.op("dve", I("reciprocal", dacc[:, :], dacc[:, :]), [r_dacc], [r_dacc])
            S.op("dve", I("tensor_tensor", masked[:, :].rearrange("p (h d) -> p h d", d=64), masked[:, :].rearrange("p (h d) -> p h d", d=64),
                          dacc[:, :].unsqueeze(2).to_broadcast([16, 16, 64]), ALU.mult), [r_masked, r_dacc], [r_masked])
            S.op("dve", I("tensor_tensor", xn[:, :], masked[:, :], szs[:, :], ALU.mult), [r_masked, r_szs], [r_xn])
            S.op("pe", [I("transpose", PSB[:, c * 16:(c + 1) * 16], xn[:, c * 128:(c + 1) * 128], identb[0:16, 0:16]) for c in range(8)],
                 [r_xn, rC], [rPSB])
            S.op("act", I("copy", xnT[:, :, :], PSB[:, 0:128].rearrange("p (c n) -> p c n", c=8)), [rPSB], [r_xnT])
            for v in range(4):
                ws = nxt("w", 2)
                dma(wb[ws][:], K.s_bwout[l, v // 2].rearrange("p (c n) -> p c n", c=8)[:, :, (v % 2) * 256:(v % 2 + 1) * 256],
                    [], [r_wb[ws]], ch_w[ws])
                bk = 5 + nxt("p", 2)
                mm(PS[bk][:16, 0:256], [(xnT[:, c, :], wb[ws][:, c, :]) for c in range(8)], [r_xnT, r_wb[ws]], [rPS[bk]])
                xs_ = XS[:, v * 256:(v + 1) * 256]
                S.op("dve", I("tensor_tensor", xs_, xs_, PS[bk][:16, 0:256], ALU.add), [rXS, rPS[bk]], [rXS])
            S.barrier()
```

```python
import numpy as np
import ml_dtypes
from contextlib import ExitStack
import concourse.bass as bass
import concourse.mybir as mybir
from concourse.bass_utils import run_bass_kernel_spmd

F32 = mybir.dt.float32
BF16 = mybir.dt.bfloat16
I32 = mybir.dt.int32
ALU = mybir.AluOpType
AF = mybir.ActivationFunctionType
AX = mybir.AxisListType
EPS = 1e-6
NCORES = 8


def I(name, *a, **kw):
    return (name, a, kw)


class Res:
    __slots__ = ("name", "w", "r")

    def __init__(self, name=""):
        self.name = name
        self.w = None
        self.r = {}


class Chan:
    __slots__ = ("sem", "count", "name")

    def __init__(self, sem, name):
        self.sem = sem
        self.count = 0
        self.name = name


class Sched:
    ENG = ("pe", "act", "dve", "pool", "sp")
    CE = ("pe", "act", "dve", "pool")

    def __init__(self, nc, esem, free):
        self.nc = nc
        self.items = {e: [] for e in self.ENG}
        self.cnt = {e: 0 for e in self.ENG}
        self.waited = {e: {} for e in self.ENG}
        self.esem = esem
        self.free = list(free)
        self.chans = []

    def chan(self, name=""):
        c = Chan(self.free.pop(), name)
        self.chans.append(c)
        return c

    def op(self, eng, fn, reads=(), writes=(), chan=None):
        deps = {}

        def add(k, v):
            if deps.get(k, 0) < v:
                deps[k] = v
        for r in reads:
            if r.w is not None:
                add(*r.w)
        for w in writes:
            if w.w is not None:
                add(*w.w)
            for k, v in w.r.items():
                add(k, v)
        waits = []
        wd = self.waited[eng]
        for k, v in deps.items():
            if chan is None and k == eng and eng == "pe":
                continue
            if isinstance(k, Chan):
                v = k.count
            if wd.get(k, 0) >= v:
                continue
            wd[k] = v
            waits.append((k.sem if isinstance(k, Chan) else self.esem[k], v))
        if chan is not None:
            chan.count += 16
            ev = (chan, chan.count)
            inc = (chan.sem, 16)
        else:
            self.cnt[eng] += 1
            ev = (eng, self.cnt[eng])
            inc = (self.esem[eng], 1)
        self.items[eng].append((waits, fn, inc))
        for w in writes:
            w.w = ev
            w.r = {}
        for r in reads:
            if r.r.get(ev[0], 0) < ev[1]:
                r.r[ev[0]] = ev[1]
        return ev

    def barrier(self):
        for e in self.ENG:
            waits = []
            wd = self.waited[e]
            for c in self.chans:
                if c.count and wd.get(c, 0) < c.count:
                    wd[c] = c.count
                    waits.append((c.sem, c.count))
            for o in self.CE:
                if o != e and self.cnt[o] and wd.get(o, 0) < self.cnt[o]:
                    wd[o] = self.cnt[o]
                    waits.append((self.esem[o], self.cnt[o]))
            if waits:
                self.items[e].append((waits, None, None))

    def emit(self, block):
        items = self.items

        def runner(name):
            def run(e):
                for waits, fn, inc in items[name]:
                    for sem, val in waits:
                        e.wait_ge(sem, val)
                    if fn is not None:
                        calls = fn if isinstance(fn, list) else [fn]
                        for (nm, a, kw) in calls:
                            ins = getattr(e, nm)(*a, **kw)
                        ins.then_inc(inc[0], inc[1])
            return run
        block.tensor(runner("pe"))
        block.scalar(runner("act"))
        block.vector(runner("dve"))
        block.gpsimd(runner("pool"))
        block.sync(runner("sp"))


def make_consts():
    bf = ml_dtypes.bfloat16
    k = np.arange(128)
    c = {}
    c["identb"] = np.eye(128, dtype=np.float32).astype(bf)
    c["identf"] = np.eye(128, dtype=np.float32)
    c["tri"] = (k[None, :] >= k[:, None]).astype(np.float32).astype(bf)
    c["lincl"] = (k[:, None] <= k[None, :]).astype(np.float32)
    c["ustr"] = (k[:, None] > k[None, :]).astype(np.float32)
    c["onesdiv"] = np.full((128, 128), 1.0 / 1024.0, np.float32)
    c["ones1"] = np.ones((128, 128), np.float32)
    c["rs0"] = np.zeros((128, 128), np.float32)
    c["rs0"][0, :] = 1.0
    eb = np.zeros((16, 16, 128), np.float32)
    ec = np.zeros((16, 16, 16), np.float32)
    for s in range(16):
        eb[s, s, :] = 1.0
        ec[:, s, s] = 1.0
    c["eb"] = eb.reshape(16, 2048)
    c["ec"] = ec.reshape(16, 256)
    bm = np.zeros((16, 1024), np.float32)
    for h in range(16):
        bm[h, h * 64:(h + 1) * 64] = 1.0
    c["bmask"] = bm
    c["pidx"] = k.astype(np.float32)[:, None].copy()
    return c


CONST_SPECS = [("identb", [128, 128], BF16), ("identf", [128, 128], F32), ("tri", [128, 128], BF16),
               ("lincl", [128, 128], F32), ("ustr", [128, 128], F32), ("onesdiv", [128, 128], F32),
               ("ones1", [128, 128], F32), ("rs0", [128, 128], F32), ("eb", [16, 2048], F32),
               ("ec", [16, 256], F32), ("bmask", [16, 1024], F32), ("pidx", [128, 1], F32)]


def build(NP=2560, upto=99, attn=True):
    nc = bass.Bass("TRN2", target_bir_lowering=False)

    def din(name, shape, dt=F32):
        return nc.dram_tensor(name, shape, dt, kind="ExternalInput").ap()

    def dout(name, shape, dt=F32):
        return nc.dram_tensor(name, shape, dt, kind="ExternalOutput").ap()

    def dscr(name, shape, dt=BF16):
        return nc.dram_tensor(name, shape, dt, kind="Internal").ap()

    x_d = din("x", [2048, 1024]); xs_d = din("xs", [16, 1024]); st_d = din("state", [2, 480, 1024])
    if attn:
        ck_d = din("ck", [NP * 128, 1024]); cv_d = din("cv", [NP * 128, 1024]); cl_d = din("cl", [NP * 128, 16])
        pt_d = din("pt", [1, 256], I32)
    a_norm = din("a_norm", [2, 1024]); a_w_in = din("a_w_in", [2, 1024, 3072]); a_cw = din("a_cw", [2, 31, 1024])
    a_cb = din("a_cb", [2, 1024]); a_lg = din("a_lg", [2, 1024]); a_lb = din("a_lb", [2, 1024])
    a_w_out = din("a_w_out", [2, 1024, 1024]); kv_norm = din("kv_norm", [1, 1024]); kv_w = din("kv_w", [1024, 2064])
    kv_fb = din("kv_fb", [1, 16]); k_norm = din("k_norm", [1, 64]); b_norm = din("b_norm", [2, 1024])
    b_w_in = din("b_w_in", [2, 1024, 2048]); q_norm = din("q_norm", [2, 64]); b_w_out = din("b_w_out", [2, 1024, 1024])
    cd = {n: din("c_" + n, s, dt) for n, s, dt in CONST_SPECS}

    y_d = dout("y", [2048, 1024]); ys_d = dout("ys", [16, 1024]); convp_d = dout("convp", [2, 30, 1024])
    convs_d = dout("convs", [2, 480, 1024]); kp_d = dout("kp", [2048, 1024]); vp_d = dout("vp", [2048, 1024])
    lfp_d = dout("lfp", [2048, 16]); ks_d = dout("ks", [16, 1024]); vs_d = dout("vs", [16, 1024]); lfs_d = dout("lfs", [16, 16])

    s_awin = dscr("s_awin", [2, 6, 128, 4096]); s_awout = dscr("s_awout", [2, 2, 128, 4096])
    s_kvw = dscr("s_kvw", [4, 128, 4096]); s_kvf = dscr("s_kvf", [128, 128])
    s_bwin = dscr("s_bwin", [2, 4, 128, 4096]); s_bwout = dscr("s_bwout", [2, 2, 128, 4096])
    s_bias = dscr("s_bias", [16, 128, 256], F32)

    with ExitStack() as G:
        esem = {e: G.enter_context(nc.semaphore("s_" + e)) for e in Sched.CE}
        free = [G.enter_context(nc.semaphore("ch%d" % i)) for i in range(80)]
        S = Sched(nc, esem, free)

        def sbt(es, name, shape, dt):
            return es.enter_context(nc.sbuf_tensor(name, shape, dt))

        def dma(dst, src, reads, writes, ch, eng="sp"):
            S.op(eng, I("dma_start", out=dst, in_=src), reads, writes, chan=ch)

        def mm(out_ap, pairs, reads, writes):
            n = len(pairs)
            S.op("pe", [I("matmul", out_ap, l, r, start=(i == 0), stop=(i == n - 1)) for i, (l, r) in enumerate(pairs)],
                 reads, writes)

        X = sbt(G, "X", [128, 16, 1024], F32); XS = sbt(G, "XS", [16, 1024], F32)
        C = {n: sbt(G, "k_" + n, s, dt) for n, s, dt in CONST_SPECS if n not in ("eb", "ec", "bmask")}
        epsc = sbt(G, "epsc", [128, 1], F32); onec = sbt(G, "onec", [128, 1], F32)
        rX = [Res("X%d" % i) for i in range(16)]; rXS = Res("XS"); rC = Res("consts")
        PS = [G.enter_context(nc.psum_tensor("ps%d" % i, [128, 512], F32)) for i in range(7)]
        PSB = G.enter_context(nc.psum_tensor("psb", [128, 1024], BF16))
        rPS = [Res("ps%d" % i) for i in range(7)]; rPSB = Res("psb")
        ld_c = S.chan("ldc"); ld_x = S.chan("ldx")
        for n in C:
            dma(C[n][:], cd[n], [], [rC], ld_c)
        S.op("dve", I("memset", epsc[:], EPS), [], [rC])
        S.op("dve", I("memset", onec[:], 1.0), [], [rC])
        xv = x_d.rearrange("(i p) d -> p i d", p=128)
        for i in range(16):
            dma(X[:, i, :], xv[:, i, :], [], [rX[i]], ld_x)
        dma(XS[:], xs_d, [], [rXS], ld_x)

        with ExitStack() as P0:
            stg = [sbt(P0, "stg%d" % i, [128, 8, 512], F32) for i in range(2)]
            cvb = [sbt(P0, "cvb%d" % i, [128, 8, 512], BF16) for i in range(2)]
            rstg = [Res(), Res()]; rcvb = [Res(), Res()]
            chl = [S.chan("cvl0"), S.chan("cvl1")]; chs = [S.chan("cvs0"), S.chan("cvs1")]
            rscr = Res("scratch")
            state = {"n": 0}

            def conv_unit(W, pieces, dst, scale, width=512):
                k = state["n"] % 2
                state["n"] += 1
                for (c0, w, d0) in pieces:
                    dma(stg[k][:, :, d0:d0 + w], W[:, c0:c0 + w].rearrange("(c p) n -> p c n", p=128), [], [rstg[k]], chl[k])
                eng = ("dve", "act")[state["n"] % 2]
                src = stg[k][:, :, 0:width]; dstt = cvb[k][:, :, 0:width]
                if eng == "act":
                    S.op("act", I("mul", dstt, src, scale), [rstg[k]], [rcvb[k]])
                elif scale == 1.0:
                    S.op(eng, I("tensor_copy", dstt, src), [rstg[k]], [rcvb[k]])
                else:
                    S.op(eng, I("tensor_scalar", dstt, src, scale, None, ALU.mult), [rstg[k]], [rcvb[k]])
                dma(dst.rearrange("p (c n) -> p c n", c=8), dstt, [rcvb[k]], [rscr], chs[k])

            for l in range(2):
                for u in range(4):
                    conv_unit(a_w_in[l], [(256 * u, 256, 0), (1024 + 256 * u, 256, 256)], s_awin[l, u], 0.5)
                for u in range(2):
                    conv_unit(a_w_in[l], [(2048 + 512 * u, 512, 0)], s_awin[l, 4 + u], 0.5)
                for u in range(2):
                    conv_unit(a_w_out[l], [(512 * u, 512, 0)], s_awout[l, u], 1.0)
            for u in range(4):
                conv_unit(kv_w, [(512 * u, 512, 0)], s_kvw[u], 1.0)
            conv_unit(kv_w, [(2048, 16, 0)], s_kvf, 1.0, width=16)
            for l in range(2):
                for u in range(4):
                    conv_unit(b_w_in[l], [(512 * u, 512, 0)], s_bwin[l, u], 1.0 if u < 2 else 0.5)
                for u in range(2):
                    conv_unit(b_w_out[l], [(512 * u, 512, 0)], s_bwout[l, u], 1.0)
            S.barrier()

        def rmsnorm_rstd(es_tmp, srcs, P, ncol, rres, wres, tag):
            ss = es_tmp["ss"]; junk = es_tmp["junk"]; rstd = es_tmp["rstd"]
            for i, (ap, r) in enumerate(srcs):
                S.op("act", I("activation", junk[:P, :], ap, AF.Square, accum_out=ss[:P, i:i + 1]),
                     [r], [es_tmp["rjunk"], es_tmp["rss"]])
            n = len(srcs)
            S.op("dve", I("tensor_scalar", rstd[:P, 0:n], ss[:P, 0:n], 1.0 / 1024.0, EPS, ALU.mult, ALU.add),
                 [es_tmp["rss"]], [es_tmp["rrstd"]])
            S.op("act", I("activation", rstd[:P, 0:n], rstd[:P, 0:n], AF.Sqrt), [es_tmp["rrstd"]], [es_tmp["rrstd"]])
            S.op("dve", I("reciprocal", rstd[:P, 0:n], rstd[:P, 0:n]), [es_tmp["rrstd"]], [es_tmp["rrstd"]])
            return rstd

        def norm_tmp(es, tag):
            return dict(ss=sbt(es, "ss" + tag, [128, 8], F32), junk=sbt(es, "junk" + tag, [128, 1024], BF16),
                        rstd=sbt(es, "rstd" + tag, [128, 8], F32), rjunk=Res(), rss=Res(), rrstd=Res())

        import os as _os
        dbg_list = []
        DEBUG = bool(_os.environ.get("KDEBUG"))
        dbg_ch = S.chan("dbg") if DEBUG else None

        dbg_state = {}
        if DEBUG:
            dbg_state["t"] = G.enter_context(nc.sbuf_tensor("dbgt", [128, 256], F32))
            dbg_state["r"] = Res()

        def dbg(name, ap, reads, P=128, N=256):
            if not DEBUG:
                return
            if "t" not in dbg_state:
                dbg_state["t"] = G.enter_context(nc.sbuf_tensor("dbgt", [128, 256], F32))
                dbg_state["r"] = Res()
            t, r = dbg_state["t"], dbg_state["r"]
            d = nc.dram_tensor("dbg_" + name, [P, N], F32, kind="ExternalOutput").ap()
            S.op("dve", I("tensor_copy", t[:P, 0:N], ap), reads, [r])
            dma(d, t[:P, 0:N], [r], [], dbg_ch)
        import types
        K = types.SimpleNamespace(**{k: v for k, v in locals().items() if k != "K"})
        K.G = G
        K.r_sbias = [Res() for _ in range(16)]
        K.ch_sb = [S.chan("sbias%d" % i) for i in range(2)]
        if upto >= 1:
            phase_a(K)
        if upto >= 2:
            phase_kv(K)
        if upto >= 3:
            phase_b(K)

        st_y = S.chan("sty")
        yv = y_d.rearrange("(i p) d -> p i d", p=128)
        for i in range(16):
            dma(yv[:, i, :], X[:, i, :], [rX[i]], [], st_y)
        dma(ys_d, XS[:], [rXS], [], st_y)
        S.barrier()
        with nc.Block() as block:
            S.emit(block)
    return nc


def brow(ap2d, row, ncols, P):
    t = ap2d.tensor
    w = ap2d.shape[1]
    return bass.AP(t, row * w, [[0, P], [1, ncols]])


def phase_a(K):
    nc, S, X, XS, C, PS, PSB = K.nc, K.S, K.X, K.XS, K.C, K.PS, K.PSB
    rX, rXS, rPS, rPSB, rC = K.rX, K.rXS, K.rPS, K.rPSB, K.rC
    dma, mm = K.dma, K.mm
    identb, identf, onesdiv = C["identb"], C["identf"], C["onesdiv"]
    with ExitStack() as P:
        def sb(name, shape, dt):
            return P.enter_context(nc.sbuf_tensor("pa_" + name, shape, dt))
        gb = sb("gb", [128, 1024], F32); prow = sb("prow", [34, 1024], F32)
        pcol = sb("pcol", [128, 8, 34], F32); lg2 = sb("lg2", [128, 8, 2], F32)
        xn = [sb("xn%d" % i, [128, 1024], BF16) for i in range(2)]
        xnT = sb("xnT", [128, 8, 512], BF16)
        wbuf = [sb("wb%d" % i, [128, 8, 512], BF16) for i in range(2)]
        glu = sb("glu", [128, 8, 542], F32)
        th = [sb("th%d" % i, [128, 512], F32) for i in range(2)]
        cv = sb("cv", [128, 8, 512], F32)
        sq = [sb("sq%d" % i, [128, 512], F32) for i in range(2)]
        mean_sb = sb("mean_sb", [128, 512], F32); rstd_sb = sb("rstd_sb", [128, 512], F32)
        sz = sb("sz", [128, 8, 512], BF16); h = sb("h", [128, 8, 512], BF16)
        sttok = [sb("sttok0", [120, 1024], F32)] * 2
        stT = sb("stT", [128, 8, 480], F32); rowt = sttok[0]
        NPE = 16
        diag = [sb("diag%d" % i, [128, NPE, 128], BF16) for i in range(2)]
        gluB = [sb("gluB%d" % i, [128, 542], BF16) for i in range(2)]
        r_diag = [Res(), Res()]; r_gluB = [Res(), Res()]
        rr = sb("rr", [128, 16], F32)
        NT = K.norm_tmp(P, "a")
        r_gb, r_prow, r_pcol = Res(), Res(), Res()
        r_xn = [Res(), Res()]; r_xnT = Res(); r_wb = [Res(), Res()]
        r_glu = [Res() for _ in range(8)]; r_cv = [Res() for _ in range(8)]; r_th = [Res(), Res()]
        r_sq = [Res(), Res()]; r_mean, r_rstd = Res(), Res(); r_sz = [Res() for _ in range(8)]
        r_h = [Res() for _ in range(8)]; r_sttok = [Res()] * 2; r_stT = Res(); r_rowt = r_sttok[0]; r_rr = Res()
        ch_w = [S.chan("aw0"), S.chan("aw1")]; ch_p = S.chan("ap"); ch_st = [S.chan("ast0")] * 2
        ch_o = S.chan("aout")
        cnt = {"w": 0, "bank": 0, "th": 0, "sq": 0, "dg": 0}

        def nxt(key, n):
            v = cnt[key] % n
            cnt[key] += 1
            return v

        for l in range(2):
            dma(gb[:], brow(K.a_norm, l, 1024, 128), [], [r_gb], ch_p)
            dma(prow[0:31, :], K.a_cw[l], [], [r_prow], ch_p)
            dma(prow[31:32, :], K.a_cb[l:l + 1, :], [], [r_prow], ch_p)
            dma(prow[32:33, :], K.a_lg[l:l + 1, :], [], [r_prow], ch_p)
            dma(prow[33:34, :], K.a_lb[l:l + 1, :], [], [r_prow], ch_p)

            S.op("pe", [I("transpose", PS[6][:, cc * 34:(cc + 1) * 34], prow[0:34, cc * 128:(cc + 1) * 128], identf[0:34, 0:34])
                        for cc in range(8)], [r_prow, rC], [rPS[6]])
            S.op("act", I("copy", pcol[:].rearrange("p c k -> p (c k)"), PS[6][:, 0:272]), [rPS[6]], [r_pcol])
            S.op("dve", I("tensor_scalar", lg2[:], pcol[:, :, 32:34], 0.5, None, ALU.mult), [r_pcol], [r_pcol])
            S.op("pool", I("memset", glu[:, :, 0:30], 0.0), [], r_glu)

            for blk in range(5):
                samp = blk == 4
                N = 16 if samp else 512
                Pn = 16 if samp else 128
                if samp:
                    srcs = [(XS[:, :], rXS)]
                else:
                    srcs = [(X[:, 4 * blk + t, :], rX[4 * blk + t]) for t in range(4)]
                rstd = K.rmsnorm_rstd(NT, srcs, Pn, len(srcs), None, None, "a")
                for t, (ap, r) in enumerate(srcs):
                    k = t % 2
                    S.op("dve", I("scalar_tensor_tensor",
                        xn[k][:Pn, :], ap, rstd[:Pn, t:t + 1], gb[:Pn, :], ALU.mult, ALU.mult),
                        [r, NT["rrstd"], r_gb], [r_xn[k]])
                    if samp:
                        S.op("pe", [I("transpose", PSB[:, c * 16:(c + 1) * 16], xn[k][:16, c * 128:(c + 1) * 128], identb[0:16, 0:16])
                                    for c in range(8)], [r_xn[k], rC], [rPSB])
                        S.op("act", I("copy", xnT[:, :, 0:16], PSB[:, 0:128].rearrange("p (c n) -> p c n", c=8)),
                             [rPSB], [r_xnT])
                    else:
                        S.op("pe", [I("transpose", PSB[:, c * 128:(c + 1) * 128], xn[k][:, c * 128:(c + 1) * 128], identb[:, :])
                                    for c in range(8)], [r_xn[k], rC], [rPSB])
                        S.op("act", I("copy", xnT[:, :, t * 128:(t + 1) * 128],
                                                          PSB[:, :].rearrange("p (c n) -> p c n", c=8)), [rPSB], [r_xnT])
                if samp:
                    for t in range(4):
                        k = t % 2
                        dma(sttok[k][:, :], K.st_d[l, 120 * t:120 * (t + 1), :], [], [r_sttok[k]], ch_st[k])
                        for half in range(2):
                            S.op("pe", [I("transpose", PS[6][:, j * 120:(j + 1) * 120],
                                          sttok[k][0:120, (4 * half + j) * 128:(4 * half + j + 1) * 128], identf[0:120, 0:120])
                                        for j in range(4)], [r_sttok[k], rC], [rPS[6]])
                            S.op("act", I("copy",
                                stT[:, 4 * half:4 * half + 4, 120 * t:120 * (t + 1)],
                                PS[6][:, 0:480].rearrange("p (c n) -> p c n", c=4)), [rPS[6]], [r_stT])
                pend = []
                pend_diag = []
                stats_q = []
                fin_add = []

                def flush():
                    while len(stats_q) > (0 if flush.final else 1):
                        cc, kq = stats_q.pop(0)
                        S.op("pe", I("matmul", PS[4][:, 0:N], onesdiv[:, :], cv[:, cc, 0:N], start=(cc == 0), stop=(cc == 7)),
                             [r_cv[cc], rC], [rPS[4]])
                        S.op("pe", I("matmul", PS[5][:, 0:N], onesdiv[:, :], sq[kq][:, 0:N], start=(cc == 0), stop=(cc == 7)),
                             [r_sq[kq], rC], [rPS[5]])
                    for (cc, dg, cb_) in pend_diag:
                        S.op("pe", [I("matmul", PS[cb_][:, 0:N], diag[dg][:, kk, :], gluB[dg][:, kk:kk + N],
                                      start=(kk == 0), stop=(kk == NPE - 1)) for kk in range(NPE)],
                             [r_diag[dg], r_gluB[dg]], [rPS[cb_]])
                    pend_diag.clear()
                    for (cc, cb_) in fin_add:
                        S.op("dve", I("tensor_tensor", cv[:, cc, 0:N], cv[:, cc, 0:N], PS[cb_][:, 0:N], ALU.add),
                             [r_cv[cc], rPS[cb_]], [r_cv[cc]])
                        kq = nxt("sq", 2)
                        S.op("act", I("activation", sq[kq][:, 0:N], cv[:, cc, 0:N], AF.Square), [r_cv[cc]], [r_sq[kq]])
                        pend.append((cc, kq))
                    fin_add.clear()
                    for item in pend:
                        stats_q.append(item)
                    pend.clear()
                flush.final = False

                for u in range(6):
                    ws = nxt("w", 2)
                    dma(wbuf[ws][:], K.s_awin[l, u].rearrange("p (c n) -> p c n", c=8), [], [r_wb[ws]], ch_w[ws])

                    def mmf(fc, bank):
                        mm(PS[bank][:, 0:N], [(wbuf[ws][:, c, fc * 128:(fc + 1) * 128], xnT[:, c, 0:N]) for c in range(8)],
                           [r_wb[ws], r_xnT], [rPS[bank]])
                    if u < 4:
                        for j in range(2):
                            cc = 2 * u + j
                            ba = nxt("bank", 3); bg = nxt("bank", 3)
                            mmf(j, ba); mmf(2 + j, bg)
                            flush()
                            kt = nxt("th", 2)
                            S.op("act", I("activation", th[kt][:, 0:N], PS[bg][:, 0:N], AF.Tanh),
                                 [rPS[bg]], [r_th[kt]])
                            S.op("dve", I("scalar_tensor_tensor",
                                glu[:, cc, 30:30 + N], th[kt][:, 0:N], 1.0, PS[ba][:, 0:N], ALU.add, ALU.mult),
                                [r_th[kt], rPS[ba]], [r_glu[cc]])
                            if not samp:
                                dg = nxt("dg", 2)
                                S.op("pool", I("tensor_copy", gluB[dg][:, :], glu[:, cc, :]), [r_glu[cc]], [r_gluB[dg]])
                                for kk in range(NPE):
                                    S.op("act", I("mul", diag[dg][:, kk, :], identb[:, :], pcol[:, cc, kk:kk + 1]),
                                         [rC, r_pcol], [r_diag[dg]])
                                cb_ = (3, 6)[dg]
                                pend_diag.append((cc, dg, cb_))
                                S.op("dve", I("tensor_scalar",
                                    cv[:, cc, 0:N], glu[:, cc, NPE:NPE + N], pcol[:, cc, NPE:NPE + 1], pcol[:, cc, 31:32], ALU.mult, ALU.add),
                                    [r_glu[cc], r_pcol], [r_cv[cc]])
                                for kk in range(NPE + 1, 31):
                                    S.op("dve", I("scalar_tensor_tensor",
                                        cv[:, cc, 0:N], glu[:, cc, kk:kk + N], pcol[:, cc, kk:kk + 1], cv[:, cc, 0:N],
                                        ALU.mult, ALU.add), [r_glu[cc], r_pcol, r_cv[cc]], [r_cv[cc]])
                                fin_add.append((cc, cb_))
                            else:
                                kq0 = nxt("sq", 2)
                                S.op("dve", I("tensor_tensor",
                                    sq[kq0][:, 0:480].rearrange("p (s k) -> p s k", s=16),
                                    stT[:, cc, :].rearrange("p (s k) -> p s k", s=16),
                                    pcol[:, cc, 0:30].unsqueeze(1).to_broadcast([128, 16, 30]), ALU.mult),
                                    [r_stT, r_pcol], [r_sq[kq0]])
                                S.op("dve", I("tensor_reduce",
                                    rr[:, 0:16], sq[kq0][:, 0:480].rearrange("p (s k) -> p s k", s=16), AX.X, ALU.add),
                                    [r_sq[kq0]], [r_rr])
                                S.op("dve", I("scalar_tensor_tensor",
                                    cv[:, cc, 0:16], glu[:, cc, 30:46], pcol[:, cc, 30:31], rr[:, 0:16], ALU.mult, ALU.add),
                                    [r_glu[cc], r_pcol, r_rr], [r_cv[cc]])
                                S.op("dve", I("tensor_scalar",
                                    cv[:, cc, 0:16], cv[:, cc, 0:16], pcol[:, cc, 31:32], None, ALU.add),
                                    [r_cv[cc], r_pcol], [r_cv[cc]])
                            if samp:
                                kq = nxt("sq", 2)
                                S.op("act", I("activation", sq[kq][:, 0:N], cv[:, cc, 0:N], AF.Square),
                                     [r_cv[cc]], [r_sq[kq]])
                                S.op("pe", I("matmul", PS[4][:, 0:N], onesdiv[:, :], cv[:, cc, 0:N], start=(cc == 0), stop=(cc == 7)),
                                     [r_cv[cc], rC], [rPS[4]])
                                S.op("pe", I("matmul", PS[5][:, 0:N], onesdiv[:, :], sq[kq][:, 0:N], start=(cc == 0), stop=(cc == 7)),
                                     [r_sq[kq], rC], [rPS[5]])
                    else:
                        for fc in range(4):
                            cc = 4 * (u - 4) + fc
                            bz = nxt("bank", 3)
                            mmf(fc, bz)
                            flush()
                            kt = nxt("th", 2)
                            S.op("act", I("activation", th[kt][:, 0:N], PS[bz][:, 0:N], AF.Tanh),
                                 [rPS[bz]], [r_th[kt]])
                            S.op("dve", I("scalar_tensor_tensor",
                                sz[:, cc, 0:N], th[kt][:, 0:N], 1.0, PS[bz][:, 0:N], ALU.add, ALU.mult),
                                [r_th[kt], rPS[bz]], [r_sz[cc]])
                flush(); flush.final = True; flush(); flush()
                kt = nxt("th", 2)
                S.op("act", I("activation", th[kt][:, 0:N], PS[4][:, 0:N], AF.Square), [rPS[4]], [r_th[kt]])
                S.op("dve", I("tensor_tensor", rstd_sb[:, 0:N], PS[5][:, 0:N], th[kt][:, 0:N], ALU.subtract),
                     [rPS[5], r_th[kt]], [r_rstd])
                S.op("act", I("activation", rstd_sb[:, 0:N], rstd_sb[:, 0:N], AF.Sqrt, bias=K.epsc[:, 0:1]),
                     [r_rstd, rC], [r_rstd])
                S.op("dve", I("reciprocal", rstd_sb[:, 0:N], rstd_sb[:, 0:N]), [r_rstd], [r_rstd])
                S.op("act", I("copy", mean_sb[:, 0:N], PS[4][:, 0:N]), [rPS[4]], [r_mean])
                for cc in range(8):
                    c_ = cv[:, cc, 0:N]
                    S.op("dve", I("tensor_tensor", c_, c_, mean_sb[:, 0:N], ALU.subtract), [r_cv[cc], r_mean], [r_cv[cc]])
                    S.op("dve", I("tensor_tensor", c_, c_, rstd_sb[:, 0:N], ALU.mult), [r_cv[cc], r_rstd], [r_cv[cc]])
                    S.op("dve", I("tensor_scalar", c_, c_, lg2[:, cc, 0:1], lg2[:, cc, 1:2], ALU.mult, ALU.add),
                         [r_cv[cc], r_pcol], [r_cv[cc]])
                    kt = nxt("th", 2)
                    S.op("act", I("activation", th[kt][:, 0:N], c_, AF.Tanh), [r_cv[cc]], [r_th[kt]])
                    S.op("dve", I("scalar_tensor_tensor", c_, th[kt][:, 0:N], 1.0, c_, ALU.add, ALU.mult),
                         [r_cv[cc], r_th[kt]], [r_cv[cc]])
                    S.op("dve", I("tensor_tensor", h[:, cc, 0:N], c_, sz[:, cc, 0:N], ALU.mult),
                         [r_cv[cc], r_sz[cc]], [r_h[cc]])
                for u in range(2):
                    ws = nxt("w", 2)
                    dma(wbuf[ws][:], K.s_awout[l, u].rearrange("p (c n) -> p c n", c=8), [], [r_wb[ws]], ch_w[ws])
                    for t in range(1 if samp else 4):
                        bk = nxt("bank", 3)
                        if samp:
                            mm(PS[bk][0:16, :], [(h[:, c, 0:16], wbuf[ws][:, c, :]) for c in range(8)], r_h + [r_wb[ws]], [rPS[bk]])
                            xs_ = XS[:, u * 512:(u + 1) * 512]
                            S.op("dve", I("tensor_tensor", xs_, xs_, PS[bk][0:16, :], ALU.add),
                                 [rXS, rPS[bk]], [rXS])
                        else:
                            i = 4 * blk + t
                            mm(PS[bk][:, :], [(h[:, c, t * 128:(t + 1) * 128], wbuf[ws][:, c, :]) for c in range(8)],
                               r_h + [r_wb[ws]], [rPS[bk]])
                            x_ = X[:, i, u * 512:(u + 1) * 512]
                            S.op("dve", I("tensor_tensor", x_, x_, PS[bk][:, :], ALU.add),
                                 [rX[i], rPS[bk]], [rX[i]])
                if blk < 3:
                    S.op("pool", I("tensor_copy", glu[:, :, 0:30], glu[:, :, 512:542]), r_glu, r_glu)
                elif blk == 3 or samp:
                    nr = 16 if samp else 30
                    c0 = 30 if samp else 512
                    for half in range(2):
                        S.op("pe", [I("transpose", PS[6][0:nr, j * 128:(j + 1) * 128], glu[:, 4 * half + j, c0:c0 + nr], identf[:, :])
                                    for j in range(4)], r_glu + [rC], [rPS[6]])
                        S.op("act", I("copy", rowt[0:nr, half * 512:(half + 1) * 512], PS[6][0:nr, :]),
                             [rPS[6]], [r_rowt])
                    if samp:
                        cs = K.convs_d[l].rearrange("(s k) d -> s k d", k=30)
                        dma(cs[:, 29, :], rowt[0:16, :], [r_rowt], [], ch_o)
                        dma(cs[:, 0:29, :], K.st_d[l].rearrange("(s k) d -> s k d", k=30)[:, 1:30, :], [], [], ch_o)
                    else:
                        dma(K.convp_d[l], rowt[0:30, :], [r_rowt], [], ch_o)
        S.barrier()


def make_in_maps(inputs, cores, NP=2560, attn=True, pt_override=None):
    consts = make_consts()
    f = lambda a: np.ascontiguousarray(a, dtype=np.float32)
    maps = []
    for c in cores:
        m = {
            "x": f(inputs["x_prompt"][c]),
            "xs": f(inputs["x_sample"][16 * c:16 * c + 16, 0]),
            "state": f(inputs["state_conv"][:, 16 * c:16 * c + 16]).reshape(2, 480, 1024),
            "a_norm": f(inputs["a_norm"]), "a_w_in": f(inputs["a_w_in"]), "a_cw": f(inputs["a_conv_w"]),
            "a_cb": f(inputs["a_conv_b"]), "a_lg": f(inputs["a_ln_g"]), "a_lb": f(inputs["a_ln_b"]),
            "a_w_out": f(inputs["a_w_out"]), "kv_norm": f(inputs["kv_norm"]).reshape(1, 1024), "kv_w": f(inputs["kv_w"]),
            "kv_fb": f(inputs["kv_fb"]).reshape(1, 16), "k_norm": f(inputs["k_norm"]).reshape(1, 64),
            "b_norm": f(inputs["b_norm"]), "b_w_in": f(inputs["b_w_in"]), "q_norm": f(inputs["q_norm"]),
            "b_w_out": f(inputs["b_w_out"]),
        }
        if attn:
            m["ck"] = inputs["cache_k"].reshape(NP * 128, 1024)
            m["cv"] = inputs["cache_v"].reshape(NP * 128, 1024)
            m["cl"] = inputs["cache_logf"].reshape(NP * 128, 16)
            pt = inputs["page_table"] if pt_override is None else pt_override
            m["pt"] = np.ascontiguousarray(pt[16 * c:16 * c + 16]).reshape(1, 256).astype(np.int32)
        for n, arr in consts.items():
            m["c_" + n] = arr
        maps.append(m)
    return maps


def assemble(results):
    cat = lambda k: np.stack([r[k] for r in results], 0)
    n = len(results)
    y = cat("y").reshape(n, 2048, 1024)
    ys = cat("ys").reshape(n * 16, 1, 1024)
    convp = np.transpose(cat("convp"), (1, 0, 2, 3))
    convs = np.transpose(cat("convs").reshape(n, 2, 16, 30, 1024), (1, 0, 2, 3, 4)).reshape(2, n * 16, 30, 1024)
    kp = cat("kp").reshape(n, 2048, 16, 64); vp = cat("vp").reshape(n, 2048, 16, 64); lfp = cat("lfp").reshape(n, 2048, 16)
    ks = cat("ks").reshape(n * 16, 1, 16, 64); vs = cat("vs").reshape(n * 16, 1, 16, 64); lfs = cat("lfs").reshape(n * 16, 1, 16)
    return tuple(np.ascontiguousarray(a, dtype=np.float32) for a in (y, ys, convp, convs, kp, vp, lfp, ks, vs, lfs))


def kernel(**inputs):
    NP = inputs["cache_k"].shape[0]
    nc = build(NP=NP)
    maps = make_in_maps(inputs, list(range(NCORES)), NP=NP)
    res = run_bass_kernel_spmd(nc, maps, core_ids=list(range(NCORES)))
    return assemble(res.results)


def phase_kv(K):
    nc, S, X, XS, C, PS, PSB = K.nc, K.S, K.X, K.XS, K.C, K.PS, K.PSB
    rX, rXS, rPS, rPSB, rC = K.rX, K.rXS, K.rPS, K.rPSB, K.rC
    dma, mm = K.dma, K.mm
    identb = C["identb"]
    PB = K.G

    def sbp(name, shape, dt):
        return PB.enter_context(nc.sbuf_tensor("pb_" + name, shape, dt))
    KT = sbp("KT", [128, 8, 2048], BF16); VA = sbp("VA", [128, 16, 1040], BF16)
    biasT = sbp("biasT", [128, 8, 16, 16], F32)
    ksn = sbp("ksn", [16, 1024], F32); vss = sbp("vss", [16, 1024], F32); lfs = sbp("lfs", [16, 16], F32)
    Cc = sbp("Cc", [128, 16, 16], F32); Rsb = sbp("Rsb", [128, 8, 16], F32)
    K.Cc, K.Rsb = Cc, Rsb
    K.KT, K.VA, K.biasT, K.ksn, K.vss, K.lfs = KT, VA, biasT, ksn, vss, lfs
    K.r_KT, K.r_VA, K.r_bias, K.r_ksn, K.r_vss, K.r_lfs = Res(), Res(), Res(), Res(), Res(), Res()
    with ExitStack() as P:
        def sb(name, shape, dt):
            return P.enter_context(nc.sbuf_tensor("kv_" + name, shape, dt))
        wkv = sb("wkv", [128, 8, 2048], BF16); wkf = sb("wkf", [128, 8, 16], BF16)
        gb = sb("gb", [128, 1024], F32); kg = sb("kg", [128, 64], F32); fbb = sb("fbb", [128, 16], F32)
        xn = sb("xn", [128, 1024], BF16); xnT = sb("xnT", [128, 8, 128], BF16)
        ofull = sb("ofull", [128, 1024], F32); sqt = sb("sqt", [128, 512], F32); kb16 = sb("kb16", [128, 1024], BF16)
        ss8 = sb("ss8", [128, 8], F32); lt = [sb("lt%d" % i, [128, 16], F32) for i in range(4)]
        lfa = sb("lfa", [128, 16, 16], F32); carry = sb("carry", [128, 16, 16], F32)
        NT = K.norm_tmp(P, "k")
        r_w, r_gb, r_xn, r_xnT, r_of, r_sq, r_kb, r_ss8, r_lt, r_lfa, r_carry, r_Cc, r_R = [Res() for _ in range(13)]
        ch_w = S.chan("kvw"); ch_o = S.chan("kvo")
        for u in range(4):
            dma(wkv[:, :, u * 512:(u + 1) * 512], K.s_kvw[u].rearrange("p (c n) -> p c n", c=8), [], [r_w], ch_w)
        dma(wkf[:], K.s_kvf[:, :].rearrange("p (c n) -> p c n", c=8), [], [r_w], ch_w)
        dma(gb[:], brow(K.kv_norm, 0, 1024, 128), [], [r_gb], ch_w)
        dma(kg[:], brow(K.k_norm, 0, 64, 128), [], [r_gb], ch_w)
        dma(fbb[:], brow(K.kv_fb, 0, 16, 128), [], [r_gb], ch_w)
        S.op("pool", I("memset", VA[:].rearrange("p i (h e) -> p (i h) e", e=65)[:, :, 64:65], 1.0), [], [K.r_VA])
        bank = [0]

        def nb():
            bank[0] = (bank[0] + 1) % 4
            return bank[0]
        kpv = K.kp_d.rearrange("(i p) d -> p i d", p=128); vpv = K.vp_d.rearrange("(i p) d -> p i d", p=128)
        lfv = K.lfp_d.rearrange("(i p) d -> p i d", p=128)
        for i in range(17):
            samp = i == 16
            Pn = 16 if samp else 128
            src, rs = (XS[:, :], rXS) if samp else (X[:, i, :], rX[i])
            rstd = K.rmsnorm_rstd(NT, [(src, rs)], Pn, 1, None, None, "k")
            S.op("dve", I("scalar_tensor_tensor", xn[:Pn, :], src, rstd[:Pn, 0:1], gb[:Pn, :], ALU.mult, ALU.mult),
                 [rs, NT["rrstd"], r_gb], [r_xn])
            S.op("pe", [I("transpose", PSB[:, c * Pn:(c + 1) * Pn], xn[:Pn, c * 128:(c + 1) * 128], identb[0:Pn, 0:Pn])
                        for c in range(8)], [r_xn, rC], [rPSB])
            S.op("act", I("copy", xnT[:, :, 0:Pn], PSB[:, 0:8 * Pn].rearrange("p (c n) -> p c n", c=8)), [rPSB], [r_xnT])
            for u in range(4):
                bk = nb()
                mm(PS[bk][:Pn, :], [(xnT[:, c, 0:Pn], wkv[:, c, u * 512:(u + 1) * 512]) for c in range(8)], [r_xnT, r_w], [rPS[bk]])
                osl = ofull[:Pn, (u % 2) * 512:(u % 2 + 1) * 512]
                if u < 2:
                    S.op("act", I("activation", sqt[:Pn, :], PS[bk][:Pn, :], AF.Square), [rPS[bk]], [r_sq])
                    S.op("dve", I("tensor_reduce", ss8[:Pn, :], sqt[:Pn, :].rearrange("p (h d) -> p h d", d=64), AX.X, ALU.add),
                         [r_sq], [r_ss8])
                    S.op("dve", I("tensor_scalar", ss8[:Pn, :], ss8[:Pn, :], 1.0 / 64.0, EPS, ALU.mult, ALU.add), [r_ss8], [r_ss8])
                    S.op("act", I("activation", ss8[:Pn, :], ss8[:Pn, :], AF.Sqrt), [r_ss8], [r_ss8])
                    S.op("dve", I("reciprocal", ss8[:Pn, :], ss8[:Pn, :]), [r_ss8], [r_ss8])
                    o3 = osl.rearrange("p (h d) -> p h d", d=64)
                    S.op("dve", I("tensor_tensor", o3, PS[bk][:Pn, :].rearrange("p (h d) -> p h d", d=64),
                                  ss8[:Pn, :].unsqueeze(2).to_broadcast([Pn, 8, 64]), ALU.mult), [rPS[bk], r_ss8], [r_of])
                    S.op("dve", I("tensor_tensor", o3, o3, kg[:Pn, :].unsqueeze(1).to_broadcast([Pn, 8, 64]), ALU.mult),
                         [r_of, r_gb], [r_of])
                    if u == 1:
                        if samp:
                            dma(K.ks_d, ofull[:16, :], [r_of], [], ch_o)
                            S.op("act", I("copy", ksn[:, :], ofull[:16, :]), [r_of], [K.r_ksn])
                        else:
                            dma(kpv[:, i, :], ofull[:, :], [r_of], [], ch_o)
                            S.op("pool", I("tensor_copy", kb16[:, :], ofull[:, :]), [r_of], [r_kb])
                            S.op("pe", [I("transpose", PSB[:, c * 128:(c + 1) * 128], kb16[:, c * 128:(c + 1) * 128], identb[:, :])
                                        for c in range(8)], [r_kb, rC], [rPSB])
                            S.op("act", I("copy", KT[:, :, i * 128:(i + 1) * 128], PSB[:, :].rearrange("p (c n) -> p c n", c=8)),
                                 [rPSB], [K.r_KT])
                else:
                    S.op("act", I("copy", osl, PS[bk][:Pn, :]), [rPS[bk]], [r_of])
                    if not samp:
                        va = VA[:, i, :].rearrange("p (h e) -> p h e", e=65)[:, 8 * (u - 2):8 * (u - 2) + 8, 0:64]
                        S.op("pool", I("tensor_copy", va, osl.rearrange("p (h d) -> p h d", d=64)), [r_of], [K.r_VA])
                    if u == 3:
                        if samp:
                            dma(K.vs_d, ofull[:16, :], [r_of], [], ch_o)
                            S.op("act", I("copy", vss[:, :], ofull[:16, :]), [r_of], [K.r_vss])
                        else:
                            dma(vpv[:, i, :], ofull[:, :], [r_of], [], ch_o)
            bk = nb()
            mm(PS[bk][:Pn, 0:16], [(xnT[:, c, 0:Pn], wkf[:, c, :]) for c in range(8)], [r_xnT, r_w], [rPS[bk]])
            u_, a_, e_, m_ = [t[:Pn, :] for t in lt]
            S.op("dve", I("tensor_tensor", u_, PS[bk][:Pn, 0:16], fbb[:Pn, :], ALU.add), [rPS[bk], r_gb], [r_lt])
            S.op("act", I("activation", a_, u_, AF.Abs), [r_lt], [r_lt])
            S.op("act", I("activation", e_, a_, AF.Exp, scale=-1.0), [r_lt], [r_lt])
            S.op("act", I("activation", e_, e_, AF.Ln, bias=K.onec[:Pn, 0:1]), [r_lt, rC], [r_lt])
            S.op("dve", I("tensor_scalar_min", m_, u_, 0.0), [r_lt], [r_lt])
            if samp:
                S.op("dve", I("tensor_tensor", lfs[:, :], m_, e_, ALU.subtract), [r_lt], [K.r_lfs])
                dma(K.lfs_d, lfs[:, :], [K.r_lfs], [], ch_o)
            else:
                S.op("dve", I("tensor_tensor", lfa[:, i, :], m_, e_, ALU.subtract), [r_lt], [r_lfa])
        dma(lfv, lfa[:, :, :], [r_lfa], [], ch_o)
        lff = lfa[:].rearrange("p i h -> p (i h)")
        mm(PS[5][:, 0:256], [(C["lincl"][:, :], lff)], [r_lfa, rC], [rPS[5]])
        mm(PS[4][:, 0:256], [(C["ones1"][:, :], lff)], [r_lfa, rC], [rPS[4]])
        S.op("dve", I("memset", carry[:, 0, :], 0.0), [], [r_carry])
        for i in range(1, 16):
            S.op("dve", I("tensor_tensor", carry[:, i, :], carry[:, i - 1, :], PS[4][:, (i - 1) * 16:i * 16], ALU.add),
                 [r_carry, rPS[4]], [r_carry])
        S.op("dve", I("tensor_tensor", Cc[:].rearrange("p i h -> p (i h)"), PS[5][:, 0:256], carry[:].rearrange("p i h -> p (i h)"),
                      ALU.add), [rPS[5], r_carry], [r_Cc])
        S.op("pe", [I("matmul", PS[6][:, m * 16:(m + 1) * 16], C["rs0"][:, :], Cc[:, 2 * m + 1, :], start=True, stop=True)
                    for m in range(8)], [r_Cc, rC], [rPS[6]])
        S.op("act", I("copy", Rsb[:].rearrange("p b h -> p (b h)"), PS[6][:, 0:128]), [rPS[6]], [r_R])
        for m in range(8):
            nj = 2 * m + 2
            S.op("dve", I("tensor_tensor", biasT[:, m, 0:nj, :], Rsb[:, m, :].unsqueeze(1).to_broadcast([128, nj, 16]),
                          Cc[:, 0:nj, :], ALU.subtract), [r_R, r_Cc], [K.r_bias])
        S.barrier()


def phase_b(K):
    nc, S, X, XS, C, PS, PSB = K.nc, K.S, K.X, K.XS, K.C, K.PS, K.PSB
    rX, rXS, rPS, rPSB, rC = K.rX, K.rXS, K.rPS, K.rPSB, K.rC
    dma, mm = K.dma, K.mm
    identb, tri = C["identb"], C["tri"]
    KT, VA, biasT = K.KT, K.VA, K.biasT
    for l in range(2):
        with ExitStack() as P:
            def sb(name, shape, dt):
                return P.enter_context(nc.sbuf_tensor("b%d_" % l + name, shape, dt))
            wb = [sb("wb%d" % i, [128, 8, 256], BF16) for i in range(2)]
            gb = sb("gb", [128, 1024], F32); qg = sb("qg", [128, 64], F32)
            xn = sb("xn", [128, 1024], BF16); xnT = sb("xnT", [128, 8, 512], BF16); QT = sb("QT", [128, 8, 512], BF16)
            qn = sb("qn", [128, 256], BF16); szb = sb("szb", [128, 4, 1024], BF16); g = sb("g", [128, 4, 1024], BF16)
            pt = [sb("pt%d" % i, [128, 256], BF16) for i in range(4)]
            th = sb("th", [128, 256], F32); sqt = sb("sqt", [128, 256], F32); ss4 = sb("ss4", [128, 4], F32)
            rec = sb("rec", [128, 4], F32)
            NT = K.norm_tmp(P, "b%d" % l)
            r_wb = [Res(), Res()]; r_gb, r_xn, r_xnT, r_QT, r_qn, r_g, r_th, r_sq, r_ss4, r_rec = [Res() for _ in range(10)]
            r_sz = [Res() for _ in range(4)]; r_pt = [Res() for _ in range(4)]
            ch_w = [S.chan("bw%d_%d" % (l, i)) for i in range(2)]; ch_p = S.chan("bp%d" % l)
            cnt = {"w": 0, "s": 0, "p": 0, "pt": 0}

            def nxt(key, n):
                v = cnt[key] % n
                cnt[key] += 1
                return v
            dma(gb[:], brow(K.b_norm, l, 1024, 128), [], [r_gb], ch_p)
            dma(qg[:], brow(K.q_norm, l, 64, 128), [], [r_gb], ch_p)
            for b in range(4):
                srcs = [(X[:, 4 * b + t, :], rX[4 * b + t]) for t in range(4)]
                rstd = K.rmsnorm_rstd(NT, srcs, 128, 4, None, None, "b")
                for t, (ap, r) in enumerate(srcs):
                    S.op("dve", I("scalar_tensor_tensor", xn[:, :], ap, rstd[:, t:t + 1], gb[:, :], ALU.mult, ALU.mult),
                         [r, NT["rrstd"], r_gb], [r_xn])
                    S.op("pe", [I("transpose", PSB[:, c * 128:(c + 1) * 128], xn[:, c * 128:(c + 1) * 128], identb[:, :])
                                for c in range(8)], [r_xn, rC], [rPSB])
                    S.op("act", I("copy", xnT[:, :, t * 128:(t + 1) * 128], PSB[:, :].rearrange("p (c n) -> p c n", c=8)),
                         [rPSB], [r_xnT])
                for v in range(8):
                    ws = nxt("w", 2)
                    dma(wb[ws][:], K.s_bwin[l, v // 2].rearrange("p (c n) -> p c n", c=8)[:, :, (v % 2) * 256:(v % 2 + 1) * 256],
                        [], [r_wb[ws]], ch_w[ws])
                    for t in range(4):
                        bk = 5 + nxt("p", 2)
                        mm(PS[bk][:, 0:256], [(xnT[:, c, t * 128:(t + 1) * 128], wb[ws][:, c, :]) for c in range(8)],
                           [r_xnT, r_wb[ws]], [rPS[bk]])
                        ps = PS[bk][:, 0:256]
                        if v < 4:
                            S.op("act", I("activation", sqt[:, :], ps, AF.Square), [rPS[bk]], [r_sq])
                            S.op("dve", I("tensor_reduce", ss4[:, :], sqt[:, :].rearrange("p (h d) -> p h d", d=64), AX.X, ALU.add),
                                 [r_sq], [r_ss4])
                            S.op("dve", I("tensor_scalar", ss4[:, :], ss4[:, :], 1.0 / 64.0, EPS, ALU.mult, ALU.add), [r_ss4], [r_ss4])
                            S.op("act", I("activation", ss4[:, :], ss4[:, :], AF.Sqrt), [r_ss4], [r_ss4])
                            S.op("dve", I("reciprocal", ss4[:, :], ss4[:, :]), [r_ss4], [r_ss4])
                            S.op("dve", I("tensor_tensor", th[:, :].rearrange("p (h d) -> p h d", d=64),
                                          ps.rearrange("p (h d) -> p h d", d=64),
                                          ss4[:, :].unsqueeze(2).to_broadcast([128, 4, 64]), ALU.mult), [rPS[bk], r_ss4], [r_th])
                            S.op("dve", I("tensor_tensor", qn[:, :].rearrange("p (h d) -> p h d", d=64),
                                          th[:, :].rearrange("p (h d) -> p h d", d=64),
                                          qg[:, :].unsqueeze(1).to_broadcast([128, 4, 64]), ALU.mult), [r_th, r_gb], [r_qn])
                            S.op("pe", [I("transpose", PSB[:, j * 128:(j + 1) * 128], qn[:, j * 128:(j + 1) * 128], identb[:, :])
                                        for j in range(2)], [r_qn, rC], [rPSB])
                            S.op("act", I("copy", QT[:, 2 * v:2 * v + 2, t * 128:(t + 1) * 128],
                                          PSB[:, 0:256].rearrange("p (c n) -> p c n", c=2)), [rPSB], [r_QT])
                        else:
                            S.op("act", I("activation", th[:, :], ps, AF.Tanh), [rPS[bk]], [r_th])
                            S.op("dve", I("scalar_tensor_tensor", szb[:, t, (v - 4) * 256:(v - 3) * 256], th[:, :], 1.0, ps,
                                          ALU.add, ALU.mult), [r_th, rPS[bk]], [r_sz[t]])
                if l == 0 and b == 0:
                    K.dbg("bias", biasT[:, 0, :, :].rearrange("p j h -> p (j h)"), [K.r_bias])
                    K.dbg("qt", QT[:, 0, 0:256], [r_QT])
                    K.dbg("kt", KT[:, 0, 0:256], [K.r_KT])
                    K.dbg("va", VA[:, 0, 0:256], [K.r_VA])
                    K.dbg("szb", szb[:, 0, 0:256], r_sz)
                steps = []
                for h in range(16):
                    for q2 in range(2):
                        m = 2 * b + q2
                        for j in range(2 * m + 2):
                            steps.append(dict(h=h, q2=q2, m=m, j=j, n0=max(0, j - 2 * m) * 128, last=(q2 == 1 and j == 2 * m + 1)))
                LA = 2

                def emit_front(st):
                    h, q2, m, j, n0 = st["h"], st["q2"], st["m"], st["j"], st["n0"]
                    c, po, q0 = h // 2, 64 * (h % 2), q2 * 256
                    bk = nxt("s", 3)
                    mm(PS[bk][:, n0:256], [(KT[po:po + 64, c, j * 128:(j + 1) * 128], QT[po:po + 64, c, q0 + n0:q0 + 256])],
                       [K.r_KT, r_QT], [rPS[bk]])
                    k = nxt("pt", 4)
                    st["k"] = k
                    S.op("act", I("activation", pt[k][:, n0:256], PS[bk][:, n0:256], AF.Exp,
                                  bias=biasT[:, m, j, h:h + 1], scale=0.125), [rPS[bk], K.r_bias], [r_pt[k]])
                    if j >= 2 * m:
                        S.op("pool", I("tensor_tensor", pt[k][:, n0:n0 + 128], pt[k][:, n0:n0 + 128], tri[:, :], ALU.mult),
                             [r_pt[k], rC], [r_pt[k]])

                def emit_back(st):
                    h, q2, m, j, n0, k = st["h"], st["q2"], st["m"], st["j"], st["n0"], st["k"]
                    ob = 3 + h % 2
                    S.op("pe", [I("matmul", PS[ob][:, (2 * q2 + tl) * 65:(2 * q2 + tl + 1) * 65], pt[k][:, tl * 128:(tl + 1) * 128],
                                  VA[:, j, h * 65:(h + 1) * 65], start=(j == 0 and q2 == 0 and tl == 0), stop=(j == 2 * m + tl))
                                for tl in range(n0 // 128, 2)], [r_pt[k], K.r_VA], [rPS[ob]])
                    if st["last"]:
                        o3 = PS[ob][:, 0:260].rearrange("p (t e) -> p t e", e=65)
                        S.op("dve", I("reciprocal", rec[:, :], o3[:, :, 64]), [rPS[ob]], [r_rec])
                        for tq in range(4):
                            S.op("dve", I("scalar_tensor_tensor", g[:, tq, h * 64:(h + 1) * 64], o3[:, tq, 0:64], rec[:, tq:tq + 1],
                                          szb[:, tq, h * 64:(h + 1) * 64], ALU.mult, ALU.mult), [rPS[ob], r_rec, r_sz[tq]], [r_g])

                for i in range(len(steps) + LA):
                    if i < len(steps):
                        emit_front(steps[i])
                    if i - LA >= 0:
                        emit_back(steps[i - LA])
                if l == 0 and b == 0:
                    K.dbg("g", g[:, 0, 0:256], [r_g])
                for t in range(4):
                    S.op("pe", [I("transpose", PSB[:, c * 128:(c + 1) * 128], g[:, t, c * 128:(c + 1) * 128], identb[:, :])
                                for c in range(8)], [r_g, rC], [rPSB])
                    S.op("act", I("copy", xnT[:, :, t * 128:(t + 1) * 128], PSB[:, :].rearrange("p (c n) -> p c n", c=8)),
                         [rPSB], [r_xnT])
                for v in range(4):
                    ws = nxt("w", 2)
                    dma(wb[ws][:], K.s_bwout[l, v // 2].rearrange("p (c n) -> p c n", c=8)[:, :, (v % 2) * 256:(v % 2 + 1) * 256],
                        [], [r_wb[ws]], ch_w[ws])
                    for t in range(4):
                        i = 4 * b + t
                        bk = 5 + nxt("p", 2)
                        mm(PS[bk][:, 0:256], [(xnT[:, c, t * 128:(t + 1) * 128], wb[ws][:, c, :]) for c in range(8)],
                           [r_xnT, r_wb[ws]], [rPS[bk]])
                        x_ = X[:, i, v * 256:(v + 1) * 256]
                        S.op("dve", I("tensor_tensor", x_, x_, PS[bk][:, 0:256], ALU.add), [rX[i], rPS[bk]], [rX[i]])
            S.barrier()
        if K.attn:
            phase_b_sample(K, l)


def phase_b_sample(K, l):
    nc, S, XS, C, PS, PSB = K.nc, K.S, K.XS, K.C, K.PS, K.PSB
    rXS, rPS, rPSB, rC = K.rXS, K.rPS, K.rPSB, K.rC
    dma, mm = K.dma, K.mm
    identb, identf, ones1, ustr = C["identb"], C["identf"], C["ones1"], C["ustr"]
    ksn, vss, lfs = K.ksn, K.vss, K.lfs
    cnt = {"w": 0, "p": 0, "k": 0, "v": 0}

    def nxt(key, n):
        v = cnt[key] % n
        cnt[key] += 1
        return v
    with ExitStack() as P:
        def sb(name, shape, dt, es=P):
            return es.enter_context(nc.sbuf_tensor("s%d_" % l + name, shape, dt))
        ec = sb("ec", [16, 256], F32); bmask = sb("bmask", [16, 1024], F32)
        xnT = sb("xnT", [128, 8, 16], BF16)
        qs = sb("qs", [16, 1024], F32); szs = sb("szs", [16, 1024], F32)
        lfsb = sb("lfsb", [128, 256], F32); dacc = sb("dacc", [16, 16], F32); pself = sb("pself", [16, 16], F32)
        masked = sb("masked", [16, 1024], F32); idx = sb("idx", [128, 256], I32)
        r_ec, r_xnT, r_qs, r_szs, r_lfsb, r_dacc, r_pself, r_masked, r_idx = [Res() for _ in range(9)]
        ch_c = S.chan("sc%d" % l); ch_w = [S.chan("sw%d_%d" % (l, i)) for i in range(2)]
        ch_k = [S.chan("sk%d_%d" % (l, i)) for i in range(4)]; ch_v = [S.chan("sv%d_%d" % (l, i)) for i in range(4)]
        ch_l = S.chan("sl%d" % l)
        dma(ec[:], K.cd["ec"], [], [r_ec], ch_c); dma(bmask[:], K.cd["bmask"], [], [r_ec], ch_c)
        with ExitStack() as T:
            wb = [sb("wb%d" % i, [128, 8, 256], BF16, T) for i in range(2)]
            gb = sb("gb", [16, 1024], F32, T); qg = sb("qg", [16, 64], F32, T)
            xn = sb("xn", [16, 1024], BF16, T)
            ptb = sb("ptb", [128, 256], I32, T); ptf = sb("ptf", [128, 256], F32, T)
            th = sb("th", [16, 256], F32, T); sqt = sb("sqt", [16, 256], F32, T); ss4 = sb("ss4", [16, 4], F32, T)
            NT = K.norm_tmp(T, "s%d" % l)
            r_wb = [Res(), Res()]; r_gb, r_qg, r_xn, r_th, r_sq, r_ss4 = [Res() for _ in range(6)]
            dma(gb[:], brow(K.b_norm, l, 1024, 16), [], [r_gb], ch_c); dma(qg[:], brow(K.q_norm, l, 64, 16), [], [r_qg], ch_c)
            dma(ptb[:], bass.AP(K.pt_d.tensor, 0, [[0, 128], [1, 256]]), [], [r_idx], ch_c)
            S.op("dve", I("tensor_copy", ptf[:, :], ptb[:, :]), [r_idx], [r_idx])
            S.op("dve", I("tensor_scalar", ptf[:, :], ptf[:, :], 128.0, C["pidx"][:, 0:1], ALU.mult, ALU.add), [r_idx, rC], [r_idx])
            S.op("dve", I("tensor_copy", idx[:, :], ptf[:, :]), [r_idx], [r_idx])
            S.op("dve", I("memset", dacc[:, :], 0.0), [], [r_dacc])
            rstd = K.rmsnorm_rstd(NT, [(XS[:, :], rXS)], 16, 1, None, None, "s")
            S.op("dve", I("scalar_tensor_tensor", xn[:, :], XS[:, :], rstd[:16, 0:1], gb[:, :], ALU.mult, ALU.mult),
                 [rXS, NT["rrstd"], r_gb], [r_xn])
            S.op("pe", [I("transpose", PSB[:, c * 16:(c + 1) * 16], xn[:, c * 128:(c + 1) * 128], identb[0:16, 0:16]) for c in range(8)],
                 [r_xn, rC], [rPSB])
            S.op("act", I("copy", xnT[:, :, :], PSB[:, 0:128].rearrange("p (c n) -> p c n", c=8)), [rPSB], [r_xnT])
            for v in range(8):
                ws = nxt("w", 2)
                dma(wb[ws][:], K.s_bwin[l, v // 2].rearrange("p (c n) -> p c n", c=8)[:, :, (v % 2) * 256:(v % 2 + 1) * 256],
                    [], [r_wb[ws]], ch_w[ws])
                bk = 5 + nxt("p", 2)
                mm(PS[bk][:16, 0:256], [(xnT[:, c, :], wb[ws][:, c, :]) for c in range(8)], [r_xnT, r_wb[ws]], [rPS[bk]])
                ps = PS[bk][:16, 0:256]
                if v < 4:
                    S.op("act", I("activation", sqt[:, :], ps, AF.Square), [rPS[bk]], [r_sq])
                    S.op("dve", I("tensor_reduce", ss4[:, :], sqt[:, :].rearrange("p (h d) -> p h d", d=64), AX.X, ALU.add), [r_sq], [r_ss4])
                    S.op("dve", I("tensor_scalar", ss4[:, :], ss4[:, :], 1.0 / 64.0, EPS, ALU.mult, ALU.add), [r_ss4], [r_ss4])
                    S.op("act", I("activation", ss4[:, :], ss4[:, :], AF.Sqrt), [r_ss4], [r_ss4])
                    S.op("dve", I("reciprocal", ss4[:, :], ss4[:, :]), [r_ss4], [r_ss4])
                    S.op("dve", I("tensor_tensor", th[:, :].rearrange("p (h d) -> p h d", d=64), ps.rearrange("p (h d) -> p h d", d=64),
                                  ss4[:, :].unsqueeze(2).to_broadcast([16, 4, 64]), ALU.mult), [rPS[bk], r_ss4], [r_th])
                    S.op("dve", I("tensor_tensor", qs[:, v * 256:(v + 1) * 256].rearrange("p (h d) -> p h d", d=64),
                                  th[:, :].rearrange("p (h d) -> p h d", d=64),
                                  qg[:, :].unsqueeze(1).to_broadcast([16, 4, 64]), ALU.mult), [r_th, r_qg], [r_qs])
                else:
                    S.op("act", I("activation", th[:, :], ps, AF.Tanh), [rPS[bk]], [r_th])
                    S.op("dve", I("scalar_tensor_tensor", szs[:, (v - 4) * 256:(v - 3) * 256], th[:, :], 1.0, ps, ALU.add, ALU.mult),
                         [r_th, rPS[bk]], [r_szs])
            S.op("dve", I("tensor_tensor", masked[:, :], qs[:, :], ksn[:, :], ALU.mult), [r_qs, K.r_ksn], [r_masked])
            S.op("dve", I("tensor_reduce", pself[:, :], masked[:, :].rearrange("p (h d) -> p h d", d=64), AX.X, ALU.add),
                 [r_masked], [r_pself])
            S.op("act", I("activation", pself[:, :], pself[:, :], AF.Exp, scale=0.125), [r_pself], [r_pself])
            S.op("pe", [I("matmul", PS[6][:, s * 16:(s + 1) * 16], identf[0:16, s:s + 1].to_broadcast([16, 128]), lfs[:, :],
                          start=True, stop=True) for s in range(16)], [K.r_lfs, rC], [rPS[6]])
            S.op("act", I("copy", lfsb[:, :], PS[6][:, 0:256]), [rPS[6]], [r_lfsb])
            S.barrier()
        with ExitStack() as T:
            qb = sb("qb", [128, 1024], F32, T)
            kb = [sb("kb%d" % i, [128, 1024], F32, T) for i in range(4)]
            vb16 = [sb("vb16_%d" % i, [128, 1024], BF16, T) for i in range(4)]
            sc = sb("sc", [128, 256], F32, T); bias_s = sb("bias_s", [128, 256], F32, T)
            p16 = [sb("p16_%d" % i, [128, 256], BF16, T) for i in range(2)]
            lfpg = sb("lfpg", [128, 256], F32, T); suf = sb("suf", [128, 256], F32, T)
            psumh = sb("psumh", [128, 16], F32, T); den_sb = [sb("den%d" % i, [16, 1], F32, T) for i in range(2)]
            dmask = sb("dmask", [16, 16], F32, T)
            r_qb, r_sc, r_bias, r_lfpg, r_suf, r_psumh, r_dmask = [Res() for _ in range(7)]
            r_lfpgs = [Res() for _ in range(16)]
            r_kb = [Res() for _ in range(4)]; r_vb16 = [Res() for _ in range(4)]
            r_p16 = [Res(), Res()]; r_den = [Res(), Res()]
            ck2, cv2, cl2 = K.ck_d, K.cv_d, K.cl_d

            def k_pass(s):
                pbuf = s % 2
                for hf in range(2):
                    mm(PS[hf][:, :], [(identf[0:16, s:s + 1].to_broadcast([16, 128]), qs[:, hf * 512:(hf + 1) * 512])],
                       [r_qs, rC], [rPS[hf]])
                    S.op("act", I("copy", qb[:, hf * 512:(hf + 1) * 512], PS[hf][:, :]), [rPS[hf]], [r_qb])
                if l == 0:
                    for pg in range(16):
                        S.op("pool", I("indirect_dma_start", out=lfpg[:, pg * 16:(pg + 1) * 16], out_offset=None, in_=cl2,
                                       in_offset=bass.IndirectOffsetOnAxis(ap=idx[:, s * 16 + pg:s * 16 + pg + 1], axis=0)),
                             [r_idx], [r_lfpgs[pg]], chan=ch_l)
                    yield
                    mm(PS[6][:, 0:256], [(ustr[:, :], lfpg[:, :])], r_lfpgs + [rC], [rPS[6]])
                    mm(PS[6][:, 256:512], [(ones1[:, :], lfpg[:, :])], r_lfpgs + [rC], [rPS[6]])
                    S.op("dve", I("tensor_copy", suf[:, 240:256], lfsb[:, s * 16:(s + 1) * 16]), [r_lfsb], [r_suf])
                    for pg in range(14, -1, -1):
                        S.op("dve", I("tensor_tensor", suf[:, pg * 16:(pg + 1) * 16], suf[:, (pg + 1) * 16:(pg + 2) * 16],
                                      PS[6][:, 256 + (pg + 1) * 16:256 + (pg + 2) * 16], ALU.add), [r_suf, rPS[6]], [r_suf])
                    S.op("dve", I("tensor_tensor", bias_s[:, :], PS[6][:, 0:256], suf[:, :], ALU.add), [rPS[6], r_suf], [r_bias])
                    dma(K.s_bias[s], bias_s[:, :], [r_bias], [K.r_sbias[s]], K.ch_sb[0])
                else:
                    dma(bias_s[:, :], K.s_bias[s], [K.r_sbias[s]], [r_bias], K.ch_sb[1])
                yield
                for pg in range(16):
                    k = nxt("k", 4)
                    S.op("pool", I("indirect_dma_start", out=kb[k][:, :], out_offset=None, in_=ck2,
                                   in_offset=bass.IndirectOffsetOnAxis(ap=idx[:, s * 16 + pg:s * 16 + pg + 1], axis=0)),
                         [r_idx], [r_kb[k]], chan=ch_k[k])
                    S.op("dve", I("tensor_tensor", kb[k][:, :], kb[k][:, :], qb[:, :], ALU.mult), [r_kb[k], r_qb], [r_kb[k]])
                    S.op("dve", I("tensor_reduce", sc[:, pg * 16:(pg + 1) * 16], kb[k][:, :].rearrange("p (h d) -> p h d", d=64),
                                  AX.X, ALU.add), [r_kb[k]], [r_sc])
                    yield
                S.op("dve", I("scalar_tensor_tensor", sc[:, :], sc[:, :], 0.125, bias_s[:, :], ALU.mult, ALU.add), [r_sc, r_bias], [r_sc])
                S.op("act", I("activation", p16[pbuf][:, :], sc[:, :], AF.Exp), [r_sc], [r_p16[pbuf]])
                S.op("dve", I("tensor_reduce", psumh[:, :], p16[pbuf][:, :].rearrange("p (g h) -> p h g", h=16), AX.X, ALU.add),
                     [r_p16[pbuf]], [r_psumh])
                mm(PS[6][:16, 0:1], [(psumh[:, :], ones1[:, 0:1])], [r_psumh, rC], [rPS[6]])
                S.op("act", I("copy", den_sb[pbuf][:, :], PS[6][:16, 0:1]), [rPS[6]], [r_den[pbuf]])
                yield

            def v_pass(s):
                pbuf = s % 2
                for pg in range(16):
                    k = nxt("v", 4)
                    S.op("pool", I("indirect_dma_start", out=vb16[k][:, :], out_offset=None, in_=cv2,
                                   in_offset=bass.IndirectOffsetOnAxis(ap=idx[:, s * 16 + pg:s * 16 + pg + 1], axis=0)),
                         [r_idx], [r_vb16[k]], chan=ch_v[k])
                    for hf in range(2):
                        S.op("pe", I("matmul", PS[2 + hf][:16, :], p16[pbuf][:, pg * 16:(pg + 1) * 16], vb16[k][:, hf * 512:(hf + 1) * 512],
                                     start=(pg == 0), stop=(pg == 15)), [r_p16[pbuf], r_vb16[k]], [rPS[2 + hf]])
                    yield
                for hf in range(2):
                    S.op("dve", I("tensor_tensor", masked[:, hf * 512:(hf + 1) * 512], PS[2 + hf][:16, :], bmask[:, hf * 512:(hf + 1) * 512],
                                  ALU.mult), [rPS[2 + hf], r_ec], [r_masked])
                    S.op("pe", I("matmul", PS[4 + hf][:16, :], ec[:, s * 16:(s + 1) * 16], masked[:, hf * 512:(hf + 1) * 512],
                                 start=(s == 0), stop=(s == 15)), [r_masked, r_ec], [rPS[4 + hf]])
                S.op("dve", I("tensor_scalar", dmask[:, :], identf[0:16, 0:16], den_sb[pbuf][:, 0:1], None, ALU.mult),
                     [r_den[pbuf], rC], [r_dmask])
                mm(PS[6][:16, 32:48], [(ec[:, s * 16:(s + 1) * 16], dmask[:, :])], [r_dmask, r_ec], [rPS[6]])
                S.op("dve", I("tensor_tensor", dacc[:, :], dacc[:, :], PS[6][:16, 32:48], ALU.add), [r_dacc, rPS[6]], [r_dacc])
                yield

            for s in range(17):
                gens = []
                if s < 16:
                    gens.append(k_pass(s))
                if s >= 1:
                    gens.append(v_pass(s - 1))
                while gens:
                    for g_ in list(gens):
                        try:
                            next(g_)
                        except StopIteration:
                            gens.remove(g_)
            S.barrier()
        with ExitStack() as T:
            wb = [sb("wc%d" % i, [128, 8, 256], BF16, T) for i in range(2)]
            xn = sb("xo", [16, 1024], BF16, T)
            r_wb = [Res(), Res()]; r_xn = Res()
            S.op("dve", I("tensor_tensor", masked[:, :].rearrange("p (h d) -> p h d", d=64), vss[:, :].rearrange("p (h d) -> p h d", d=64),
                          pself[:, :].unsqueeze(2).to_broadcast([16, 16, 64]), ALU.mult), [K.r_vss, r_pself], [r_masked])
            for hf in range(2):
                S.op("dve", I("tensor_tensor", masked[:, hf * 512:(hf + 1) * 512], masked[:, hf * 512:(hf + 1) * 512], PS[4 + hf][:16, :],
                              ALU.add), [r_masked, rPS[4 + hf]], [r_masked])
            S.op("dve", I("tensor_tensor", dacc[:, :], dacc[:, :], pself[:, :], ALU.add), [r_dacc, r_pself], [r_dacc])
            S.op("dve", I("reciprocal", dacc[:, :], dacc[:, :]), [r_dacc], [r_dacc])
            S.op("dve", I("tensor_tensor", masked[:, :].rearrange("p (h d) -> p h d", d=64), masked[:, :].rearrange("p (h d) -> p h d", d=64),
                          dacc[:, :].unsqueeze(2).to_broadcast([16, 16, 64]), ALU.mult), [r_masked, r_dacc], [r_masked])
            S.op("dve", I("tensor_tensor", xn[:, :], masked[:, :], szs[:, :], ALU.mult), [r_masked, r_szs], [r_xn])
            S.op("pe", [I("transpose", PSB[:, c * 16:(c + 1) * 16], xn[:, c * 128:(c + 1) * 128], identb[0:16, 0:16]) for c in range(8)],
                 [r_xn, rC], [rPSB])
            S.op("act", I("copy", xnT[:, :, :], PSB[:, 0:128].rearrange("p (c n) -> p c n", c=8)), [rPSB], [r_xnT])
            for v in range(4):
                ws = nxt("w", 2)
                dma(wb[ws][:], K.s_bwout[l, v // 2].rearrange("p (c n) -> p c n", c=8)[:, :, (v % 2) * 256:(v % 2 + 1) * 256],
                    [], [r_wb[ws]], ch_w[ws])
                bk = 5 + nxt("p", 2)
                mm(PS[bk][:16, 0:256], [(xnT[:, c, :], wb[ws][:, c, :]) for c in range(8)], [r_xnT, r_wb[ws]], [rPS[bk]])
                xs_ = XS[:, v * 256:(v + 1) * 256]
                S.op("dve", I("tensor_tensor", xs_, xs_, PS[bk][:16, 0:256], ALU.add), [rXS, rPS[bk]], [rXS])
            S.barrier()
```

```python
import numpy as np
import ml_dtypes
from contextlib import ExitStack
import concourse.bass as bass
import concourse.mybir as mybir
from concourse.bass_utils import run_bass_kernel_spmd

F32 = mybir.dt.float32
BF16 = mybir.dt.bfloat16
I32 = mybir.dt.int32
ALU = mybir.AluOpType
AF = mybir.ActivationFunctionType
AX = mybir.AxisListType
EPS = 1e-6
NCORES = 8


def I(name, *a, **kw):
    return (name, a, kw)


class Res:
    __slots__ = ("name", "w", "r")

    def __init__(self, name=""):
        self.name = name
        self.w = None
        self.r = {}


class Chan:
    __slots__ = ("sem", "count", "name")

    def __init__(self, sem, name):
        self.sem = sem
        self.count = 0
        self.name = name


class Sched:
    ENG = ("pe", "act", "dve", "pool", "sp")
    CE = ("pe", "act", "dve", "pool")

    def __init__(self, nc, esem, free):
        self.nc = nc
        self.items = {e: [] for e in self.ENG}
        self.cnt = {e: 0 for e in self.ENG}
        self.waited = {e: {} for e in self.ENG}
        self.esem = esem
        self.free = list(free)
        self.chans = []

    def chan(self, name=""):
        c = Chan(self.free.pop(), name)
        self.chans.append(c)
        return c

    def op(self, eng, fn, reads=(), writes=(), chan=None):
        deps = {}

        def add(k, v):
            if deps.get(k, 0) < v:
                deps[k] = v
        for r in reads:
            if r.w is not None:
                add(*r.w)
        for w in writes:
            if w.w is not None:
                add(*w.w)
            for k, v in w.r.items():
                add(k, v)
        waits = []
        wd = self.waited[eng]
        for k, v in deps.items():
            if chan is None and k == eng and eng == "pe":
                continue
            if isinstance(k, Chan):
                v = k.count
            if wd.get(k, 0) >= v:
                continue
            wd[k] = v
            waits.append((k.sem if isinstance(k, Chan) else self.esem[k], v))
        if chan is not None:
            chan.count += 16
            ev = (chan, chan.count)
            inc = (chan.sem, 16)
        else:
            self.cnt[eng] += 1
            ev = (eng, self.cnt[eng])
            inc = (self.esem[eng], 1)
        self.items[eng].append((waits, fn, inc))
        for w in writes:
            w.w = ev
            w.r = {}
        for r in reads:
            if r.r.get(ev[0], 0) < ev[1]:
                r.r[ev[0]] = ev[1]
        return ev

    def barrier(self):
        for e in self.ENG:
            waits = []
            wd = self.waited[e]
            for c in self.chans:
                if c.count and wd.get(c, 0) < c.count:
                    wd[c] = c.count
                    waits.append((c.sem, c.count))
            for o in self.CE:
                if o != e and self.cnt[o] and wd.get(o, 0) < self.cnt[o]:
                    wd[o] = self.cnt[o]
                    waits.append((self.esem[o], self.cnt[o]))
            if waits:
                self.items[e].append((waits, None, None))

    def emit(self, block):
        items = self.items

        def runner(name):
            def run(e):
                for waits, fn, inc in items[name]:
                    for sem, val in waits:
                        e.wait_ge(sem, val)
                    if fn is not None:
                        calls = fn if isinstance(fn, list) else [fn]
                        for (nm, a, kw) in calls:
                            ins = getattr(e, nm)(*a, **kw)
                        ins.then_inc(inc[0], inc[1])
            return run
        block.tensor(runner("pe"))
        block.scalar(runner("act"))
        block.vector(runner("dve"))
        block.gpsimd(runner("pool"))
        block.sync(runner("sp"))


def make_consts():
    bf = ml_dtypes.bfloat16
    k = np.arange(128)
    c = {}
    c["identb"] = np.eye(128, dtype=np.float32).astype(bf)
    c["identf"] = np.eye(128, dtype=np.float32)
    c["tri"] = (k[None, :] >= k[:, None]).astype(np.float32).astype(bf)
    c["lincl"] = (k[:, None] <= k[None, :]).astype(np.float32)
    c["ustr"] = (k[:, None] > k[None, :]).astype(np.float32)
    c["onesdiv"] = np.full((128, 128), 1.0 / 1024.0, np.float32)
    c["ones1"] = np.ones((128, 128), np.float32)
    c["rs0"] = np.zeros((128, 128), np.float32)
    c["rs0"][0, :] = 1.0
    eb = np.zeros((16, 16, 128), np.float32)
    ec = np.zeros((16, 16, 16), np.float32)
    for s in range(16):
        eb[s, s, :] = 1.0
        ec[:, s, s] = 1.0
    c["eb"] = eb.reshape(16, 2048)
    c["ec"] = ec.reshape(16, 256)
    bm = np.zeros((16, 1024), np.float32)
    for h in range(16):
        bm[h, h * 64:(h + 1) * 64] = 1.0
    c["bmask"] = bm
    c["pidx"] = k.astype(np.float32)[:, None].copy()
    return c


CONST_SPECS = [("identb", [128, 128], BF16), ("identf", [128, 128], F32), ("tri", [128, 128], BF16),
               ("lincl", [128, 128], F32), ("ustr", [128, 128], F32), ("onesdiv", [128, 128], F32),
               ("ones1", [128, 128], F32), ("rs0", [128, 128], F32), ("eb", [16, 2048], F32),
               ("ec", [16, 256], F32), ("bmask", [16, 1024], F32), ("pidx", [128, 1], F32)]


def build(NP=2560, upto=99, attn=True):
    nc = bass.Bass("TRN2", target_bir_lowering=False)

    def din(name, shape, dt=F32):
        return nc.dram_tensor(name, shape, dt, kind="ExternalInput").ap()

    def dout(name, shape, dt=F32):
        return nc.dram_tensor(name, shape, dt, kind="ExternalOutput").ap()

    def dscr(name, shape, dt=BF16):
        return nc.dram_tensor(name, shape, dt, kind="Internal").ap()

    x_d = din("x", [2048, 1024]); xs_d = din("xs", [16, 1024]); st_d = din("state", [2, 480, 1024])
    if attn:
        ck_d = din("ck", [NP * 128, 1024]); cv_d = din("cv", [NP * 128, 1024]); cl_d = din("cl", [NP * 128, 16])
        pt_d = din("pt", [1, 256], I32)
    a_norm = din("a_norm", [2, 1024]); a_w_in = din("a_w_in", [2, 1024, 3072]); a_cw = din("a_cw", [2, 31, 1024])
    a_cb = din("a_cb", [2, 1024]); a_lg = din("a_lg", [2, 1024]); a_lb = din("a_lb", [2, 1024])
    a_w_out = din("a_w_out", [2, 1024, 1024]); kv_norm = din("kv_norm", [1, 1024]); kv_w = din("kv_w", [1024, 2064])
    kv_fb = din("kv_fb", [1, 16]); k_norm = din("k_norm", [1, 64]); b_norm = din("b_norm", [2, 1024])
    b_w_in = din("b_w_in", [2, 1024, 2048]); q_norm = din("q_norm", [2, 64]); b_w_out = din("b_w_out", [2, 1024, 1024])
    cd = {n: din("c_" + n, s, dt) for n, s, dt in CONST_SPECS}

    y_d = dout("y", [2048, 1024]); ys_d = dout("ys", [16, 1024]); convp_d = dout("convp", [2, 30, 1024])
    convs_d = dout("convs", [2, 480, 1024]); kp_d = dout("kp", [2048, 1024]); vp_d = dout("vp", [2048, 1024])
    lfp_d = dout("lfp", [2048, 16]); ks_d = dout("ks", [16, 1024]); vs_d = dout("vs", [16, 1024]); lfs_d = dout("lfs", [16, 16])

    s_awin = dscr("s_awin", [2, 6, 128, 4096]); s_awout = dscr("s_awout", [2, 2, 128, 4096])
    s_kvw = dscr("s_kvw", [4, 128, 4096]); s_kvf = dscr("s_kvf", [128, 128])
    s_bwin = dscr("s_bwin", [2, 4, 128, 4096]); s_bwout = dscr("s_bwout", [2, 2, 128, 4096])
    s_bias = dscr("s_bias", [16, 128, 256], F32)

    with ExitStack() as G:
        esem = {e: G.enter_context(nc.semaphore("s_" + e)) for e in Sched.CE}
        free = [G.enter_context(nc.semaphore("ch%d" % i)) for i in range(80)]
        S = Sched(nc, esem, free)

        def sbt(es, name, shape, dt):
            return es.enter_context(nc.sbuf_tensor(name, shape, dt))

        def dma(dst, src, reads, writes, ch, eng="sp"):
            S.op(eng, I("dma_start", out=dst, in_=src), reads, writes, chan=ch)

        def mm(out_ap, pairs, reads, writes):
            n = len(pairs)
            S.op("pe", [I("matmul", out_ap, l, r, start=(i == 0), stop=(i == n - 1)) for i, (l, r) in enumerate(pairs)],
                 reads, writes)

        X = sbt(G, "X", [128, 16, 1024], F32); XS = sbt(G, "XS", [16, 1024], F32)
        C = {n: sbt(G, "k_" + n, s, dt) for n, s, dt in CONST_SPECS if n not in ("eb", "ec", "bmask")}
        epsc = sbt(G, "epsc", [128, 1], F32); onec = sbt(G, "onec", [128, 1], F32)
        rX = [Res("X%d" % i) for i in range(16)]; rXS = Res("XS"); rC = Res("consts")
        PS = [G.enter_context(nc.psum_tensor("ps%d" % i, [128, 512], F32)) for i in range(7)]
        PSB = G.enter_context(nc.psum_tensor("psb", [128, 1024], BF16))
        rPS = [Res("ps%d" % i) for i in range(7)]; rPSB = Res("psb")
        ld_c = S.chan("ldc"); ld_x = S.chan("ldx")
        for n in C:
            dma(C[n][:], cd[n], [], [rC], ld_c)
        S.op("dve", I("memset", epsc[:], EPS), [], [rC])
        S.op("dve", I("memset", onec[:], 1.0), [], [rC])
        xv = x_d.rearrange("(i p) d -> p i d", p=128)
        for i in range(16):
            dma(X[:, i, :], xv[:, i, :], [], [rX[i]], ld_x)
        dma(XS[:], xs_d, [], [rXS], ld_x)

        with ExitStack() as P0:
            stg = [sbt(P0, "stg%d" % i, [128, 8, 512], F32) for i in range(2)]
            cvb = [sbt(P0, "cvb%d" % i, [128, 8, 512], BF16) for i in range(2)]
            rstg = [Res(), Res()]; rcvb = [Res(), Res()]
            chl = [S.chan("cvl0"), S.chan("cvl1")]; chs = [S.chan("cvs0"), S.chan("cvs1")]
            rscr = Res("scratch")
            state = {"n": 0}

            def conv_unit(W, pieces, dst, scale, width=512):
                k = state["n"] % 2
                state["n"] += 1
                for (c0, w, d0) in pieces:
                    dma(stg[k][:, :, d0:d0 + w], W[:, c0:c0 + w].rearrange("(c p) n -> p c n", p=128), [], [rstg[k]], chl[k])
                eng = ("dve", "act")[state["n"] % 2]
                src = stg[k][:, :, 0:width]; dstt = cvb[k][:, :, 0:width]
                if eng == "act":
                    S.op("act", I("mul", dstt, src, scale), [rstg[k]], [rcvb[k]])
                elif scale == 1.0:
                    S.op(eng, I("tensor_copy", dstt, src), [rstg[k]], [rcvb[k]])
                else:
                    S.op(eng, I("tensor_scalar", dstt, src, scale, None, ALU.mult), [rstg[k]], [rcvb[k]])
                dma(dst.rearrange("p (c n) -> p c n", c=8), dstt, [rcvb[k]], [rscr], chs[k])

            for l in range(2):
                for u in range(4):
                    conv_unit(a_w_in[l], [(256 * u, 256, 0), (1024 + 256 * u, 256, 256)], s_awin[l, u], 0.5)
                for u in range(2):
                    conv_unit(a_w_in[l], [(2048 + 512 * u, 512, 0)], s_awin[l, 4 + u], 0.5)
                for u in range(2):
                    conv_unit(a_w_out[l], [(512 * u, 512, 0)], s_awout[l, u], 1.0)
            for u in range(4):
                conv_unit(kv_w, [(512 * u, 512, 0)], s_kvw[u], 1.0)
            conv_unit(kv_w, [(2048, 16, 0)], s_kvf, 1.0, width=16)
            for l in range(2):
                for u in range(4):
                    conv_unit(b_w_in[l], [(512 * u, 512, 0)], s_bwin[l, u], 1.0 if u < 2 else 0.5)
                for u in range(2):
                    conv_unit(b_w_out[l], [(512 * u, 512, 0)], s_bwout[l, u], 1.0)
            S.barrier()

        def rmsnorm_rstd(es_tmp, srcs, P, ncol, rres, wres, tag):
            ss = es_tmp["ss"]; junk = es_tmp["junk"]; rstd = es_tmp["rstd"]
            for i, (ap, r) in enumerate(srcs):
                S.op("act", I("activation", junk[:P, :], ap, AF.Square, accum_out=ss[:P, i:i + 1]),
                     [r], [es_tmp["rjunk"], es_tmp["rss"]])
            n = len(srcs)
            S.op("dve", I("tensor_scalar", rstd[:P, 0:n], ss[:P, 0:n], 1.0 / 1024.0, EPS, ALU.mult, ALU.add),
                 [es_tmp["rss"]], [es_tmp["rrstd"]])
            S.op("act", I("activation", rstd[:P, 0:n], rstd[:P, 0:n], AF.Sqrt), [es_tmp["rrstd"]], [es_tmp["rrstd"]])
            S.op("dve", I("reciprocal", rstd[:P, 0:n], rstd[:P, 0:n]), [es_tmp["rrstd"]], [es_tmp["rrstd"]])
            return rstd

        def norm_tmp(es, tag):
            return dict(ss=sbt(es, "ss" + tag, [128, 8], F32), junk=sbt(es, "junk" + tag, [128, 1024], BF16),
                        rstd=sbt(es, "rstd" + tag, [128, 8], F32), rjunk=Res(), rss=Res(), rrstd=Res())

        import os as _os
        dbg_list = []
        DEBUG = bool(_os.environ.get("KDEBUG"))
        dbg_ch = S.chan("dbg") if DEBUG else None

        dbg_state = {}
        if DEBUG:
            dbg_state["t"] = G.enter_context(nc.sbuf_tensor("dbgt", [128, 256], F32))
            dbg_state["r"] = Res()

        def dbg(name, ap, reads, P=128, N=256):
            if not DEBUG:
                return
            if "t" not in dbg_state:
                dbg_state["t"] = G.enter_context(nc.sbuf_tensor("dbgt", [128, 256], F32))
                dbg_state["r"] = Res()
            t, r = dbg_state["t"], dbg_state["r"]
            d = nc.dram_tensor("dbg_" + name, [P, N], F32, kind="ExternalOutput").ap()
            S.op("dve", I("tensor_copy", t[:P, 0:N], ap), reads, [r])
            dma(d, t[:P, 0:N], [r], [], dbg_ch)
        import types
        K = types.SimpleNamespace(**{k: v for k, v in locals().items() if k != "K"})
        K.G = G
        K.r_sbias = [Res() for _ in range(16)]
        K.ch_sb = [S.chan("sbias%d" % i) for i in range(2)]
        if upto >= 1:
            phase_a(K)
        if upto >= 2:
            phase_kv(K)
        if upto >= 3:
            phase_b(K)

        st_y = S.chan("sty")
        yv = y_d.rearrange("(i p) d -> p i d", p=128)
        for i in range(16):
            dma(yv[:, i, :], X[:, i, :], [rX[i]], [], st_y)
        dma(ys_d, XS[:], [rXS], [], st_y)
        S.barrier()
        with nc.Block() as block:
            S.emit(block)
    return nc


def brow(ap2d, row, ncols, P):
    t = ap2d.tensor
    w = ap2d.shape[1]
    return bass.AP(t, row * w, [[0, P], [1, ncols]])


def phase_a(K):
    nc, S, X, XS, C, PS, PSB = K.nc, K.S, K.X, K.XS, K.C, K.PS, K.PSB
    rX, rXS, rPS, rPSB, rC = K.rX, K.rXS, K.rPS, K.rPSB, K.rC
    dma, mm = K.dma, K.mm
    identb, identf, onesdiv = C["identb"], C["identf"], C["onesdiv"]
    with ExitStack() as P:
        def sb(name, shape, dt):
            return P.enter_context(nc.sbuf_tensor("pa_" + name, shape, dt))
        gb = sb("gb", [128, 1024], F32); prow = sb("prow", [34, 1024], F32)
        pcol = sb("pcol", [128, 8, 34], F32); lg2 = sb("lg2", [128, 8, 2], F32)
        xn = [sb("xn%d" % i, [128, 1024], BF16) for i in range(2)]
        xnT = sb("xnT", [128, 8, 512], BF16)
        wbuf = [sb("wb%d" % i, [128, 8, 512], BF16) for i in range(2)]
        glu = sb("glu", [128, 8, 542], F32)
        th = [sb("th%d" % i, [128, 512], F32) for i in range(2)]
        cv = sb("cv", [128, 8, 512], F32)
        sq = [sb("sq%d" % i, [128, 512], F32) for i in range(2)]
        mean_sb = sb("mean_sb", [128, 512], F32); rstd_sb = sb("rstd_sb", [128, 512], F32)
        sz = sb("sz", [128, 8, 512], BF16); h = sb("h", [128, 8, 512], BF16)
        sttok = [sb("sttok0", [120, 1024], F32)] * 2
        stT = sb("stT", [128, 8, 480], F32); rowt = sttok[0]
        NPE = 16
        diag = [sb("diag%d" % i, [128, NPE, 128], BF16) for i in range(2)]
        gluB = [sb("gluB%d" % i, [128, 542], BF16) for i in range(2)]
        r_diag = [Res(), Res()]; r_gluB = [Res(), Res()]
        rr = sb("rr", [128, 16], F32)
        NT = K.norm_tmp(P, "a")
        r_gb, r_prow, r_pcol = Res(), Res(), Res()
        r_xn = [Res(), Res()]; r_xnT = Res(); r_wb = [Res(), Res()]
        r_glu = [Res() for _ in range(8)]; r_cv = [Res() for _ in range(8)]; r_th = [Res(), Res()]
        r_sq = [Res(), Res()]; r_mean, r_rstd = Res(), Res(); r_sz = [Res() for _ in range(8)]
        r_h = [Res() for _ in range(8)]; r_sttok = [Res()] * 2; r_stT = Res(); r_rowt = r_sttok[0]; r_rr = Res()
        ch_w = [S.chan("aw0"), S.chan("aw1")]; ch_p = S.chan("ap"); ch_st = [S.chan("ast0")] * 2
        ch_o = S.chan("aout")
        cnt = {"w": 0, "bank": 0, "th": 0, "sq": 0, "dg": 0}

        def nxt(key, n):
            v = cnt[key] % n
            cnt[key] += 1
            return v

        for l in range(2):
            dma(gb[:], brow(K.a_norm, l, 1024, 128), [], [r_gb], ch_p)
            dma(prow[0:31, :], K.a_cw[l], [], [r_prow], ch_p)
            dma(prow[31:32, :], K.a_cb[l:l + 1, :], [], [r_prow], ch_p)
            dma(prow[32:33, :], K.a_lg[l:l + 1, :], [], [r_prow], ch_p)
            dma(prow[33:34, :], K.a_lb[l:l + 1, :], [], [r_prow], ch_p)

            S.op("pe", [I("transpose", PS[6][:, cc * 34:(cc + 1) * 34], prow[0:34, cc * 128:(cc + 1) * 128], identf[0:34, 0:34])
                        for cc in range(8)], [r_prow, rC], [rPS[6]])
            S.op("act", I("copy", pcol[:].rearrange("p c k -> p (c k)"), PS[6][:, 0:272]), [rPS[6]], [r_pcol])
            S.op("dve", I("tensor_scalar", lg2[:], pcol[:, :, 32:34], 0.5, None, ALU.mult), [r_pcol], [r_pcol])
            S.op("pool", I("memset", glu[:, :, 0:30], 0.0), [], r_glu)

            for blk in range(5):
                samp = blk == 4
                N = 16 if samp else 512
                Pn = 16 if samp else 128
                if samp:
                    srcs = [(XS[:, :], rXS)]
                else:
                    srcs = [(X[:, 4 * blk + t, :], rX[4 * blk + t]) for t in range(4)]
                rstd = K.rmsnorm_rstd(NT, srcs, Pn, len(srcs), None, None, "a")
                for t, (ap, r) in enumerate(srcs):
                    k = t % 2
                    S.op("dve", I("scalar_tensor_tensor",
                        xn[k][:Pn, :], ap, rstd[:Pn, t:t + 1], gb[:Pn, :], ALU.mult, ALU.mult),
                        [r, NT["rrstd"], r_gb], [r_xn[k]])
                    if samp:
                        S.op("pe", [I("transpose", PSB[:, c * 16:(c + 1) * 16], xn[k][:16, c * 128:(c + 1) * 128], identb[0:16, 0:16])
                                    for c in range(8)], [r_xn[k], rC], [rPSB])
                        S.op("act", I("copy", xnT[:, :, 0:16], PSB[:, 0:128].rearrange("p (c n) -> p c n", c=8)),
                             [rPSB], [r_xnT])
                    else:
                        S.op("pe", [I("transpose", PSB[:, c * 128:(c + 1) * 128], xn[k][:, c * 128:(c + 1) * 128], identb[:, :])
                                    for c in range(8)], [r_xn[k], rC], [rPSB])
                        S.op("act", I("copy", xnT[:, :, t * 128:(t + 1) * 128],
                                                          PSB[:, :].rearrange("p (c n) -> p c n", c=8)), [rPSB], [r_xnT])
                if samp:
                    for t in range(4):
                        k = t % 2
                        dma(sttok[k][:, :], K.st_d[l, 120 * t:120 * (t + 1), :], [], [r_sttok[k]], ch_st[k])
                        for half in range(2):
                            S.op("pe", [I("transpose", PS[6][:, j * 120:(j + 1) * 120],
                                          sttok[k][0:120, (4 * half + j) * 128:(4 * half + j + 1) * 128], identf[0:120, 0:120])
                                        for j in range(4)], [r_sttok[k], rC], [rPS[6]])
                            S.op("act", I("copy",
                                stT[:, 4 * half:4 * half + 4, 120 * t:120 * (t + 1)],
                                PS[6][:, 0:480].rearrange("p (c n) -> p c n", c=4)), [rPS[6]], [r_stT])
                pend = []
                pend_diag = []
                stats_q = []
                fin_add = []

                def flush():
                    while len(stats_q) > (0 if flush.final else 1):
                        cc, kq = stats_q.pop(0)
                        S.op("pe", I("matmul", PS[4][:, 0:N], onesdiv[:, :], cv[:, cc, 0:N], start=(cc == 0), stop=(cc == 7)),
                             [r_cv[cc], rC], [rPS[4]])
                        S.op("pe", I("matmul", PS[5][:, 0:N], onesdiv[:, :], sq[kq][:, 0:N], start=(cc == 0), stop=(cc == 7)),
                             [r_sq[kq], rC], [rPS[5]])
                    for (cc, dg, cb_) in pend_diag:
                        S.op("pe", [I("matmul", PS[cb_][:, 0:N], diag[dg][:, kk, :], gluB[dg][:, kk:kk + N],
                                      start=(kk == 0), stop=(kk == NPE - 1)) for kk in range(NPE)],
                             [r_diag[dg], r_gluB[dg]], [rPS[cb_]])
                    pend_diag.clear()
                    for (cc, cb_) in fin_add:
                        S.op("dve", I("tensor_tensor", cv[:, cc, 0:N], cv[:, cc, 0:N], PS[cb_][:, 0:N], ALU.add),
                             [r_cv[cc], rPS[cb_]], [r_cv[cc]])
                        kq = nxt("sq", 2)
                        S.op("act", I("activation", sq[kq][:, 0:N], cv[:, cc, 0:N], AF.Square), [r_cv[cc]], [r_sq[kq]])
                        pend.append((cc, kq))
                    fin_add.clear()
                    for item in pend:
                        stats_q.append(item)
                    pend.clear()
                flush.final = False

                for u in range(6):
                    ws = nxt("w", 2)
                    dma(wbuf[ws][:], K.s_awin[l, u].rearrange("p (c n) -> p c n", c=8), [], [r_wb[ws]], ch_w[ws])

                    def mmf(fc, bank):
                        mm(PS[bank][:, 0:N], [(wbuf[ws][:, c, fc * 128:(fc + 1) * 128], xnT[:, c, 0:N]) for c in range(8)],
                           [r_wb[ws], r_xnT], [rPS[bank]])
                    if u < 4:
                        for j in range(2):
                            cc = 2 * u + j
                            ba = nxt("bank", 3); bg = nxt("bank", 3)
                            mmf(j, ba); mmf(2 + j, bg)
                            flush()
                            kt = nxt("th", 2)
                            S.op("act", I("activation", th[kt][:, 0:N], PS[bg][:, 0:N], AF.Tanh),
                                 [rPS[bg]], [r_th[kt]])
                            S.op("dve", I("scalar_tensor_tensor",
                                glu[:, cc, 30:30 + N], th[kt][:, 0:N], 1.0, PS[ba][:, 0:N], ALU.add, ALU.mult),
                                [r_th[kt], rPS[ba]], [r_glu[cc]])
                            if not samp:
                                dg = nxt("dg", 2)
                                S.op("pool", I("tensor_copy", gluB[dg][:, :], glu[:, cc, :]), [r_glu[cc]], [r_gluB[dg]])
                                for kk in range(NPE):
                                    S.op("act", I("mul", diag[dg][:, kk, :], identb[:, :], pcol[:, cc, kk:kk + 1]),
                                         [rC, r_pcol], [r_diag[dg]])
                                cb_ = (3, 6)[dg]
                                pend_diag.append((cc, dg, cb_))
                                S.op("dve", I("tensor_scalar",
                                    cv[:, cc, 0:N], glu[:, cc, NPE:NPE + N], pcol[:, cc, NPE:NPE + 1], pcol[:, cc, 31:32], ALU.mult, ALU.add),
                                    [r_glu[cc], r_pcol], [r_cv[cc]])
                                for kk in range(NPE + 1, 31):
                                    S.op("dve", I("scalar_tensor_tensor",
                                        cv[:, cc, 0:N], glu[:, cc, kk:kk + N], pcol[:, cc, kk:kk + 1], cv[:, cc, 0:N],
                                        ALU.mult, ALU.add), [r_glu[cc], r_pcol, r_cv[cc]], [r_cv[cc]])
                                fin_add.append((cc, cb_))
                            else:
                                kq0 = nxt("sq", 2)
                                S.op("dve", I("tensor_tensor",
                                    sq[kq0][:, 0:480].rearrange("p (s k) -> p s k", s=16),
                                    stT[:, cc, :].rearrange("p (s k) -> p s k", s=16),
                                    pcol[:, cc, 0:30].unsqueeze(1).to_broadcast([128, 16, 30]), ALU.mult),
                                    [r_stT, r_pcol], [r_sq[kq0]])
                                S.op("dve", I("tensor_reduce",
                                    rr[:, 0:16], sq[kq0][:, 0:480].rearrange("p (s k) -> p s k", s=16), AX.X, ALU.add),
                                    [r_sq[kq0]], [r_rr])
                                S.op("dve", I("scalar_tensor_tensor",
                                    cv[:, cc, 0:16], glu[:, cc, 30:46], pcol[:, cc, 30:31], rr[:, 0:16], ALU.mult, ALU.add),
                                    [r_glu[cc], r_pcol, r_rr], [r_cv[cc]])
                                S.op("dve", I("tensor_scalar",
                                    cv[:, cc, 0:16], cv[:, cc, 0:16], pcol[:, cc, 31:32], None, ALU.add),
                                    [r_cv[cc], r_pcol], [r_cv[cc]])
                            if samp:
                                kq = nxt("sq", 2)
                                S.op("act", I("activation", sq[kq][:, 0:N], cv[:, cc, 0:N], AF.Square),
                                     [r_cv[cc]], [r_sq[kq]])
                                S.op("pe", I("matmul", PS[4][:, 0:N], onesdiv[:, :], cv[:, cc, 0:N], start=(cc == 0), stop=(cc == 7)),
                                     [r_cv[cc], rC], [rPS[4]])
                                S.op("pe", I("matmul", PS[5][:, 0:N], onesdiv[:, :], sq[kq][:, 0:N], start=(cc == 0), stop=(cc == 7)),
                                     [r_sq[kq], rC], [rPS[5]])
                    else:
                        for fc in range(4):
                            cc = 4 * (u - 4) + fc
                            bz = nxt("bank", 3)
                            mmf(fc, bz)
                            flush()
                            kt = nxt("th", 2)
                            S.op("act", I("activation", th[kt][:, 0:N], PS[bz][:, 0:N], AF.Tanh),
                                 [rPS[bz]], [r_th[kt]])
                            S.op("dve", I("scalar_tensor_tensor",
                                sz[:, cc, 0:N], th[kt][:, 0:N], 1.0, PS[bz][:, 0:N], ALU.add, ALU.mult),
                                [r_th[kt], rPS[bz]], [r_sz[cc]])
                flush(); flush.final = True; flush(); flush()
                kt = nxt("th", 2)
                S.op("act", I("activation", th[kt][:, 0:N], PS[4][:, 0:N], AF.Square), [rPS[4]], [r_th[kt]])
                S.op("dve", I("tensor_tensor", rstd_sb[:, 0:N], PS[5][:, 0:N], th[kt][:, 0:N], ALU.subtract),
                     [rPS[5], r_th[kt]], [r_rstd])
                S.op("act", I("activation", rstd_sb[:, 0:N], rstd_sb[:, 0:N], AF.Sqrt, bias=K.epsc[:, 0:1]),
                     [r_rstd, rC], [r_rstd])
                S.op("dve", I("reciprocal", rstd_sb[:, 0:N], rstd_sb[:, 0:N]), [r_rstd], [r_rstd])
                S.op("act", I("copy", mean_sb[:, 0:N], PS[4][:, 0:N]), [rPS[4]], [r_mean])
                for cc in range(8):
                    c_ = cv[:, cc, 0:N]
                    S.op("dve", I("tensor_tensor", c_, c_, mean_sb[:, 0:N], ALU.subtract), [r_cv[cc], r_mean], [r_cv[cc]])
                    S.op("dve", I("tensor_tensor", c_, c_, rstd_sb[:, 0:N], ALU.mult), [r_cv[cc], r_rstd], [r_cv[cc]])
                    S.op("dve", I("tensor_scalar", c_, c_, lg2[:, cc, 0:1], lg2[:, cc, 1:2], ALU.mult, ALU.add),
                         [r_cv[cc], r_pcol], [r_cv[cc]])
                    kt = nxt("th", 2)
                    S.op("act", I("activation", th[kt][:, 0:N], c_, AF.Tanh), [r_cv[cc]], [r_th[kt]])
                    S.op("dve", I("scalar_tensor_tensor", c_, th[kt][:, 0:N], 1.0, c_, ALU.add, ALU.mult),
                         [r_cv[cc], r_th[kt]], [r_cv[cc]])
                    S.op("dve", I("tensor_tensor", h[:, cc, 0:N], c_, sz[:, cc, 0:N], ALU.mult),
                         [r_cv[cc], r_sz[cc]], [r_h[cc]])
                for u in range(2):
                    ws = nxt("w", 2)
                    dma(wbuf[ws][:], K.s_awout[l, u].rearrange("p (c n) -> p c n", c=8), [], [r_wb[ws]], ch_w[ws])
                    for t in range(1 if samp else 4):
                        bk = nxt("bank", 3)
                        if samp:
                            mm(PS[bk][0:16, :], [(h[:, c, 0:16], wbuf[ws][:, c, :]) for c in range(8)], r_h + [r_wb[ws]], [rPS[bk]])
                            xs_ = XS[:, u * 512:(u + 1) * 512]
                            S.op("dve", I("tensor_tensor", xs_, xs_, PS[bk][0:16, :], ALU.add),
                                 [rXS, rPS[bk]], [rXS])
                        else:
                            i = 4 * blk + t
                            mm(PS[bk][:, :], [(h[:, c, t * 128:(t + 1) * 128], wbuf[ws][:, c, :]) for c in range(8)],
                               r_h + [r_wb[ws]], [rPS[bk]])
                            x_ = X[:, i, u * 512:(u + 1) * 512]
                            S.op("dve", I("tensor_tensor", x_, x_, PS[bk][:, :], ALU.add),
                                 [rX[i], rPS[bk]], [rX[i]])
                if blk < 3:
                    S.op("pool", I("tensor_copy", glu[:, :, 0:30], glu[:, :, 512:542]), r_glu, r_glu)
                elif blk == 3 or samp:
                    nr = 16 if samp else 30
                    c0 = 30 if samp else 512
                    for half in range(2):
                        S.op("pe", [I("transpose", PS[6][0:nr, j * 128:(j + 1) * 128], glu[:, 4 * half + j, c0:c0 + nr], identf[:, :])
                                    for j in range(4)], r_glu + [rC], [rPS[6]])
                        S.op("act", I("copy", rowt[0:nr, half * 512:(half + 1) * 512], PS[6][0:nr, :]),
                             [rPS[6]], [r_rowt])
                    if samp:
                        cs = K.convs_d[l].rearrange("(s k) d -> s k d", k=30)
                        dma(cs[:, 29, :], rowt[0:16, :], [r_rowt], [], ch_o)
                        dma(cs[:, 0:29, :], K.st_d[l].rearrange("(s k) d -> s k d", k=30)[:, 1:30, :], [], [], ch_o)
                    else:
                        dma(K.convp_d[l], rowt[0:30, :], [r_rowt], [], ch_o)
        S.barrier()


def make_in_maps(inputs, cores, NP=2560, attn=True, pt_override=None):
    consts = make_consts()
    f = lambda a: np.ascontiguousarray(a, dtype=np.float32)
    maps = []
    for c in cores:
        m = {
            "x": f(inputs["x_prompt"][c]),
            "xs": f(inputs["x_sample"][16 * c:16 * c + 16, 0]),
            "state": f(inputs["state_conv"][:, 16 * c:16 * c + 16]).reshape(2, 480, 1024),
            "a_norm": f(inputs["a_norm"]), "a_w_in": f(inputs["a_w_in"]), "a_cw": f(inputs["a_conv_w"]),
            "a_cb": f(inputs["a_conv_b"]), "a_lg": f(inputs["a_ln_g"]), "a_lb": f(inputs["a_ln_b"]),
            "a_w_out": f(inputs["a_w_out"]), "kv_norm": f(inputs["kv_norm"]).reshape(1, 1024), "kv_w": f(inputs["kv_w"]),
            "kv_fb": f(inputs["kv_fb"]).reshape(1, 16), "k_norm": f(inputs["k_norm"]).reshape(1, 64),
            "b_norm": f(inputs["b_norm"]), "b_w_in": f(inputs["b_w_in"]), "q_norm": f(inputs["q_norm"]),
            "b_w_out": f(inputs["b_w_out"]),
        }
        if attn:
            m["ck"] = inputs["cache_k"].reshape(NP * 128, 1024)
            m["cv"] = inputs["cache_v"].reshape(NP * 128, 1024)
            m["cl"] = inputs["cache_logf"].reshape(NP * 128, 16)
            pt = inputs["page_table"] if pt_override is None else pt_override
            m["pt"] = np.ascontiguousarray(pt[16 * c:16 * c + 16]).reshape(1, 256).astype(np.int32)
        for n, arr in consts.items():
            m["c_" + n] = arr
        maps.append(m)
    return maps


def assemble(results):
    cat = lambda k: np.stack([r[k] for r in results], 0)
    n = len(results)
    y = cat("y").reshape(n, 2048, 1024)
    ys = cat("ys").reshape(n * 16, 1, 1024)
    convp = np.transpose(cat("convp"), (1, 0, 2, 3))
    convs = np.transpose(cat("convs").reshape(n, 2, 16, 30, 1024), (1, 0, 2, 3, 4)).reshape(2, n * 16, 30, 1024)
    kp = cat("kp").reshape(n, 2048, 16, 64); vp = cat("vp").reshape(n, 2048, 16, 64); lfp = cat("lfp").reshape(n, 2048, 16)
    ks = cat("ks").reshape(n * 16, 1, 16, 64); vs = cat("vs").reshape(n * 16, 1, 16, 64); lfs = cat("lfs").reshape(n * 16, 1, 16)
    return tuple(np.ascontiguousarray(a, dtype=np.float32) for a in (y, ys, convp, convs, kp, vp, lfp, ks, vs, lfs))


def kernel(**inputs):
    NP = inputs["cache_k"].shape[0]
    nc = build(NP=NP)
    maps = make_in_maps(inputs, list(range(NCORES)), NP=NP)
    res = run_bass_kernel_spmd(nc, maps, core_ids=list(range(NCORES)))
    return assemble(res.results)


def phase_kv(K):
    nc, S, X, XS, C, PS, PSB = K.nc, K.S, K.X, K.XS, K.C, K.PS, K.PSB
    rX, rXS, rPS, rPSB, rC = K.rX, K.rXS, K.rPS, K.rPSB, K.rC
    dma, mm = K.dma, K.mm
    identb = C["identb"]
    PB = K.G

    def sbp(name, shape, dt):
        return PB.enter_context(nc.sbuf_tensor("pb_" + name, shape, dt))
    KT = sbp("KT", [128, 8, 2048], BF16); VA = sbp("VA", [128, 16, 1040], BF16)
    biasT = sbp("biasT", [128, 8, 16, 16], F32)
    ksn = sbp("ksn", [16, 1024], F32); vss = sbp("vss", [16, 1024], F32); lfs = sbp("lfs", [16, 16], F32)
    Cc = sbp("Cc", [128, 16, 16], F32); Rsb = sbp("Rsb", [128, 8, 16], F32)
    K.Cc, K.Rsb = Cc, Rsb
    K.KT, K.VA, K.biasT, K.ksn, K.vss, K.lfs = KT, VA, biasT, ksn, vss, lfs
    K.r_KT, K.r_VA, K.r_bias, K.r_ksn, K.r_vss, K.r_lfs = Res(), Res(), Res(), Res(), Res(), Res()
    with ExitStack() as P:
        def sb(name, shape, dt):
            return P.enter_context(nc.sbuf_tensor("kv_" + name, shape, dt))
        wkv = sb("wkv", [128, 8, 2048], BF16); wkf = sb("wkf", [128, 8, 16], BF16)
        gb = sb("gb", [128, 1024], F32); kg = sb("kg", [128, 64], F32); fbb = sb("fbb", [128, 16], F32)
        xn = sb("xn", [128, 1024], BF16); xnT = sb("xnT", [128, 8, 128], BF16)
        ofull = sb("ofull", [128, 1024], F32); sqt = sb("sqt", [128, 512], F32); kb16 = sb("kb16", [128, 1024], BF16)
        ss8 = sb("ss8", [128, 8], F32); lt = [sb("lt%d" % i, [128, 16], F32) for i in range(4)]
        lfa = sb("lfa", [128, 16, 16], F32); carry = sb("carry", [128, 16, 16], F32)
        NT = K.norm_tmp(P, "k")
        r_w, r_gb, r_xn, r_xnT, r_of, r_sq, r_kb, r_ss8, r_lt, r_lfa, r_carry, r_Cc, r_R = [Res() for _ in range(13)]
        ch_w = S.chan("kvw"); ch_o = S.chan("kvo")
        for u in range(4):
            dma(wkv[:, :, u * 512:(u + 1) * 512], K.s_kvw[u].rearrange("p (c n) -> p c n", c=8), [], [r_w], ch_w)
        dma(wkf[:], K.s_kvf[:, :].rearrange("p (c n) -> p c n", c=8), [], [r_w], ch_w)
        dma(gb[:], brow(K.kv_norm, 0, 1024, 128), [], [r_gb], ch_w)
        dma(kg[:], brow(K.k_norm, 0, 64, 128), [], [r_gb], ch_w)
        dma(fbb[:], brow(K.kv_fb, 0, 16, 128), [], [r_gb], ch_w)
        S.op("pool", I("memset", VA[:].rearrange("p i (h e) -> p (i h) e", e=65)[:, :, 64:65], 1.0), [], [K.r_VA])
        bank = [0]

        def nb():
            bank[0] = (bank[0] + 1) % 4
            return bank[0]
        kpv = K.kp_d.rearrange("(i p) d -> p i d", p=128); vpv = K.vp_d.rearrange("(i p) d -> p i d", p=128)
        lfv = K.lfp_d.rearrange("(i p) d -> p i d", p=128)
        for i in range(17):
            samp = i == 16
            Pn = 16 if samp else 128
            src, rs = (XS[:, :], rXS) if samp else (X[:, i, :], rX[i])
            rstd = K.rmsnorm_rstd(NT, [(src, rs)], Pn, 1, None, None, "k")
            S.op("dve", I("scalar_tensor_tensor", xn[:Pn, :], src, rstd[:Pn, 0:1], gb[:Pn, :], ALU.mult, ALU.mult),
                 [rs, NT["rrstd"], r_gb], [r_xn])
            S.op("pe", [I("transpose", PSB[:, c * Pn:(c + 1) * Pn], xn[:Pn, c * 128:(c + 1) * 128], identb[0:Pn, 0:Pn])
                        for c in range(8)], [r_xn, rC], [rPSB])
            S.op("act", I("copy", xnT[:, :, 0:Pn], PSB[:, 0:8 * Pn].rearrange("p (c n) -> p c n", c=8)), [rPSB], [r_xnT])
            for u in range(4):
                bk = nb()
                mm(PS[bk][:Pn, :], [(xnT[:, c, 0:Pn], wkv[:, c, u * 512:(u + 1) * 512]) for c in range(8)], [r_xnT, r_w], [rPS[bk]])
                osl = ofull[:Pn, (u % 2) * 512:(u % 2 + 1) * 512]
                if u < 2:
                    S.op("act", I("activation", sqt[:Pn, :], PS[bk][:Pn, :], AF.Square), [rPS[bk]], [r_sq])
                    S.op("dve", I("tensor_reduce", ss8[:Pn, :], sqt[:Pn, :].rearrange("p (h d) -> p h d", d=64), AX.X, ALU.add),
                         [r_sq], [r_ss8])
                    S.op("dve", I("tensor_scalar", ss8[:Pn, :], ss8[:Pn, :], 1.0 / 64.0, EPS, ALU.mult, ALU.add), [r_ss8], [r_ss8])
                    S.op("act", I("activation", ss8[:Pn, :], ss8[:Pn, :], AF.Sqrt), [r_ss8], [r_ss8])
                    S.op("dve", I("reciprocal", ss8[:Pn, :], ss8[:Pn, :]), [r_ss8], [r_ss8])
                    o3 = osl.rearrange("p (h d) -> p h d", d=64)
                    S.op("dve", I("tensor_tensor", o3, PS[bk][:Pn, :].rearrange("p (h d) -> p h d", d=64),
                                  ss8[:Pn, :].unsqueeze(2).to_broadcast([Pn, 8, 64]), ALU.mult), [rPS[bk], r_ss8], [r_of])
                    S.op("dve", I("tensor_tensor", o3, o3, kg[:Pn, :].unsqueeze(1).to_broadcast([Pn, 8, 64]), ALU.mult),
                         [r_of, r_gb], [r_of])
                    if u == 1:
                        if samp:
                            dma(K.ks_d, ofull[:16, :], [r_of], [], ch_o)
                            S.op("act", I("copy", ksn[:, :], ofull[:16, :]), [r_of], [K.r_ksn])
                        else:
                            dma(kpv[:, i, :], ofull[:, :], [r_of], [], ch_o)
                            S.op("pool", I("tensor_copy", kb16[:, :], ofull[:, :]), [r_of], [r_kb])
                            S.op("pe", [I("transpose", PSB[:, c * 128:(c + 1) * 128], kb16[:, c * 128:(c + 1) * 128], identb[:, :])
                                        for c in range(8)], [r_kb, rC], [rPSB])
                            S.op("act", I("copy", KT[:, :, i * 128:(i + 1) * 128], PSB[:, :].rearrange("p (c n) -> p c n", c=8)),
                                 [rPSB], [K.r_KT])
                else:
                    S.op("act", I("copy", osl, PS[bk][:Pn, :]), [rPS[bk]], [r_of])
                    if not samp:
                        va = VA[:, i, :].rearrange("p (h e) -> p h e", e=65)[:, 8 * (u - 2):8 * (u - 2) + 8, 0:64]
                        S.op("pool", I("tensor_copy", va, osl.rearrange("p (h d) -> p h d", d=64)), [r_of], [K.r_VA])
                    if u == 3:
                        if samp:
                            dma(K.vs_d, ofull[:16, :], [r_of], [], ch_o)
                            S.op("act", I("copy", vss[:, :], ofull[:16, :]), [r_of], [K.r_vss])
                        else:
                            dma(vpv[:, i, :], ofull[:, :], [r_of], [], ch_o)
            bk = nb()
            mm(PS[bk][:Pn, 0:16], [(xnT[:, c, 0:Pn], wkf[:, c, :]) for c in range(8)], [r_xnT, r_w], [rPS[bk]])
            u_, a_, e_, m_ = [t[:Pn, :] for t in lt]
            S.op("dve", I("tensor_tensor", u_, PS[bk][:Pn, 0:16], fbb[:Pn, :], ALU.add), [rPS[bk], r_gb], [r_lt])
            S.op("act", I("activation", a_, u_, AF.Abs), [r_lt], [r_lt])
            S.op("act", I("activation", e_, a_, AF.Exp, scale=-1.0), [r_lt], [r_lt])
            S.op("act", I("activation", e_, e_, AF.Ln, bias=K.onec[:Pn, 0:1]), [r_lt, rC], [r_lt])
            S.op("dve", I("tensor_scalar_min", m_, u_, 0.0), [r_lt], [r_lt])
            if samp:
                S.op("dve", I("tensor_tensor", lfs[:, :], m_, e_, ALU.subtract), [r_lt], [K.r_lfs])
                dma(K.lfs_d, lfs[:, :], [K.r_lfs], [], ch_o)
            else:
                S.op("dve", I("tensor_tensor", lfa[:, i, :], m_, e_, ALU.subtract), [r_lt], [r_lfa])
        dma(lfv, lfa[:, :, :], [r_lfa], [], ch_o)
        lff = lfa[:].rearrange("p i h -> p (i h)")
        mm(PS[5][:, 0:256], [(C["lincl"][:, :], lff)], [r_lfa, rC], [rPS[5]])
        mm(PS[4][:, 0:256], [(C["ones1"][:, :], lff)], [r_lfa, rC], [rPS[4]])
        S.op("dve", I("memset", carry[:, 0, :], 0.0), [], [r_carry])
        for i in range(1, 16):
            S.op("dve", I("tensor_tensor", carry[:, i, :], carry[:, i - 1, :], PS[4][:, (i - 1) * 16:i * 16], ALU.add),
                 [r_carry, rPS[4]], [r_carry])
        S.op("dve", I("tensor_tensor", Cc[:].rearrange("p i h -> p (i h)"), PS[5][:, 0:256], carry[:].rearrange("p i h -> p (i h)"),
                      ALU.add), [rPS[5], r_carry], [r_Cc])
        S.op("pe", [I("matmul", PS[6][:, m * 16:(m + 1) * 16], C["rs0"][:, :], Cc[:, 2 * m + 1, :], start=True, stop=True)
                    for m in range(8)], [r_Cc, rC], [rPS[6]])
        S.op("act", I("copy", Rsb[:].rearrange("p b h -> p (b h)"), PS[6][:, 0:128]), [rPS[6]], [r_R])
        for m in range(8):
            nj = 2 * m + 2
            S.op("dve", I("tensor_tensor", biasT[:, m, 0:nj, :], Rsb[:, m, :].unsqueeze(1).to_broadcast([128, nj, 16]),
                          Cc[:, 0:nj, :], ALU.subtract), [r_R, r_Cc], [K.r_bias])
        S.barrier()


def phase_b(K):
    nc, S, X, XS, C, PS, PSB = K.nc, K.S, K.X, K.XS, K.C, K.PS, K.PSB
    rX, rXS, rPS, rPSB, rC = K.rX, K.rXS, K.rPS, K.rPSB, K.rC
    dma, mm = K.dma, K.mm
    identb, tri = C["identb"], C["tri"]
    KT, VA, biasT = K.KT, K.VA, K.biasT
    for l in range(2):
        with ExitStack() as P:
            def sb(name, shape, dt):
                return P.enter_context(nc.sbuf_tensor("b%d_" % l + name, shape, dt))
            wb = [sb("wb%d" % i, [128, 8, 256], BF16) for i in range(2)]
            gb = sb("gb", [128, 1024], F32); qg = sb("qg", [128, 64], F32)
            xn = sb("xn", [128, 1024], BF16); xnT = sb("xnT", [128, 8, 512], BF16); QT = sb("QT", [128, 8, 512], BF16)
            qn2 = [sb("qn%d" % i, [128, 256], BF16) for i in range(2)]
            szb = sb("szb", [128, 4, 1024], BF16); g = sb("g", [128, 4, 1024], BF16)
            r_qn2 = [Res(), Res()]
            pt = [sb("pt%d" % i, [128, 256], BF16) for i in range(4)]
            th = sb("th", [128, 256], F32); sqt = sb("sqt", [128, 256], F32); ss4 = sb("ss4", [128, 4], F32)
            rec = sb("rec", [128, 4], F32)
            NT = K.norm_tmp(P, "b%d" % l)
            r_wb = [Res(), Res()]; r_gb, r_xn, r_xnT, r_QT, r_qn, r_g, r_th, r_sq, r_ss4, r_rec = [Res() for _ in range(10)]
            r_sz = [Res() for _ in range(4)]; r_pt = [Res() for _ in range(4)]
            ch_w = [S.chan("bw%d_%d" % (l, i)) for i in range(2)]; ch_p = S.chan("bp%d" % l)
            cnt = {"w": 0, "s": 0, "p": 0, "pt": 0}

            def nxt(key, n):
                v = cnt[key] % n
                cnt[key] += 1
                return v
            dma(gb[:], brow(K.b_norm, l, 1024, 128), [], [r_gb], ch_p)
            dma(qg[:], brow(K.q_norm, l, 64, 128), [], [r_gb], ch_p)
            for b in range(4):
                srcs = [(X[:, 4 * b + t, :], rX[4 * b + t]) for t in range(4)]
                rstd = K.rmsnorm_rstd(NT, srcs, 128, 4, None, None, "b")
                for t, (ap, r) in enumerate(srcs):
                    S.op("dve", I("scalar_tensor_tensor", xn[:, :], ap, rstd[:, t:t + 1], gb[:, :], ALU.mult, ALU.mult),
                         [r, NT["rrstd"], r_gb], [r_xn])
                    S.op("pe", [I("transpose", PSB[:, c * 128:(c + 1) * 128], xn[:, c * 128:(c + 1) * 128], identb[:, :])
                                for c in range(8)], [r_xn, rC], [rPSB])
                    S.op("act", I("copy", xnT[:, :, t * 128:(t + 1) * 128], PSB[:, :].rearrange("p (c n) -> p c n", c=8)),
                         [rPSB], [r_xnT])
                pend_q = []

                def flush_q(keep):
                    while len(pend_q) > keep:
                        qi, v_, t_ = pend_q.pop(0)
                        S.op("pe", [I("transpose", PSB[:, j * 128:(j + 1) * 128], qn2[qi][:, j * 128:(j + 1) * 128], identb[:, :])
                                    for j in range(2)], [r_qn2[qi], rC], [rPSB])
                        S.op("act", I("copy", QT[:, 2 * v_:2 * v_ + 2, t_ * 128:(t_ + 1) * 128],
                                      PSB[:, 0:256].rearrange("p (c n) -> p c n", c=2)), [rPSB], [r_QT])
                for v in range(8):
                    ws = nxt("w", 2)
                    dma(wb[ws][:], K.s_bwin[l, v // 2].rearrange("p (c n) -> p c n", c=8)[:, :, (v % 2) * 256:(v % 2 + 1) * 256],
                        [], [r_wb[ws]], ch_w[ws])
                    for t in range(4):
                        bk = 5 + nxt("p", 2)
                        mm(PS[bk][:, 0:256], [(xnT[:, c, t * 128:(t + 1) * 128], wb[ws][:, c, :]) for c in range(8)],
                           [r_xnT, r_wb[ws]], [rPS[bk]])
                        flush_q(1)
                        ps = PS[bk][:, 0:256]
                        if v < 4:
                            S.op("act", I("activation", sqt[:, :], ps, AF.Square), [rPS[bk]], [r_sq])
                            S.op("dve", I("tensor_reduce", ss4[:, :], sqt[:, :].rearrange("p (h d) -> p h d", d=64), AX.X, ALU.add),
                                 [r_sq], [r_ss4])
                            S.op("dve", I("tensor_scalar", ss4[:, :], ss4[:, :], 1.0 / 64.0, EPS, ALU.mult, ALU.add), [r_ss4], [r_ss4])
                            S.op("act", I("activation", ss4[:, :], ss4[:, :], AF.Sqrt), [r_ss4], [r_ss4])
                            S.op("dve", I("reciprocal", ss4[:, :], ss4[:, :]), [r_ss4], [r_ss4])
                            S.op("dve", I("tensor_tensor", th[:, :].rearrange("p (h d) -> p h d", d=64),
                                          ps.rearrange("p (h d) -> p h d", d=64),
                                          ss4[:, :].unsqueeze(2).to_broadcast([128, 4, 64]), ALU.mult), [rPS[bk], r_ss4], [r_th])
                            qi = (4 * v + t) % 2
                            S.op("dve", I("tensor_tensor", qn2[qi][:, :].rearrange("p (h d) -> p h d", d=64),
                                          th[:, :].rearrange("p (h d) -> p h d", d=64),
                                          qg[:, :].unsqueeze(1).to_broadcast([128, 4, 64]), ALU.mult), [r_th, r_gb], [r_qn2[qi]])
                            pend_q.append((qi, v, t))
                        else:
                            S.op("act", I("activation", th[:, :], ps, AF.Tanh), [rPS[bk]], [r_th])
                            S.op("dve", I("scalar_tensor_tensor", szb[:, t, (v - 4) * 256:(v - 3) * 256], th[:, :], 1.0, ps,
                                          ALU.add, ALU.mult), [r_th, rPS[bk]], [r_sz[t]])
                if l == 0 and b == 0:
                    K.dbg("bias", biasT[:, 0, :, :].rearrange("p j h -> p (j h)"), [K.r_bias])
                    K.dbg("qt", QT[:, 0, 0:256], [r_QT])
                    K.dbg("kt", KT[:, 0, 0:256], [K.r_KT])
                    K.dbg("va", VA[:, 0, 0:256], [K.r_VA])
                    K.dbg("szb", szb[:, 0, 0:256], r_sz)
                flush_q(0)
                steps = []
                for h in range(16):
                    for q2 in range(2):
                        m = 2 * b + q2
                        for j in range(2 * m + 2):
                            steps.append(dict(h=h, q2=q2, m=m, j=j, n0=max(0, j - 2 * m) * 128, last=(q2 == 1 and j == 2 * m + 1)))
                LA = 2

                def emit_front(st):
                    h, q2, m, j, n0 = st["h"], st["q2"], st["m"], st["j"], st["n0"]
                    c, po, q0 = h // 2, 64 * (h % 2), q2 * 256
                    bk = nxt("s", 3)
                    mm(PS[bk][:, n0:256], [(KT[po:po + 64, c, j * 128:(j + 1) * 128], QT[po:po + 64, c, q0 + n0:q0 + 256])],
                       [K.r_KT, r_QT], [rPS[bk]])
                    k = nxt("pt", 4)
                    st["k"] = k
                    S.op("act", I("activation", pt[k][:, n0:256], PS[bk][:, n0:256], AF.Exp,
                                  bias=biasT[:, m, j, h:h + 1], scale=0.125), [rPS[bk], K.r_bias], [r_pt[k]])
                    if j >= 2 * m:
                        S.op("pool", I("tensor_tensor", pt[k][:, n0:n0 + 128], pt[k][:, n0:n0 + 128], tri[:, :], ALU.mult),
                             [r_pt[k], rC], [r_pt[k]])

                def emit_back(st):
                    h, q2, m, j, n0, k = st["h"], st["q2"], st["m"], st["j"], st["n0"], st["k"]
                    ob = 3 + h % 2
                    S.op("pe", [I("matmul", PS[ob][:, (2 * q2 + tl) * 65:(2 * q2 + tl + 1) * 65], pt[k][:, tl * 128:(tl + 1) * 128],
                                  VA[:, j, h * 65:(h + 1) * 65], start=(j == 0 and q2 == 0 and tl == 0), stop=(j == 2 * m + tl))
                                for tl in range(n0 // 128, 2)], [r_pt[k], K.r_VA], [rPS[ob]])
                    if st["last"]:
                        o3 = PS[ob][:, 0:260].rearrange("p (t e) -> p t e", e=65)
                        S.op("dve", I("reciprocal", rec[:, :], o3[:, :, 64]), [rPS[ob]], [r_rec])
                        for tq in range(4):
                            S.op("dve", I("scalar_tensor_tensor", g[:, tq, h * 64:(h + 1) * 64], o3[:, tq, 0:64], rec[:, tq:tq + 1],
                                          szb[:, tq, h * 64:(h + 1) * 64], ALU.mult, ALU.mult), [rPS[ob], r_rec, r_sz[tq]], [r_g])

                for i in range(len(steps) + LA):
                    if i < len(steps):
                        emit_front(steps[i])
                    if i - LA >= 0:
                        emit_back(steps[i - LA])
                if l == 0 and b == 0:
                    K.dbg("g", g[:, 0, 0:256], [r_g])
                for t in range(4):
                    S.op("pe", [I("transpose", PSB[:, c * 128:(c + 1) * 128], g[:, t, c * 128:(c + 1) * 128], identb[:, :])
                                for c in range(8)], [r_g, rC], [rPSB])
                    S.op("act", I("copy", xnT[:, :, t * 128:(t + 1) * 128], PSB[:, :].rearrange("p (c n) -> p c n", c=8)),
                         [rPSB], [r_xnT])
                for v in range(4):
                    ws = nxt("w", 2)
                    dma(wb[ws][:], K.s_bwout[l, v // 2].rearrange("p (c n) -> p c n", c=8)[:, :, (v % 2) * 256:(v % 2 + 1) * 256],
                        [], [r_wb[ws]], ch_w[ws])
                    for t in range(4):
                        i = 4 * b + t
                        bk = 5 + nxt("p", 2)
                        mm(PS[bk][:, 0:256], [(xnT[:, c, t * 128:(t + 1) * 128], wb[ws][:, c, :]) for c in range(8)],
                           [r_xnT, r_wb[ws]], [rPS[bk]])
                        x_ = X[:, i, v * 256:(v + 1) * 256]
                        S.op("dve", I("tensor_tensor", x_, x_, PS[bk][:, 0:256], ALU.add), [rX[i], rPS[bk]], [rX[i]])
            S.barrier()
        if K.attn:
            phase_b_sample(K, l)


def phase_b_sample(K, l):
    nc, S, XS, C, PS, PSB = K.nc, K.S, K.XS, K.C, K.PS, K.PSB
    rXS, rPS, rPSB, rC = K.rXS, K.rPS, K.rPSB, K.rC
    dma, mm = K.dma, K.mm
    identb, identf, ones1, ustr = C["identb"], C["identf"], C["ones1"], C["ustr"]
    ksn, vss, lfs = K.ksn, K.vss, K.lfs
    cnt = {"w": 0, "p": 0, "k": 0, "v": 0}

    def nxt(key, n):
        v = cnt[key] % n
        cnt[key] += 1
        return v
    with ExitStack() as P:
        def sb(name, shape, dt, es=P):
            return es.enter_context(nc.sbuf_tensor("s%d_" % l + name, shape, dt))
        ec = sb("ec", [16, 256], F32); bmask = sb("bmask", [16, 1024], F32)
        xnT = sb("xnT", [128, 8, 16], BF16)
        qs = sb("qs", [16, 1024], F32); szs = sb("szs", [16, 1024], F32)
        lfsb = sb("lfsb", [128, 256], F32); dacc = sb("dacc", [16, 16], F32); pself = sb("pself", [16, 16], F32)
        masked = sb("masked", [16, 1024], F32); idx = sb("idx", [128, 256], I32)
        r_ec, r_xnT, r_qs, r_szs, r_lfsb, r_dacc, r_pself, r_masked, r_idx = [Res() for _ in range(9)]
        ch_c = S.chan("sc%d" % l); ch_w = [S.chan("sw%d_%d" % (l, i)) for i in range(2)]
        ch_k = [S.chan("sk%d_%d" % (l, i)) for i in range(4)]; ch_v = [S.chan("sv%d_%d" % (l, i)) for i in range(4)]
        ch_l = S.chan("sl%d" % l)
        dma(ec[:], K.cd["ec"], [], [r_ec], ch_c); dma(bmask[:], K.cd["bmask"], [], [r_ec], ch_c)
        with ExitStack() as T:
            wb = [sb("wb%d" % i, [128, 8, 256], BF16, T) for i in range(2)]
            gb = sb("gb", [16, 1024], F32, T); qg = sb("qg", [16, 64], F32, T)
            xn = sb("xn", [16, 1024], BF16, T)
            ptb = sb("ptb", [128, 256], I32, T); ptf = sb("ptf", [128, 256], F32, T)
            th = sb("th", [16, 256], F32, T); sqt = sb("sqt", [16, 256], F32, T); ss4 = sb("ss4", [16, 4], F32, T)
            NT = K.norm_tmp(T, "s%d" % l)
            r_wb = [Res(), Res()]; r_gb, r_qg, r_xn, r_th, r_sq, r_ss4 = [Res() for _ in range(6)]
            dma(gb[:], brow(K.b_norm, l, 1024, 16), [], [r_gb], ch_c); dma(qg[:], brow(K.q_norm, l, 64, 16), [], [r_qg], ch_c)
            dma(ptb[:], bass.AP(K.pt_d.tensor, 0, [[0, 128], [1, 256]]), [], [r_idx], ch_c)
            S.op("dve", I("tensor_copy", ptf[:, :], ptb[:, :]), [r_idx], [r_idx])
            S.op("dve", I("tensor_scalar", ptf[:, :], ptf[:, :], 128.0, C["pidx"][:, 0:1], ALU.mult, ALU.add), [r_idx, rC], [r_idx])
            S.op("dve", I("tensor_copy", idx[:, :], ptf[:, :]), [r_idx], [r_idx])
            S.op("dve", I("memset", dacc[:, :], 0.0), [], [r_dacc])
            rstd = K.rmsnorm_rstd(NT, [(XS[:, :], rXS)], 16, 1, None, None, "s")
            S.op("dve", I("scalar_tensor_tensor", xn[:, :], XS[:, :], rstd[:16, 0:1], gb[:, :], ALU.mult, ALU.mult),
                 [rXS, NT["rrstd"], r_gb], [r_xn])
            S.op("pe", [I("transpose", PSB[:, c * 16:(c + 1) * 16], xn[:, c * 128:(c + 1) * 128], identb[0:16, 0:16]) for c in range(8)],
                 [r_xn, rC], [rPSB])
            S.op("act", I("copy", xnT[:, :, :], PSB[:, 0:128].rearrange("p (c n) -> p c n", c=8)), [rPSB], [r_xnT])
            for v in range(8):
                ws = nxt("w", 2)
                dma(wb[ws][:], K.s_bwin[l, v // 2].rearrange("p (c n) -> p c n", c=8)[:, :, (v % 2) * 256:(v % 2 + 1) * 256],
                    [], [r_wb[ws]], ch_w[ws])
                bk = 5 + nxt("p", 2)
                mm(PS[bk][:16, 0:256], [(xnT[:, c, :], wb[ws][:, c, :]) for c in range(8)], [r_xnT, r_wb[ws]], [rPS[bk]])
                ps = PS[bk][:16, 0:256]
                if v < 4:
                    S.op("act", I("activation", sqt[:, :], ps, AF.Square), [rPS[bk]], [r_sq])
                    S.op("dve", I("tensor_reduce", ss4[:, :], sqt[:, :].rearrange("p (h d) -> p h d", d=64), AX.X, ALU.add), [r_sq], [r_ss4])
                    S.op("dve", I("tensor_scalar", ss4[:, :], ss4[:, :], 1.0 / 64.0, EPS, ALU.mult, ALU.add), [r_ss4], [r_ss4])
                    S.op("act", I("activation", ss4[:, :], ss4[:, :], AF.Sqrt), [r_ss4], [r_ss4])
                    S.op("dve", I("reciprocal", ss4[:, :], ss4[:, :]), [r_ss4], [r_ss4])
                    S.op("dve", I("tensor_tensor", th[:, :].rearrange("p (h d) -> p h d", d=64), ps.rearrange("p (h d) -> p h d", d=64),
                                  ss4[:, :].unsqueeze(2).to_broadcast([16, 4, 64]), ALU.mult), [rPS[bk], r_ss4], [r_th])
                    S.op("dve", I("tensor_tensor", qs[:, v * 256:(v + 1) * 256].rearrange("p (h d) -> p h d", d=64),
                                  th[:, :].rearrange("p (h d) -> p h d", d=64),
                                  qg[:, :].unsqueeze(1).to_broadcast([16, 4, 64]), ALU.mult), [r_th, r_qg], [r_qs])
                else:
                    S.op("act", I("activation", th[:, :], ps, AF.Tanh), [rPS[bk]], [r_th])
                    S.op("dve", I("scalar_tensor_tensor", szs[:, (v - 4) * 256:(v - 3) * 256], th[:, :], 1.0, ps, ALU.add, ALU.mult),
                         [r_th, rPS[bk]], [r_szs])
            S.op("dve", I("tensor_tensor", masked[:, :], qs[:, :], ksn[:, :], ALU.mult), [r_qs, K.r_ksn], [r_masked])
            S.op("dve", I("tensor_reduce", pself[:, :], masked[:, :].rearrange("p (h d) -> p h d", d=64), AX.X, ALU.add),
                 [r_masked], [r_pself])
            S.op("act", I("activation", pself[:, :], pself[:, :], AF.Exp, scale=0.125), [r_pself], [r_pself])
            S.op("pe", [I("matmul", PS[6][:, s * 16:(s + 1) * 16], identf[0:16, s:s + 1].to_broadcast([16, 128]), lfs[:, :],
                          start=True, stop=True) for s in range(16)], [K.r_lfs, rC], [rPS[6]])
            S.op("act", I("copy", lfsb[:, :], PS[6][:, 0:256]), [rPS[6]], [r_lfsb])
            S.barrier()
        with ExitStack() as T:
            qb = sb("qb", [128, 1024], F32, T)
            kb = [sb("kb%d" % i, [128, 1024], F32, T) for i in range(4)]
            vb16 = [sb("vb16_%d" % i, [128, 1024], BF16, T) for i in range(4)]
            sc = sb("sc", [128, 256], F32, T); bias_s = sb("bias_s", [128, 256], F32, T)
            p16 = [sb("p16_%d" % i, [128, 256], BF16, T) for i in range(2)]
            lfpg = sb("lfpg", [128, 256], F32, T); suf = sb("suf", [128, 256], F32, T)
            psumh = sb("psumh", [128, 16], F32, T); den_sb = [sb("den%d" % i, [16, 1], F32, T) for i in range(2)]
            dmask = sb("dmask", [16, 16], F32, T)
            r_qb, r_sc, r_bias, r_lfpg, r_suf, r_psumh, r_dmask = [Res() for _ in range(7)]
            r_lfpgs = [Res() for _ in range(16)]
            r_kb = [Res() for _ in range(4)]; r_vb16 = [Res() for _ in range(4)]
            r_p16 = [Res(), Res()]; r_den = [Res(), Res()]
            ck2, cv2, cl2 = K.ck_d, K.cv_d, K.cl_d

            def k_pass(s):
                pbuf = s % 2
                for hf in range(2):
                    mm(PS[hf][:, :], [(identf[0:16, s:s + 1].to_broadcast([16, 128]), qs[:, hf * 512:(hf + 1) * 512])],
                       [r_qs, rC], [rPS[hf]])
                    S.op("act", I("copy", qb[:, hf * 512:(hf + 1) * 512], PS[hf][:, :]), [rPS[hf]], [r_qb])
                if l == 0:
                    for pg in range(16):
                        S.op("pool", I("indirect_dma_start", out=lfpg[:, pg * 16:(pg + 1) * 16], out_offset=None, in_=cl2,
                                       in_offset=bass.IndirectOffsetOnAxis(ap=idx[:, s * 16 + pg:s * 16 + pg + 1], axis=0)),
                             [r_idx], [r_lfpgs[pg]], chan=ch_l)
                    yield
                    mm(PS[6][:, 0:256], [(ustr[:, :], lfpg[:, :])], r_lfpgs + [rC], [rPS[6]])
                    mm(PS[6][:, 256:512], [(ones1[:, :], lfpg[:, :])], r_lfpgs + [rC], [rPS[6]])
                    S.op("dve", I("tensor_copy", suf[:, 240:256], lfsb[:, s * 16:(s + 1) * 16]), [r_lfsb], [r_suf])
                    for pg in range(14, -1, -1):
                        S.op("dve", I("tensor_tensor", suf[:, pg * 16:(pg + 1) * 16], suf[:, (pg + 1) * 16:(pg + 2) * 16],
                                      PS[6][:, 256 + (pg + 1) * 16:256 + (pg + 2) * 16], ALU.add), [r_suf, rPS[6]], [r_suf])
                    S.op("dve", I("tensor_tensor", bias_s[:, :], PS[6][:, 0:256], suf[:, :], ALU.add), [rPS[6], r_suf], [r_bias])
                    dma(K.s_bias[s], bias_s[:, :], [r_bias], [K.r_sbias[s]], K.ch_sb[0])
                else:
                    dma(bias_s[:, :], K.s_bias[s], [K.r_sbias[s]], [r_bias], K.ch_sb[1])
                yield
                for pg in range(16):
                    k = nxt("k", 4)
                    S.op("pool", I("indirect_dma_start", out=kb[k][:, :], out_offset=None, in_=ck2,
                                   in_offset=bass.IndirectOffsetOnAxis(ap=idx[:, s * 16 + pg:s * 16 + pg + 1], axis=0)),
                         [r_idx], [r_kb[k]], chan=ch_k[k])
                    S.op("dve", I("tensor_tensor", kb[k][:, :], kb[k][:, :], qb[:, :], ALU.mult), [r_kb[k], r_qb], [r_kb[k]])
                    S.op("dve", I("tensor_reduce", sc[:, pg * 16:(pg + 1) * 16], kb[k][:, :].rearrange("p (h d) -> p h d", d=64),
                                  AX.X, ALU.add), [r_kb[k]], [r_sc])
                    yield
                S.op("dve", I("scalar_tensor_tensor", sc[:, :], sc[:, :], 0.125, bias_s[:, :], ALU.mult, ALU.add), [r_sc, r_bias], [r_sc])
                S.op("act", I("activation", p16[pbuf][:, :], sc[:, :], AF.Exp), [r_sc], [r_p16[pbuf]])
                S.op("dve", I("tensor_reduce", psumh[:, :], p16[pbuf][:, :].rearrange("p (g h) -> p h g", h=16), AX.X, ALU.add),
                     [r_p16[pbuf]], [r_psumh])
                mm(PS[6][:16, 0:1], [(psumh[:, :], ones1[:, 0:1])], [r_psumh, rC], [rPS[6]])
                S.op("act", I("copy", den_sb[pbuf][:, :], PS[6][:16, 0:1]), [rPS[6]], [r_den[pbuf]])
                yield

            def v_pass(s):
                pbuf = s % 2
                for pg in range(16):
                    k = nxt("v", 4)
                    S.op("pool", I("indirect_dma_start", out=vb16[k][:, :], out_offset=None, in_=cv2,
                                   in_offset=bass.IndirectOffsetOnAxis(ap=idx[:, s * 16 + pg:s * 16 + pg + 1], axis=0)),
                         [r_idx], [r_vb16[k]], chan=ch_v[k])
                    for hf in range(2):
                        S.op("pe", I("matmul", PS[2 + hf][:16, :], p16[pbuf][:, pg * 16:(pg + 1) * 16], vb16[k][:, hf * 512:(hf + 1) * 512],
                                     start=(pg == 0), stop=(pg == 15)), [r_p16[pbuf], r_vb16[k]], [rPS[2 + hf]])
                    yield
                for hf in range(2):
                    S.op("dve", I("tensor_tensor", masked[:, hf * 512:(hf + 1) * 512], PS[2 + hf][:16, :], bmask[:, hf * 512:(hf + 1) * 512],
                                  ALU.mult), [rPS[2 + hf], r_ec], [r_masked])
                    S.op("pe", I("matmul", PS[4 + hf][:16, :], ec[:, s * 16:(s + 1) * 16], masked[:, hf * 512:(hf + 1) * 512],
                                 start=(s == 0), stop=(s == 15)), [r_masked, r_ec], [rPS[4 + hf]])
                S.op("dve", I("tensor_scalar", dmask[:, :], identf[0:16, 0:16], den_sb[pbuf][:, 0:1], None, ALU.mult),
                     [r_den[pbuf], rC], [r_dmask])
                mm(PS[6][:16, 32:48], [(ec[:, s * 16:(s + 1) * 16], dmask[:, :])], [r_dmask, r_ec], [rPS[6]])
                S.op("dve", I("tensor_tensor", dacc[:, :], dacc[:, :], PS[6][:16, 32:48], ALU.add), [r_dacc, rPS[6]], [r_dacc])
                yield

            for s in range(17):
                gens = []
                if s < 16:
                    gens.append(k_pass(s))
                if s >= 1:
                    gens.append(v_pass(s - 1))
                while gens:
                    for g_ in list(gens):
                        try:
                            next(g_)
                        except StopIteration:
                            gens.remove(g_)
            S.barrier()
        with ExitStack() as T:
            wb = [sb("wc%d" % i, [128, 8, 256], BF16, T) for i in range(2)]
            xn = sb("xo", [16, 1024], BF16, T)
            r_wb = [Res(), Res()]; r_xn = Res()
            S.op("dve", I("tensor_tensor", masked[:, :].rearrange("p (h d) -> p h d", d=64), vss[:, :].rearrange("p (h d) -> p h d", d=64),
                          pself[:, :].unsqueeze(2).to_broadcast([16, 16, 64]), ALU.mult), [K.r_vss, r_pself], [r_masked])
            for hf in range(2):
                S.op("dve", I("tensor_tensor", masked[:, hf * 512:(hf + 1) * 512], masked[:, hf * 512:(hf + 1) * 512], PS[4 + hf][:16, :],
                              ALU.add), [r_masked, rPS[4 + hf]], [r_masked])
            S.op("dve", I("tensor_tensor", dacc[:, :], dacc[:, :], pself[:, :], ALU.add), [r_dacc, r_pself], [r_dacc])
            S.op("dve", I("reciprocal", dacc[:, :], dacc[:, :]), [r_dacc], [r_dacc])
            S.op("dve", I("tensor_tensor", masked[:, :].rearrange("p (h d) -> p h d", d=64), masked[:, :].rearrange("p (h d) -> p h d", d=64),
                          dacc[:, :].unsqueeze(2).to_broadcast([16, 16, 64]), ALU.mult), [r_masked, r_dacc], [r_masked])
            S.op("dve", I("tensor_tensor", xn[:, :], masked[:, :], szs[:, :], ALU.mult), [r_masked, r_szs], [r_xn])
            S.op("pe", [I("transpose", PSB[:, c * 16:(c + 1) * 16], xn[:, c * 128:(c + 1) * 128], identb[0:16, 0:16]) for c in range(8)],
                 [r_xn, rC], [rPSB])
            S.op("act", I("copy", xnT[:, :, :], PSB[:, 0:128].rearrange("p (c n) -> p c n", c=8)), [rPSB], [r_xnT])
            for v in range(4):
                ws = nxt("w", 2)
                dma(wb[ws][:], K.s_bwout[l, v // 2].rearrange("p (c n) -> p c n", c=8)[:, :, (v % 2) * 256:(v % 2 + 1) * 256],
                    [], [r_wb[ws]], ch_w[ws])
                bk = 5 + nxt("p", 2)
                mm(PS[bk][:16, 0:256], [(xnT[:, c, :], wb[ws][:, c, :]) for c in range(8)], [r_xnT, r_wb[ws]], [rPS[bk]])
                xs_ = XS[:, v * 256:(v + 1) * 256]
                S.op("dve", I("tensor_tensor", xs_, xs_, PS[bk][:16, 0:256], ALU.add), [rXS, rPS[bk]], [rXS])
            S.barrier()
```
